# Optimizing a Trainium2 kernel written in Bass

```python
import math
import jax, jax.numpy as jnp
from jax import lax
import numpy as np

D_MODEL = 1024
BATCH = 16
SEQ = 2048
DEPTH = 4

CTX_LEN = 256
GRID_W = 64
MIX = D_MODEL
BRANCH_W = MIX // 4
Q_BLOCK = 128
ROPE_THETA = 10000.0
EPS = 1e-6
DA_HEADS = 4
DA_QK = BRANCH_W // (2 * DA_HEADS)
DA_V = BRANCH_W // DA_HEADS
GQ_HEADS = 4
GQ_KV_HEADS = 2
GQ_HD = BRANCH_W // GQ_HEADS
SG_GROUPS = 4
SG_CHUNK = 128
GLA_HEADS = 4
GLA_DV = BRANCH_W // GLA_HEADS
GLA_DK = GLA_DV // 2
GLA_RANK = 16
GLA_NORMALIZER = 16.0
GLA_CHUNK = 32

IN_SIZES = (
    DA_HEADS * 2 * DA_QK, DA_HEADS * 2 * DA_QK, DA_HEADS * DA_V, BRANCH_W,
    GQ_HEADS * GQ_HD, GQ_KV_HEADS * GQ_HD, GQ_KV_HEADS * GQ_HD, BRANCH_W,
    BRANCH_W, BRANCH_W, BRANCH_W,
    GLA_HEADS * GLA_DK, GLA_HEADS * GLA_DK, GLA_HEADS * GLA_DV, BRANCH_W,
    GLA_RANK, GLA_RANK,
)
P_IN = sum(IN_SIZES)

kernel_name = "hybrid_parallel_heads_flow_backbone"


def rms_norm(x, g):
    xf = x.astype(jnp.float32)
    y = xf * lax.rsqrt(jnp.mean(xf * xf, axis=-1, keepdims=True) + EPS)
    return (y * g.astype(jnp.float32)).astype(x.dtype)


def layer_norm(x, g, b):
    xf = x.astype(jnp.float32)
    mu = jnp.mean(xf, axis=-1, keepdims=True)
    var = jnp.mean(jnp.square(xf - mu), axis=-1, keepdims=True)
    y = (xf - mu) * lax.rsqrt(var + EPS)
    return (y * g.astype(jnp.float32) + b.astype(jnp.float32)).astype(x.dtype)


def split_cols(p):
    out = []
    off = 0
    for n in IN_SIZES:
        out.append(p[..., off:off + n])
        off += n
    return out


def rope_angles(pos, dim):
    inv = ROPE_THETA ** (-jnp.arange(0, dim, 2, dtype=jnp.float32) / dim)
    return pos.astype(jnp.float32)[:, None] * inv[None, :]


def rope_1d(x, ang):
    half = x.shape[-1] // 2
    shape = (ang.shape[0],) + (1,) * (x.ndim - 3) + (half,)
    cos = jnp.cos(ang).reshape(shape)
    sin = jnp.sin(ang).reshape(shape)
    xf = x.astype(jnp.float32)
    x1, x2 = xf[..., :half], xf[..., half:]
    return jnp.concatenate([x1 * cos - x2 * sin, x1 * sin + x2 * cos], axis=-1).astype(x.dtype)


def rope_2d(x, rows, cols):
    h = x.shape[-1] // 2
    return jnp.concatenate([rope_1d(x[..., :h], rope_angles(rows, h)),
                            rope_1d(x[..., h:], rope_angles(cols, h))], axis=-1)


def sweep_query_blocks(fn, q):
    b, t = q.shape[:2]
    nb = t // Q_BLOCK
    qb = jnp.moveaxis(q.reshape((b, nb, Q_BLOCK) + q.shape[2:]), 1, 0)
    ob = lax.map(fn, qb)
    return jnp.moveaxis(ob, 0, 1).reshape((b, t) + ob.shape[3:])


def diff_attn_core(q, k, v, lam, lam_init, subln_g):
    s = jnp.einsum('bqhmd,bkhmd->bhmqk', q, k).astype(jnp.float32) * (DA_QK ** -0.5)
    p = jax.nn.softmax(s, axis=-1)
    w = p[:, :, 0] - lam * p[:, :, 1]
    o = jnp.einsum('bhqk,bkhd->bqhd', w.astype(v.dtype), v)
    return rms_norm(o, subln_g) * (1.0 - lam_init)


def diff_attention_branch(q_l, k_l, v_l, q_c, k_c, v_c, lq1, lk1, lq2, lk2, subln_g,
                          lam_init, rows, cols, with_ctx):
    b, t = q_l.shape[:2]
    tc = q_c.shape[1]
    q_l = rope_2d(q_l.reshape(b, t, DA_HEADS, 2, DA_QK), rows, cols)
    k_l = rope_2d(k_l.reshape(b, t, DA_HEADS, 2, DA_QK), rows, cols)
    k_c = k_c.reshape(b, tc, DA_HEADS, 2, DA_QK)
    v_l = v_l.reshape(b, t, DA_HEADS, DA_V)
    v_c = v_c.reshape(b, tc, DA_HEADS, DA_V)
    lam = (jnp.exp(jnp.sum((lq1 * lk1).astype(jnp.float32)))
           - jnp.exp(jnp.sum((lq2 * lk2).astype(jnp.float32))) + lam_init)
    k_all = jnp.concatenate([k_c, k_l], axis=1)
    v_all = jnp.concatenate([v_c, v_l], axis=1)
    y_l = sweep_query_blocks(lambda qb: diff_attn_core(qb, k_all, v_all, lam, lam_init, subln_g), q_l)
    y_l = y_l.reshape(b, t, DA_HEADS * DA_V)
    y_c = None
    if with_ctx:
        q_c = q_c.reshape(b, tc, DA_HEADS, 2, DA_QK)
        y_c = diff_attn_core(q_c, k_c, v_c, lam, lam_init, subln_g).reshape(b, tc, DA_HEADS * DA_V)
    return y_l, y_c


def gqa_core(q, k, v):
    s = jnp.einsum('bqhgd,bkhd->bhgqk', q, k).astype(jnp.float32) * (GQ_HD ** -0.5)
    p = jax.nn.softmax(s, axis=-1)
    return jnp.einsum('bhgqk,bkhd->bqhgd', p.astype(v.dtype), v)


def gqa_branch(q_l, k_l, v_l, q_c, k_c, v_c, qn_g, kn_g, rows, cols, with_ctx):
    b, t = q_l.shape[:2]
    tc = q_c.shape[1]
    grp = GQ_HEADS // GQ_KV_HEADS
    q_l = rope_2d(rms_norm(q_l.reshape(b, t, GQ_HEADS, GQ_HD), qn_g), rows, cols)
    k_l = rope_2d(rms_norm(k_l.reshape(b, t, GQ_KV_HEADS, GQ_HD), kn_g), rows, cols)
    k_c = rms_norm(k_c.reshape(b, tc, GQ_KV_HEADS, GQ_HD), kn_g)
    v_l = v_l.reshape(b, t, GQ_KV_HEADS, GQ_HD)
    v_c = v_c.reshape(b, tc, GQ_KV_HEADS, GQ_HD)
    k_all = jnp.concatenate([k_c, k_l], axis=1)
    v_all = jnp.concatenate([v_c, v_l], axis=1)
    q_l = q_l.reshape(b, t, GQ_KV_HEADS, grp, GQ_HD)
    y_l = sweep_query_blocks(lambda qb: gqa_core(qb, k_all, v_all), q_l).reshape(b, t, GQ_HEADS * GQ_HD)
    y_c = None
    if with_ctx:
        q_c = rms_norm(q_c.reshape(b, tc, GQ_HEADS, GQ_HD), qn_g).reshape(b, tc, GQ_KV_HEADS, grp, GQ_HD)
        y_c = gqa_core(q_c, k_c, v_c).reshape(b, tc, GQ_HEADS * GQ_HD)
    return y_l, y_c


def chunk_sgu(u, v, ln_g, ln_b, w_s, b_s):
    b, t, w = u.shape
    n = t // SG_CHUNK
    vn = layer_norm(v, ln_g, ln_b).reshape(b, n, SG_CHUNK, SG_GROUPS, w // SG_GROUPS)
    mixed = jnp.einsum('gij,bnjgc->bnigc', w_s, vn) + b_s.T[:, :, None]
    return u * mixed.reshape(b, t, w)


def gla_scan(q, k, v, g, s0):
    b, t, h, _ = q.shape
    dv = v.shape[-1]
    nc = t // GLA_CHUNK

    def to_chunks(a):
        return a.astype(jnp.float32).reshape(b, nc, GLA_CHUNK, h, a.shape[-1]).transpose(1, 0, 3, 2, 4)

    mask = jnp.tril(jnp.ones((GLA_CHUNK, GLA_CHUNK), dtype=bool))

    def step(s, inp):
        qc, kc, vc, gc = inp
        cum = jnp.cumsum(gc, axis=2)
        cum_last = cum[:, :, -1:, :]
        qe = qc * jnp.exp(cum)
        a = jnp.einsum('bhid,bhjd->bhij', qe, kc * jnp.exp(-cum))
        a = jnp.where(mask, a, 0.0)
        o = jnp.einsum('bhij,bhjv->bhiv', a, vc) + jnp.einsum('bhid,bhdv->bhiv', qe, s)
        s = (jnp.exp(cum_last[:, :, 0, :])[..., None] * s
             + jnp.einsum('bhjd,bhjv->bhdv', kc * jnp.exp(cum_last - cum), vc))
        return s, o

    s_fin, o = lax.scan(step, s0, (to_chunks(q), to_chunks(k), to_chunks(v), to_chunks(g)))
    o = o.transpose(1, 0, 3, 2, 4).reshape(b, t, h, dv)
    return o.astype(v.dtype), s_fin


def gla_heads(q, k, v, rf, rb, w2_f, b_f, w2_b, b_b):
    b, t = q.shape[:2]
    q = q.reshape(b, t, GLA_HEADS, GLA_DK) * (GLA_DK ** -0.5)
    k = k.reshape(b, t, GLA_HEADS, GLA_DK)
    v = v.reshape(b, t, GLA_HEADS, GLA_DV)
    gf = (jax.nn.log_sigmoid((rf @ w2_f + b_f).astype(jnp.float32)) / GLA_NORMALIZER).reshape(b, t, GLA_HEADS, GLA_DK)
    gb = (jax.nn.log_sigmoid((rb @ w2_b + b_b).astype(jnp.float32)) / GLA_NORMALIZER).reshape(b, t, GLA_HEADS, GLA_DK)
    return q, k, v, gf, gb


def gla_branch(q_l, k_l, v_l, rf_l, rb_l, q_c, k_c, v_c, rf_c, rb_c,
               w2_f, b_f, w2_b, b_b, norm_g, with_ctx):
    ql, kl, vl, gfl, gbl = gla_heads(q_l, k_l, v_l, rf_l, rb_l, w2_f, b_f, w2_b, b_b)
    qc, kc, vc, gfc, gbc = gla_heads(q_c, k_c, v_c, rf_c, rb_c, w2_f, b_f, w2_b, b_b)
    b, t = ql.shape[:2]
    tc = qc.shape[1]
    s0 = jnp.zeros((b, GLA_HEADS, GLA_DK, GLA_DV), jnp.float32)
    flip = lambda a: jnp.flip(a, axis=1)
    oc_f, sc_f = gla_scan(qc, kc, vc, gfc, s0)
    oc_b, sc_b = gla_scan(flip(qc), flip(kc), flip(vc), flip(gbc), s0)
    ol_f, _ = gla_scan(ql, kl, vl, gfl, sc_f)
    ol_b, _ = gla_scan(flip(ql), flip(kl), flip(vl), flip(gbl), sc_b)
    y_l = rms_norm(ol_f + flip(ol_b), norm_g).reshape(b, t, GLA_HEADS * GLA_DV)
    y_c = None
    if with_ctx:
        y_c = rms_norm(oc_f + flip(oc_b), norm_g).reshape(b, tc, GLA_HEADS * GLA_DV)
    return y_l, y_c


def setup_inputs(seed: int = 0) -> dict:
    key = jax.random.key(seed)
    ks = jax.random.split(key, 32)

    def nrm(k, shape, s):
        return jax.random.normal(k, shape, jnp.float32) * s

    L = DEPTH
    return {
        "x": nrm(ks[0], (BATCH, SEQ, D_MODEL), 1.0),
        "c": nrm(ks[1], (BATCH, D_MODEL), 1.0),
        "ctx": nrm(ks[2], (BATCH, CTX_LEN, D_MODEL), 1.0),
        "c_ctx": nrm(ks[3], (D_MODEL,), 1.0),
        "ada_w": nrm(ks[4], (L, D_MODEL, 3 * D_MODEL), 0.5 * D_MODEL ** -0.5),
        "ada_b": nrm(ks[5], (L, 3 * D_MODEL), 0.02),
        "norm_g": 1.0 + nrm(ks[6], (L, D_MODEL), 0.02),
        "w_in": nrm(ks[7], (L, D_MODEL, P_IN), D_MODEL ** -0.5),
        "da_lq1": nrm(ks[8], (L, DA_QK), 0.1),
        "da_lk1": nrm(ks[9], (L, DA_QK), 0.1),
        "da_lq2": nrm(ks[10], (L, DA_QK), 0.1),
        "da_lk2": nrm(ks[11], (L, DA_QK), 0.1),
        "da_subln_g": 1.0 + nrm(ks[12], (L, DA_V), 0.02),
        "gq_qnorm_g": 1.0 + nrm(ks[13], (L, GQ_HD), 0.02),
        "gq_knorm_g": 1.0 + nrm(ks[14], (L, GQ_HD), 0.02),
        "sg_ln_g": 1.0 + nrm(ks[15], (L, BRANCH_W), 0.02),
        "sg_ln_b": nrm(ks[16], (L, BRANCH_W), 0.02),
        "sg_w": nrm(ks[17], (L, SG_GROUPS, SG_CHUNK, SG_CHUNK), SG_CHUNK ** -0.5),
        "sg_b": 1.0 + nrm(ks[18], (L, SG_GROUPS, SG_CHUNK), 0.02),
        "gla_w2_f": nrm(ks[19], (L, GLA_RANK, GLA_HEADS * GLA_DK), GLA_RANK ** -0.5),
        "gla_b_f": nrm(ks[20], (L, GLA_HEADS * GLA_DK), 0.5),
        "gla_w2_b": nrm(ks[21], (L, GLA_RANK, GLA_HEADS * GLA_DK), GLA_RANK ** -0.5),
        "gla_b_b": nrm(ks[22], (L, GLA_HEADS * GLA_DK), 0.5),
        "gla_norm_g": 1.0 + nrm(ks[23], (L, GLA_DV), 0.02),
        "w_out": nrm(ks[24], (L, MIX, D_MODEL), MIX ** -0.5),
        "final_norm_g": 1.0 + nrm(ks[25], (D_MODEL,), 0.02),
    }


def reference(x, c, ctx, c_ctx, ada_w, ada_b, norm_g, w_in, da_lq1, da_lk1, da_lq2, da_lk2,
              da_subln_g, gq_qnorm_g, gq_knorm_g, sg_ln_g, sg_ln_b, sg_w, sg_b,
              gla_w2_f, gla_b_f, gla_w2_b, gla_b_b, gla_norm_g, w_out, final_norm_g):
    n_rows = x.shape[1] // GRID_W
    rows = jnp.repeat(jnp.arange(n_rows, dtype=jnp.int32), GRID_W)
    cols = jnp.tile(jnp.arange(GRID_W, dtype=jnp.int32), n_rows)
    silu_c = jax.nn.silu(c)
    silu_cc = jax.nn.silu(c_ctx)
    xl, xc = x, ctx
    for i in range(DEPTH):
        with_ctx = i < DEPTH - 1
        lam_init = 0.8 - 0.6 * math.exp(-0.3 * i)
        mod_l = silu_c @ ada_w[i] + ada_b[i]
        mod_c = silu_cc @ ada_w[i] + ada_b[i]
        sh_l, sc_l, gt_l = jnp.split(mod_l, 3, axis=-1)
        sh_c, sc_c, gt_c = jnp.split(mod_c, 3, axis=-1)
        hl = rms_norm(xl, norm_g[i]) * (1.0 + sc_l[:, None]) + sh_l[:, None]
        hc = rms_norm(xc, norm_g[i]) * (1.0 + sc_c) + sh_c
        (aq_l, ak_l, av_l, az_l, bq_l, bk_l, bv_l, bz_l, cu_l, cv_l, cz_l,
         dq_l, dk_l, dv_l, dz_l, drf_l, drb_l) = split_cols(hl @ w_in[i])
        (aq_c, ak_c, av_c, az_c, bq_c, bk_c, bv_c, bz_c, cu_c, cv_c, cz_c,
         dq_c, dk_c, dv_c, dz_c, drf_c, drb_c) = split_cols(hc @ w_in[i])

        ya_l, ya_c = diff_attention_branch(aq_l, ak_l, av_l, aq_c, ak_c, av_c,
                                           da_lq1[i], da_lk1[i], da_lq2[i], da_lk2[i], da_subln_g[i],
                                           lam_init, rows, cols, with_ctx)
        yb_l, yb_c = gqa_branch(bq_l, bk_l, bv_l, bq_c, bk_c, bv_c,
                                gq_qnorm_g[i], gq_knorm_g[i], rows, cols, with_ctx)
        yc_l = chunk_sgu(cu_l, cv_l, sg_ln_g[i], sg_ln_b[i], sg_w[i], sg_b[i])
        yd_l, yd_c = gla_branch(dq_l, dk_l, dv_l, drf_l, drb_l, dq_c, dk_c, dv_c, drf_c, drb_c,
                                gla_w2_f[i], gla_b_f[i], gla_w2_b[i], gla_b_b[i], gla_norm_g[i], with_ctx)

        y_l = jnp.concatenate([ya_l * jax.nn.silu(az_l), yb_l * jax.nn.silu(bz_l),
                               yc_l * jax.nn.silu(cz_l), yd_l * jax.nn.silu(dz_l)], axis=-1)
        xl = xl + gt_l[:, None] * (y_l @ w_out[i])
        if with_ctx:
            yc_c = chunk_sgu(cu_c, cv_c, sg_ln_g[i], sg_ln_b[i], sg_w[i], sg_b[i])
            y_c = jnp.concatenate([ya_c * jax.nn.silu(az_c), yb_c * jax.nn.silu(bz_c),
                                   yc_c * jax.nn.silu(cz_c), yd_c * jax.nn.silu(dz_c)], axis=-1)
            xc = xc + gt_c * (y_c @ w_out[i])
    return rms_norm(xl, final_norm_g)
```

```python
import math
from contextlib import ExitStack

import numpy as np
import ml_dtypes

import concourse.bass as bass
import concourse.mybir as mybir
from concourse.bass_utils import run_bass_kernel_spmd

F32 = mybir.dt.float32
BF16 = mybir.dt.bfloat16
AF = mybir.ActivationFunctionType
ALU = mybir.AluOpType
AX = mybir.AxisListType

D = 1024
DEPTH = 4
NB = 2
TL = 2048
TC = 256
NT = 18
P_IN = 3360
EPS = 1e-6
NPV = 1152


def _esz(dt):
    return 2 if dt == BF16 else 4


class Sched:
    ENGS = ("pe", "act", "dve", "pool", "sp")

    def __init__(self, same_sync=True, n_dma=10):
        self.ops = {e: [] for e in self.ENGS}
        self.cnt = {e: 0 for e in self.ENGS}
        self.clock = {e: {} for e in self.ENGS}
        self.recs = {}
        self.snap = {}
        self.same_sync = same_sync
        self.n_dma = n_dma
        self.dq = {e: dict(next=0, cnt=[0] * n_dma, last=[None] * n_dma) for e in ("sp", "pool", "act")}
        self.nwaits = 0

    @staticmethod
    def box(a):
        if isinstance(a, tuple):
            return a
        t = a.tensor
        ps = 1
        for s in t.shape[1:]:
            ps *= s
        esz = _esz(a.dtype)
        off = a.offset
        p0 = off // ps
        f0 = off % ps
        apl = a.ap
        npart = apl[0][1]
        ext = 1
        for st, c in apl[1:]:
            ext += (c - 1) * abs(st)
        if t.name.startswith("ps"):
            return (t.name, 0, 128, 0, 1 << 20)
        return (t.name, p0, p0 + npart, f0 * esz, (f0 + ext) * esz)

    def op(self, eng, fn, reads=(), writes=(), dma=False, inc=True):
        deps = {}

        def add(tok):
            k, v = tok
            if deps.get(k, 0) < v:
                deps[k] = v

        rb = [self.box(a) for a in reads]
        wb = [self.box(a) for a in writes]
        own_k = ("e", eng)
        for b in rb:
            isps = b[0].startswith("ps")
            for r in self.recs.get(b[0], ()):
                if (r[5] or (isps and r[6][0] != own_k)) and r[1] < b[2] and b[1] < r[2] and r[3] < b[4] and b[3] < r[4]:
                    add(r[6])
        for b in wb:
            for r in self.recs.get(b[0], ()):
                if r[1] < b[2] and b[1] < r[2] and r[3] < b[4] and b[3] < r[4]:
                    add(r[6])
        if dma:
            q = self.dq[eng]
            i = q["next"]
            q["next"] = (i + 1) % self.n_dma
            if q["last"][i] is not None:
                add(q["last"][i])
            q["cnt"][i] += 1
            tok = (("d", eng, i), q["cnt"][i] * 16)
            q["last"][i] = tok
        else:
            if inc:
                self.cnt[eng] += 1
                tok = (("e", eng), self.cnt[eng])
            else:
                tok = (("e", eng), self.cnt[eng] + 1)
        clk = self.clock[eng]
        waits = []
        own = ("e", eng)
        for k, v in deps.items():
            if k == own and (eng == "pe" or not self.same_sync):
                continue
            if clk.get(k, 0) >= v:
                continue
            waits.append((k, v))
            sn = self.snap.get((k, v))
            if sn:
                for kk, vv in sn.items():
                    if clk.get(kk, 0) < vv:
                        clk[kk] = vv
            clk[k] = v
        self.nwaits += len(waits)
        if inc or dma:
            self.snap[tok] = dict(clk)
        self.ops[eng].append((waits, fn, tok if (inc or dma) else None))
        for b in wb:
            lst = self.recs.setdefault(b[0], [])
            lst[:] = [r for r in lst if not (b[1] <= r[1] and r[2] <= b[2] and b[3] <= r[3] and r[4] <= b[4])]
            lst.append([b[0], b[1], b[2], b[3], b[4], True, tok])
        for b in rb:
            lst = self.recs.setdefault(b[0], [])
            found = False
            for r in lst:
                if (not r[5]) and r[1] == b[1] and r[2] == b[2] and r[3] == b[3] and r[4] == b[4] and r[6][0] == tok[0]:
                    if r[6][1] < tok[1]:
                        r[6] = tok
                    found = True
                    break
            if not found:
                lst.append([b[0], b[1], b[2], b[3], b[4], False, tok])
        return tok

    def replay(self, eng, e, sems):
        for waits, fn, tok in self.ops[eng]:
            for k, v in waits:
                e.wait_ge(sems[k], v)
            ins = fn(e)
            if tok is not None:
                ins.then_inc(sems[tok[0]], 16 if tok[0][0] == "d" else 1)

    def final_wait(self, eng, toks):
        ws = [(k, v) for (k, v) in toks]
        self.ops[eng].append((ws, None, None))


class _Stop(Exception):
    pass


def build_program(depth=DEPTH, nb=NB, debug_y=False, same_sync=True, stop_at=None):
    nc = bass.Bass("TRN2", target_bir_lowering=False)
    S = Sched(same_sync=same_sync)
    st = ExitStack()

    def din(name, shape, dt=F32):
        return nc.dram_tensor(name, list(shape), dt, kind="ExternalInput").ap()

    x_d = din("x", [nb, TL, D])
    ctx_d = din("ctx", [nb, TC, D])
    cT_d = din("cT", [128, 8, 3])
    adaw_d = din("ada_w", [DEPTH, D, 3 * D])
    adab_d = din("ada_b", [DEPTH, 3 * D])
    ng_d = din("norm_g", [DEPTH, D])
    win_d = din("w_in", [DEPTH, D, P_IN])
    wout_d = din("w_out", [DEPTH, D, D])
    pvec_d = din("pvec", [DEPTH, NPV])
    sgbT_d = din("sgbT", [DEPTH, 128, 4])
    sgwT_d = din("sgwT", [DEPTH, 128, 4, 128])
    w2bd_d = din("w2bd", [DEPTH, 32, 256])
    fng_d = din("fng", [D])
    cf32_d = din("cf32", [128, 647])
    cbf_d = din("cbf", [128, 3200], BF16)
    out_d = nc.dram_tensor("out", [nb, TL, D], F32, kind="ExternalOutput").ap()
    if debug_y:
        dbg_d = nc.dram_tensor("dbg", [NT * 128, D], F32, kind="ExternalOutput").ap()
    xres_d = nc.dram_tensor("xres", [nb, NT * 128, D], F32).ap()
    mods_d = nc.dram_tensor("modsd", [DEPTH, 3, 3, D], F32).ap()

    def sb(name, free, dt):
        return st.enter_context(nc.sbuf_tensor("s_" + name, [128, free], dt))

    ident_t = sb("ident", 128, BF16)
    rope_t = sb("rope", 3072, BF16)
    cf_t = sb("cf", 647, F32)
    pbc_t = sb("pbc", NPV, F32)
    bsT_t = sb("bsT", 4, F32)
    w2bd_t = sb("w2bd", 256, BF16)
    wsT_t = sb("wsT", 512, BF16)
    sm_t = sb("sm", 64, F32)
    sgeff_t = sb("sgeff", 64, F32)
    fp_t = sb("fpool", 8192, F32)
    fpb_t = fp_t.bitcast(BF16)
    hT_t = sb("hT", 8 * 2304, BF16)
    y_t = sb("y", NT * 1024, BF16)
    NRING = 3
    ring_t = sb("ring", NRING * 4096, BF16)
    kTA_t = sb("kTA", 2 * 2304, BF16)
    VA_t = sb("VA", NT * 4 * 65 + 64, BF16)
    kTB_t = sb("kTB", 2304, BF16)
    VB_t = sb("VB", NT * 2 * 65 + 64, BF16)
    hbf_t = sb("hbf", 2048, BF16)
    gate_t = sb("gate", 2 * 1024, BF16)
    pT_t = sb("pT", 4 * 512, BF16)
    ytT_t = sb("ytT", 2 * 1024, BF16)
    qr_t = sb("qr", 2 * 256, BF16)
    oT_t = sb("oT", 2 * 512, F32)
    cmb_t = sb("cmb", 512, F32)
    st_t = sb("stats", 128, F32)
    rTb_t = sb("rTb", 128, BF16)
    qeb_t = sb("qeb", 128, BF16)
    keb_t = sb("keb", 128, BF16)
    vb_t = sb("vb", 256, BF16)
    qeT_t = sb("qeT", 128, BF16)
    kTm_t = sb("kTm", 512, BF16)
    am_t = sb("am", 512, BF16)
    Sst_t = sb("Sst", 256, F32)
    Sb_t = sb("Sb", 256, BF16)
    Wt_t = sb("Wt", 256, F32)
    gl_t = sb("gl", 512, F32)
    vnb_t = sb("vnb", 256, BF16)
    uu_t = sb("uu", 512, F32)
    scT_t = sb("scT", 48, F32)

    ps = []
    for i in range(8):
        ps.append(st.enter_context(nc.psum_tensor(f"ps{i}", [128, 512], F32)))
    ps1b = ps[1].bitcast(BF16)

    sems = {}
    for e in Sched.ENGS:
        sems[("e", e)] = st.enter_context(nc.semaphore(f"s_{e}"))
    for e in ("sp", "pool", "act"):
        for i in range(S.n_dma):
            sems[("d", e, i)] = st.enter_context(nc.semaphore(f"d_{e}{i}"))

    def V(t, off=0, dims=None, p0=0, npart=128):
        ps_ = 1
        for s in t.shape[1:]:
            ps_ *= s
        if dims is None:
            dims = [(1, ps_ - off)]
        return bass.AP(t, p0 * ps_ + off, [[ps_, npart]] + [[s, c] for (s, c) in dims])

    def dma(eng, out, in_, reads=(), writes=()):
        S.op(eng, lambda e: e.dma_start(out=out, in_=in_), reads=list(reads), writes=list(writes), dma=True)

    def mm(out, lhsT, rhs, start, stop, **kw):
        S.op("pe", lambda e: e.matmul(out, lhsT, rhs, start=start, stop=stop, **kw),
             reads=[lhsT, rhs], writes=[out], inc=bool(stop))

    def tr(out, in_, inc=True):
        S.op("pe", lambda e: e.transpose(out, in_, V(ident_t, 0, [(1, 128)])), reads=[in_], writes=[out], inc=inc)

    def act(out, in_, func, scale=1.0, bias=None, accum=None):
        rd = [in_]
        wr = [out]
        kw = {}
        if bias is not None:
            kw["bias"] = bias
            if not isinstance(bias, float):
                rd.append(bias)
        if not isinstance(scale, float):
            rd.append(scale)
        if accum is not None:
            kw["accum_out"] = accum
            wr.append(accum)
        S.op("act", lambda e: e.activation(out=out, in_=in_, func=func, scale=scale, **kw), reads=rd, writes=wr)

    def tt(out, in0, in1, op, eng="dve"):
        S.op(eng, lambda e: e.tensor_tensor(out=out, in0=in0, in1=in1, op=op), reads=[in0, in1], writes=[out])

    def ts(out, in0, s1, s2, op0, op1=None, eng="dve"):
        rd = [in0]
        if not isinstance(s1, float):
            rd.append(s1)
        if s2 is not None and not isinstance(s2, float):
            rd.append(s2)
        if op1 is None:
            S.op(eng, lambda e: e.tensor_scalar(out=out, in0=in0, scalar1=s1, scalar2=None, op0=op0), reads=rd, writes=[out])
        else:
            S.op(eng, lambda e: e.tensor_scalar(out=out, in0=in0, scalar1=s1, scalar2=s2, op0=op0, op1=op1), reads=rd, writes=[out])

    def stt(out, in0, scalar, in1, op0, op1):
        rd = [in0, in1]
        if not isinstance(scalar, float):
            rd.append(scalar)
        S.op("dve", lambda e: e.scalar_tensor_tensor(out=out, in0=in0, scalar=scalar, in1=in1, op0=op0, op1=op1),
             reads=rd, writes=[out])

    def cp(out, in_, eng="dve"):
        if eng == "act":
            S.op("act", lambda e: e.copy(out=out, in_=in_), reads=[in_], writes=[out])
        else:
            S.op(eng, lambda e: e.tensor_copy(out=out, in_=in_), reads=[in_], writes=[out])

    def memset(ap, val, eng="dve"):
        S.op(eng, lambda e: e.memset(ap, val), writes=[ap])

    def ttr(out, in0, in1, accum):
        S.op("dve", lambda e: e.tensor_tensor_reduce(out=out, in0=in0, in1=in1, scale=1.0, scalar=0.0,
                                                    op0=ALU.mult, op1=ALU.add, accum_out=accum),
             reads=[in0, in1], writes=[out, accum])

    def red(out, in_):
        S.op("dve", lambda e: e.tensor_reduce(out=out, in_=in_, axis=AX.X, op=ALU.add), reads=[in_], writes=[out])

    def recip(out, in_):
        S.op("dve", lambda e: e.reciprocal(out=out, in_=in_), reads=[in_], writes=[out])

    def rstd_from(out, ssum, k):
        act(out, ssum, AF.Ln, scale=float(k), bias=float(EPS))
        act(out, out, AF.Exp, scale=-0.5)

    rot = {}

    def nxt(key, n):
        i = rot.get(key, 0)
        rot[key] = (i + 1) % n
        return i

    pbanks = [0]

    def psP():
        return ps[pbanks[nxt("P", len(pbanks)) % len(pbanks)]]

    def psT():
        return ps1b

    def psS():
        return ps[(0, 2, 3)[nxt("S", 3)]]

    def psO():
        return ps[(5, 6, 7)[nxt("O", 3)]]

    def mods(slot, off=0, n=1024):
        return V(fp_t, slot * 1024 + off, [(1, n)])

    def xt(slot):
        return V(fp_t, 4096 + slot * 1024, [(1, 1024)])

    sA = lambda off=0, n=1024: V(fp_t, 6144 + off, [(1, n)])
    sB = lambda off=0, n=512: V(fp_t, 7168 + off, [(1, n)])
    sC = lambda off=0, n=512: V(fp_t, 7680 + off, [(1, n)])

    def stat(i, n=1):
        return V(st_t, i, [(1, n)])

    PO = dict(lng=0, lnb=256, gb=512, subg=768, qng=832, kng=896, gng=960, lq=1024, lk=1088)

    def pb(name, n, off=0):
        return V(pbc_t, PO[name] + off, [(1, n)])

    LtriF = V(cf_t, 0, [(1, 128)])
    LtriB = V(cf_t, 128, [(1, 128)])
    blockmask = V(cf_t, 256, [(1, 256)])
    ones_c = V(cf_t, 516, [(1, 1)])

    wreq = []
    wstate = dict(issued=0, used=0)

    def wblock_aps(kind, l, c0, ncol, dst_off):
        src = win_d if kind == "in" else wout_d
        ncols_total = P_IN if kind == "in" else D
        in_ap = bass.AP(src.tensor, l * D * ncols_total + c0, [[ncols_total, 128], [128 * ncols_total, 8], [1, ncol]])
        return dst_off, in_ap, ncol

    def plan_weights():
        plan = []
        for b in range(nb):
            for l in range(depth):
                plan.append([("in", l, 256, 512, 0)])
                plan.append([("in", l, 0, 256, 0), ("in", l, 768, 256, 256)])
                plan.append([("in", l, 1280, 256, 0)])
                plan.append([("in", l, 1024, 256, 0), ("in", l, 1536, 256, 256)])
                plan.append([("in", l, 1792, 512, 0)])
                plan.append([("in", l, 2304, 256, 0)])
                plan.append([("in", l, 2560, 512, 0)])
                plan.append([("in", l, 3072, 288, 0)])
                plan.append([("out", l, 0, 512, 0)])
                plan.append([("out", l, 512, 512, 0)])
        return plan

    wplan = plan_weights()

    def ring_ap(slot, kc, c0, n):
        return V(ring_t, slot * 4096 + kc * 512 + c0, [(1, n)])

    def issue_block(n):
        slot = n % NRING
        for (kind, l, c0, ncol, dst) in wplan[n]:
            _, in_ap, _ = wblock_aps(kind, l, c0, ncol, dst)
            out_ap = V(ring_t, slot * 4096 + dst, [(512, 8), (1, ncol)])
            dma("pool", out_ap, in_ap, writes=[out_ap])

    def issue_weights(upto):
        while wstate["issued"] < min(upto, len(wplan)):
            issue_block(wstate["issued"])
            wstate["issued"] += 1

    wstate["released"] = 0

    def next_wslot():
        n = wstate["used"]
        wstate["used"] += 1
        assert wstate["issued"] > n, "weight block not prefetched"
        return n % NRING

    def wrelease(k=1):
        for _ in range(k):
            wstate["released"] += 1
            issue_weights(wstate["released"] + NRING)

    dma("sp", V(ident_t), cbf_d[:, 0:128], writes=[V(ident_t)])
    dma("sp", V(rope_t), cbf_d[:, 128:3200], writes=[V(rope_t)])
    dma("sp", V(cf_t), cf32_d[:, :], writes=[V(cf_t)])
    dma("sp", V(scT_t, 0, [(1, 24)]), cT_d.rearrange("p a b -> p (a b)"), writes=[V(scT_t, 0, [(1, 24)])])
    memset(V(VA_t), 1.0)
    memset(V(VB_t), 1.0)
    act(V(scT_t, 24, [(1, 24)]), V(scT_t, 0, [(1, 24)]), AF.Exp, scale=-1.0)
    act(V(scT_t, 24, [(1, 24)]), V(scT_t, 24, [(1, 24)]), AF.Ln, bias=1.0)
    act(V(scT_t, 24, [(1, 24)]), V(scT_t, 24, [(1, 24)]), AF.Exp, scale=-1.0)
    tt(V(scT_t, 0, [(1, 24)]), V(scT_t, 24, [(1, 24)]), V(scT_t, 0, [(1, 24)]), ALU.mult)

    pbanks[:] = [0, 2, 3]

    y32_t = y_t.bitcast(F32)

    def mods_gen(l, T_, bank_fn, dq):
        def R3(off, n):
            return V(T_, 4096 + off, [(1, n)], 0, 3)
        adab_bc = bass.AP(adab_d.tensor, l * 3 * D, [[0, 3], [1, 3 * D]])
        dma("sp", R3(0, 3072), adab_bc, writes=[R3(0, 3072)])
        ng_bc = bass.AP(ng_d.tensor, l * D, [[0, 3], [1, D]])
        dma("sp", R3(3072, 1024), ng_bc, writes=[R3(3072, 1024)])
        yield
        for cb in range(12):
            stg = cb % 2
            stg_ap = V(T_, stg * 2048, [(256, 8), (1, 256)])
            in_ap = bass.AP(adaw_d.tensor, l * D * 3 * D + cb * 256, [[3 * D, 128], [128 * 3 * D, 8], [1, 256]])
            dma(dq, stg_ap, in_ap, writes=[stg_ap])
            yield
            pt = bank_fn()
            for kc in range(8):
                mm(V(pt, 0, [(1, 256)], 0, 3), V(scT_t, kc * 3, [(1, 3)]), V(T_, stg * 2048 + kc * 256, [(1, 256)]),
                   start=(kc == 0), stop=(kc == 7))
            tt(R3(cb * 256, 256), V(pt, 0, [(1, 256)], 0, 3), R3(cb * 256, 256), ALU.add)
            yield
        stt(R3(1024, 1024), R3(1024, 1024), 1.0, R3(3072, 1024), ALU.add, ALU.mult)
        yield
        for j, src_off in ((0, 1024), (1, 0), (2, 2048)):
            o_ap = bass.AP(mods_d.tensor, l * 9 * D + j * D, [[3 * D, 3], [1, D]])
            dma("sp", o_ap, R3(src_off, 1024), reads=[R3(src_off, 1024)], writes=[("modsd", l, l + 1, 0, 1)])
        yield

    for _ in mods_gen(0, fp_t, psP, "sp"):
        pass

    issue_weights(NRING)
    nonlocal_dummy = None

    def proj(t, slot, c0, ncol, pt=None, kparts=128):
        if pt is None:
            pt = psP()
        for kc in range(8):
            mm(V(pt, 0, [(1, ncol)]), V(hT_t, kc * 2304 + t * 128, [(1, 128)]), ring_ap(slot, kc, c0, ncol),
               start=(kc == 0), stop=(kc == 7))
        return pt

    def rope(dst, src, lt, which, G, t1o=6144, t2o=7168, scr=None):
        scr = fp_t if scr is None else scr
        if which == "A":
            W, Fq, cb, sbo = 32, 8, 0, 512
        else:
            W, Fq, cb, sbo = 64, 16, 1024, 2048
        cos_bc = V(rope_t, cb + lt * W, [(0, G), (1, W)])
        t1 = V(scr, t1o, [(1, G * W)])
        t1v = V(scr, t1o, [(W, G), (1, W)])
        t2 = V(scr, t2o, [(1, G * W)])
        in0v = src(0, [(W, G), (1, W)])
        S.op("dve", lambda e, in0v=in0v, t1v=t1v, cos_bc=cos_bc: e.tensor_tensor(out=t1v, in0=in0v, in1=cos_bc, op=ALU.mult),
             reads=[src(0, [(1, G * W)]), cos_bc], writes=[t1])
        for half in range(2):
            o = V(scr, t2o + half * Fq, [(W, G), (2 * Fq, 2), (1, Fq)])
            i0 = src((1 - half) * Fq, [(W, G), (2 * Fq, 2), (1, Fq)])
            i1 = V(rope_t, sbo + lt * W + half * Fq, [(0, G), (2 * Fq, 2), (1, Fq)])
            S.op("dve", lambda e, o=o, i0=i0, i1=i1: e.tensor_tensor(out=o, in0=i0, in1=i1, op=ALU.mult),
                 reads=[src(0, [(1, G * W)]), V(rope_t, sbo + lt * W, [(1, W)])], writes=[t2])
        dst(t1, t2)

    def silu_gate(out_bf, z_ps, n, tmp):
        act(tmp, z_ps, AF.Exp, scale=-1.0)
        act(tmp, tmp, AF.Ln, bias=1.0)
        act(tmp, tmp, AF.Exp, scale=-1.0)
        tt(out_bf, tmp, z_ps, ALU.mult)

    out_toks = []

    S.marks = []

    def chk(name):
        if len(name) <= 8 and not any(ch.isdigit() for ch in name[2:]):
            S.marks.append((name, len(S.ops["pe"])))
        if stop_at == name:
            raise _Stop()

    def main_loops():
      for b in range(nb):
        for l in range(depth):
            chk("prologue")
            last = (l == DEPTH - 1)
            lam_init = 0.8 - 0.6 * math.exp(-0.3 * l)
            tiles = list(range(NT))
            out_tiles = list(range(2, NT)) if last else tiles

            pv_bc = bass.AP(pvec_d.tensor, l * NPV, [[0, 128], [1, NPV]])
            dma("sp", V(pbc_t), pv_bc, writes=[V(pbc_t)])
            dma("sp", V(bsT_t), sgbT_d[l], writes=[V(bsT_t)])
            dma("pool", V(wsT_t), sgwT_d[l].rearrange("p a b -> p (a b)"), writes=[V(wsT_t)])
            dma("pool", V(w2bd_t, 0, [(1, 256)], 0, 32), w2bd_d[l], writes=[V(w2bd_t, 0, [(1, 256)], 0, 32)])
            tt(V(sm_t, 0, [(1, 64)]), pb("lq", 64), pb("lk", 64), ALU.mult)
            red(stat(4, 2), V(sm_t, 0, [(32, 2), (1, 32)]))
            act(stat(0, 2), stat(4, 2), AF.Exp)
            tt(stat(2), stat(0), stat(1), ALU.subtract)
            ts(stat(3), stat(2), float(lam_init), -1.0, ALU.add, ALU.mult)
            ts(V(sgeff_t), pb("subg", 64), float(1.0 - lam_init), None, ALU.mult)
            neglam = stat(3)

            for slot, (row, j) in enumerate(((2, 0), (2, 1), (b, 0), (b, 1))):
                src = bass.AP(mods_d.tensor, l * 9 * D + row * 3 * D + j * D, [[0, 128], [1, D]])
                dma("sp", mods(slot), src, reads=[("modsd", l, l + 1, 0, 1)], writes=[mods(slot)])

            chk("params")
            def run_chains(gens):
                gens = list(gens)
                while gens:
                    for g_ in list(gens):
                        try:
                            next(g_)
                        except StopIteration:
                            gens.remove(g_)

            def sAp(par, off=0, n=1024):
                return V(fp_t, 6144 + par * 1024 + off, [(1, n)])

            def stage1_chain(par, tl):
                for t in tl:
                    sl = par
                    if l == 0:
                        src = ctx_d[b, t * 128:(t + 1) * 128, :] if t < 2 else x_d[b, (t - 2) * 128:(t - 1) * 128, :]
                        dma("sp", xt(sl), src, writes=[xt(sl)])
                    else:
                        dma("sp", xt(sl), xres_d[b, t * 128:(t + 1) * 128, :], reads=[("xres", b * NT + t, b * NT + t + 1, 0, 1)],
                            writes=[xt(sl)])
                    yield
                    act(sAp(par), xt(sl), AF.Square, accum=stat(8 + 2 * par))
                    yield
                    act(stat(9 + 2 * par), stat(8 + 2 * par), AF.Ln, scale=float(1.0 / D), bias=float(EPS))
                    yield
                    act(stat(9 + 2 * par), stat(9 + 2 * par), AF.Exp, scale=-0.5)
                    yield
                    ms = 0 if t < 2 else 2
                    stt(sAp(par), xt(sl), stat(9 + 2 * par), mods(ms), ALU.mult, ALU.mult)
                    yield
                    hb = V(hbf_t, par * 1024, [(1, 1024)])
                    tt(hb, sAp(par), mods(ms + 1), ALU.add)
                    yield
                    pt = psT()
                    for kc in range(8):
                        tr(V(pt, kc * 128, [(1, 128)]), V(hbf_t, par * 1024 + kc * 128, [(1, 128)]), inc=(kc == 7))
                    cp(V(hT_t, t * 128, [(2304, 8), (1, 128)]), V(pt, 0, [(128, 8), (1, 128)]),
                       eng=("dve" if par else "act"))
                    s1done.add(t)
                    yield

            s1done = set()

            chk("stage1")
            def attn_branch(which, phase, ws_in):
                isA = which == "A"
                kT_t = kTA_t if isA else kTB_t
                V_t = VA_t if isA else VB_t
                nkc = 2 if isA else 1
                nvh = 4 if isA else 2
                ycol = 0 if isA else 256
                scale = float(32 ** -0.5) if isA else float(64 ** -0.5)
                def pass1_chain(par):
                    kw = 256 if isA else 128
                    vw = 256 if isA else 128
                    if isA:
                        pbk = ps[0] if par == 0 else ps[3]
                        krt, kro = (qr_t, 0) if par == 0 else (vnb_t, 0)
                        scr, t1o, t2o, nrm = (oT_t, 0, 256, 0) if par == 0 else (cmb_t, 0, 256, 0)
                        sti = 16
                    else:
                        pbk = ps[2] if par == 0 else ps[5]
                        krt, kro = (qr_t, 256) if par == 0 else (qr_t, 384)
                        scr, t1o, t2o, nrm = (oT_t, 512, 640, 768) if par == 0 else (uu_t, 0, 128, 256)
                        sti = 16 if par == 0 else 18
                    for t in tiles[par::2]:
                        while t not in s1done:
                            yield
                        for kc in range(8):
                            mm(V(pbk, 0, [(1, kw + vw)]), V(hT_t, kc * 2304 + t * 128, [(1, 128)]), ring_ap(ws_in, kc, 0, kw + vw),
                               start=(kc == 0), stop=(kc == 7))
                        yield
                        cp(V(V_t, t * nvh * 65, [(65, nvh), (1, 64)]), V(pbk, kw, [(64, nvh), (1, 64)]), eng="act")
                        kr = V(krt, kro, [(1, kw)])
                        if isA:
                            ksrc = lambda off, dims, pbk=pbk: V(pbk, off, dims)
                        else:
                            act(V(scr, nrm, [(1, 128)]), V(pbk, 0, [(1, 128)]), AF.Square)
                            yield
                            red(stat(sti, 2), V(scr, nrm, [(64, 2), (1, 64)]))
                            yield
                            act(stat(sti, 2), stat(sti, 2), AF.Ln, scale=float(1.0 / 64), bias=float(EPS))
                            yield
                            act(stat(sti, 2), stat(sti, 2), AF.Exp, scale=-0.5)
                            yield
                            tt(V(scr, nrm, [(64, 2), (1, 64)]), V(pbk, 0, [(64, 2), (1, 64)]),
                               V(st_t, sti, [(1, 2), (0, 64)]), ALU.mult)
                            yield
                            tt(V(scr, nrm, [(64, 2), (1, 64)]), V(scr, nrm, [(64, 2), (1, 64)]),
                               V(pbc_t, PO["kng"], [(0, 2), (1, 64)]), ALU.mult)
                            ksrc = lambda off, dims, scr=scr, nrm=nrm: V(scr, nrm + off, dims)
                        yield
                        if t >= 2:
                            G = kw // (32 if isA else 64)
                            rope(lambda a, bb_, kr=kr: tt(kr, a, bb_, ALU.add), ksrc, t - 2, which, G, t1o, t2o, scr)
                        else:
                            cp(kr, ksrc(0, [(1, kw)]), eng="dve")
                        yield
                        ptt = psT()
                        for c in range(nkc):
                            tr(V(ptt, c * 128, [(1, 128)]), V(krt, kro + c * 128, [(1, 128)]), inc=(c == nkc - 1))
                        cp(V(kT_t, t * 128, [(2304, nkc), (1, 128)]), V(ptt, 0, [(128, nkc), (1, 128)]), eng="act")
                        yield

                if phase in ("p1e", "p1o"):
                    return pass1_chain(0 if phase == "p1e" else 1)
                ws = ws_in
                pbanks[:] = [1]
                groups = ([] if last else [[0, 1]]) + [[2 + 4 * g + j for j in range(4)] for g in range(4)]
                gslots = [nxt("qTm", 2) for _ in groups]

                def qproj_stages(gs, j, t):
                    cx = {}

                    def st_proj(lo, hi):
                        def f():
                            if lo == 0:
                                cx["pt"] = psP()
                                cx["qs"] = nxt("qr", 2)
                            pt = cx["pt"]
                            for kc in range(lo, hi):
                                mm(V(pt, 0, [(1, 512)]), V(hT_t, kc * 2304 + t * 128, [(1, 128)]), ring_ap(ws, kc, 0, 512),
                                   start=(kc == 0), stop=(kc == 7))
                        return f

                    def st_e1():
                        pt = cx["pt"]
                        gtmp = V(uu_t, 0, [(1, 256)])
                        act(gtmp, V(pt, 256, [(1, 256)]), AF.Exp, scale=-1.0)
                        act(gtmp, gtmp, AF.Ln, bias=1.0)
                        act(gtmp, gtmp, AF.Exp, scale=-1.0)
                        if not isA:
                            act(sC(0, 256), V(pt, 0, [(1, 256)]), AF.Square)

                    def st_e2():
                        if not isA:
                            red(stat(16, 4), V(fp_t, 7680, [(64, 4), (1, 64)]))

                    def st_e3():
                        if not isA:
                            rstd_from(stat(16, 4), stat(16, 4), 1.0 / 64)

                    def st_e4():
                        pt = cx["pt"]
                        qs = cx["qs"]
                        qrv = V(qr_t, qs * 256, [(1, 256)])
                        if isA:
                            qsrc = lambda off, dims: V(pt, off, dims)
                            if t >= 2:
                                rope(lambda a, bb_: tt(qrv, a, bb_, ALU.add), qsrc, t - 2, "A", 8)
                            else:
                                cp(qrv, V(pt, 0, [(1, 256)]), eng="dve")
                        else:
                            tt(V(fp_t, 7680, [(64, 4), (1, 64)]), V(pt, 0, [(64, 4), (1, 64)]),
                               V(st_t, 16, [(1, 4), (0, 64)]), ALU.mult)
                            tt(V(fp_t, 7680, [(64, 4), (1, 64)]), V(fp_t, 7680, [(64, 4), (1, 64)]),
                               V(pbc_t, PO["qng"], [(0, 4), (1, 64)]), ALU.mult)
                            qsrc = lambda off, dims: V(fp_t, 7680 + off, dims)
                            qperm = V(qr_t, qs * 256, [(64, 2), (128, 2), (1, 64)])
                            if t >= 2:
                                rope(lambda a, bb_, qperm=qperm: tt(qperm, V(fp_t, 6144, [(128, 2), (64, 2), (1, 64)]),
                                                                   V(fp_t, 7168, [(128, 2), (64, 2), (1, 64)]), ALU.add),
                                     qsrc, t - 2, "B", 4)
                            else:
                                cp(qperm, V(fp_t, 7680, [(128, 2), (64, 2), (1, 64)]), eng="dve")
                        tt(V(gate_t, gs * 1024 + j * 256, [(1, 256)]), V(uu_t, 0, [(1, 256)]), V(pt, 256, [(1, 256)]), ALU.mult)

                    def st_tr():
                        qs = cx["qs"]
                        ptt = psT()
                        cx["ptt"] = ptt
                        for c in range(2):
                            tr(V(ptt, c * 128, [(1, 128)]), V(qr_t, qs * 256 + c * 128, [(1, 128)]), inc=(c == 1))

                    def st_mask():
                        ptt = cx["ptt"]
                        for c in range(2):
                            if isA:
                                tt(V(fpb_t, gs * 4096 + (4 * c) * 512 + j * 128, [(512, 4), (1, 128)]),
                                   V(ptt, c * 128, [(0, 4), (1, 128)]), V(cf_t, 512, [(1, 4), (0, 128)]), ALU.mult)
                            else:
                                tt(V(fpb_t, gs * 4096 + c * 512 + j * 128, [(1024, 2), (1, 128)]),
                                   V(ptt, c * 128, [(0, 2), (1, 128)]), V(cf_t, 645, [(1, 2), (0, 128)]), ALU.mult)

                    return [st_proj(0, 4), st_proj(4, 8), st_e1, st_e2, st_e3, st_e4, st_tr, st_mask]

                def qproj_tile(gs, j, t):
                    for f in qproj_stages(gs, j, t):
                        f()

                for gi, grp in enumerate(groups):
                    isctx = grp[0] < 2
                    nq = len(grp) * 128
                    kts = [0, 1] if isctx else list(range(NT))
                    gs = gslots[gi]
                    if gi == 0:
                        for j, t in enumerate(grp):
                            qproj_tile(gs, j, t)
                    nxt_calls = []
                    if gi + 1 < len(groups):
                        nxt_calls = [(gslots[gi + 1], j, t) for j, t in enumerate(groups[gi + 1])]
                    nj = len(grp)
                    if isA:
                        items = [(h, m, ki, kt) for h in range(4) for m in range(2) for ki, kt in enumerate(kts)]
                    else:
                        items = [(hq, 0, ki, kt) for hq in range(4) for ki, kt in enumerate(kts)]
                    nk = len(kts)
                    obank = {}

                    def post(h, m):
                        ob = obank[(h, m)]
                        osl = nxt("oT", 2)
                        cp(V(oT_t, osl * 512, [(1, nq)], 0, 65), V(ob, 0, [(1, nq)], 0, 65), eng="dve")
                        return post2(h, m, osl)

                    def post2(h, m, osl):
                        def s_tr():
                            for j in range(nj):
                                if isA:
                                    cbk = ps[6 + j // 2]
                                    col = (m * 2 + j % 2) * 65
                                else:
                                    cbk = ps[6 + h % 2]
                                    col = j * 65
                                S.op("pe", lambda e, cbk=cbk, col=col, osl=osl, j=j: e.transpose(
                                    V(cbk, col, [(1, 65)]), V(oT_t, osl * 512 + j * 128, [(1, 128)], 0, 65),
                                    V(cf_t, 517, [(1, 65)], 0, 65)),
                                    reads=[V(oT_t, osl * 512 + j * 128, [(1, 128)], 0, 65)], writes=[V(cbk, col, [(1, 65)])])
                        stages = [(0, s_tr)]
                        if isA and m == 0:
                            return stages
                        t0_ = grp[0]
                        if not isA:
                            def s_b():
                                cbk = ps[6 + h % 2]
                                recip(stat(64, nj), V(cbk, 64, [(65, nj)]))
                                tt(V(cmb_t, 0, [(64, nj), (1, 64)]), V(cbk, 0, [(65, nj), (1, 64)]),
                                   V(st_t, 64, [(1, nj), (0, 64)]), ALU.mult)
                                tt(V(y_t, t0_ * 1024 + ycol + h * 64, [(1024, nj), (1, 64)]), V(cmb_t, 0, [(64, nj), (1, 64)]),
                                   V(gate_t, gs * 1024 + h * 64, [(256, nj), (1, 64)]), ALU.mult)
                            stages.append((2, s_b))
                            return stages
                        nbk = (nj + 1) // 2

                        def s_a1():
                            for bk in range(nbk):
                                cbk = ps[6 + bk]
                                so = 64 + bk * 8
                                oo = bk * 128
                                recip(stat(so, 4), V(cbk, 64, [(65, 4)]))
                                tt(stat(so + 4, 2), stat(so + 2, 2), V(st_t, 3, [(0, 2)]), ALU.mult)
                                tt(V(cmb_t, 256 + oo, [(64, 2), (1, 64)]), V(cbk, 130, [(65, 2), (1, 64)]),
                                   V(st_t, so + 4, [(1, 2), (0, 64)]), ALU.mult)
                                tt(V(cmb_t, oo, [(64, 2), (1, 64)]), V(cbk, 0, [(65, 2), (1, 64)]),
                                   V(st_t, so, [(1, 2), (0, 64)]), ALU.mult)
                                tt(V(cmb_t, oo, [(1, 128)]), V(cmb_t, oo, [(1, 128)]), V(cmb_t, 256 + oo, [(1, 128)]), ALU.add)
                                tt(V(cmb_t, 256 + oo, [(1, 128)]), V(cmb_t, oo, [(1, 128)]), V(cmb_t, oo, [(1, 128)]), ALU.mult)
                                red(stat(so + 6, 2), V(cmb_t, 256 + oo, [(64, 2), (1, 64)]))

                        def s_a2():
                            for bk in range(nbk):
                                so = 64 + bk * 8
                                act(stat(so + 6, 2), stat(so + 6, 2), AF.Ln, scale=float(1.0 / 64), bias=float(EPS))
                                act(stat(so + 6, 2), stat(so + 6, 2), AF.Exp, scale=-0.5)

                        def s_a3():
                            for bk in range(nbk):
                                so = 64 + bk * 8
                                oo = bk * 128
                                tt(V(cmb_t, oo, [(64, 2), (1, 64)]), V(cmb_t, oo, [(64, 2), (1, 64)]),
                                   V(st_t, so + 6, [(1, 2), (0, 64)]), ALU.mult)
                                tt(V(cmb_t, oo, [(64, 2), (1, 64)]), V(cmb_t, oo, [(64, 2), (1, 64)]),
                                   V(sgeff_t, 0, [(0, 2), (1, 64)]), ALU.mult)
                                tj = grp[2 * bk]
                                tt(V(y_t, tj * 1024 + ycol + h * 64, [(1024, 2), (1, 64)]), V(cmb_t, oo, [(64, 2), (1, 64)]),
                                   V(gate_t, gs * 1024 + (2 * bk) * 256 + h * 64, [(256, 2), (1, 64)]), ALU.mult)
                        stages += [(2, s_a1), (6, s_a2), (8, s_a3)]
                        return stages

                    LOOK = 2
                    pending = []
                    nsteps = len(items)
                    sched_at = {}
                    if nxt_calls:
                        spacing = max(1, nsteps // len(nxt_calls))
                        for qi, call in enumerate(nxt_calls):
                            base = 1 + qi * spacing
                            for si, f in enumerate(qproj_stages(*call)):
                                off = (0, 1, 3, 5, 7, 9, 12, 15)[si] if spacing >= 17 else 0
                                sched_at.setdefault(base + off, []).append(f)
                    delayed = []
                    for idx in range(len(items) + LOOK):
                        for f in sched_at.pop(idx, []):
                            f()
                        while delayed and delayed[0][0] <= idx:
                            delayed.pop(0)[1]()
                        if idx < len(items):
                            h, m, ki, kt = items[idx]
                            sbk = psS()
                            if isA:
                                hm = h * 2 + m
                                mm(V(sbk, 0, [(1, nq)]),
                                   V(kT_t, (hm // 4) * 2304 + kt * 128, [(1, 128)]),
                                   V(fpb_t, gs * 4096 + hm * 512, [(1, nq)]),
                                   start=True, stop=True)
                            else:
                                mm(V(sbk, 0, [(1, nq)]),
                                   V(kT_t, kt * 128, [(1, 128)]),
                                   V(fpb_t, gs * 4096 + h * 512, [(1, nq)]),
                                   start=True, stop=True)
                            pi = nxt("pT", 4)
                            act(V(pT_t, pi * 512, [(1, nq)]), V(sbk, 0, [(1, nq)]), AF.Exp, scale=scale)
                            pending.append((h, m, ki, kt, pi))
                        if idx >= LOOK and pending:
                            h, m, ki, kt, pi = pending.pop(0)
                            if ki == 0:
                                obank[(h, m)] = ps[4 + nxt("OT", 2)]
                            hv = h if isA else h // 2
                            mm(V(obank[(h, m)], 0, [(1, nq)]),
                               V(V_t, (kt * nvh + hv) * 65, [(1, 128)]),
                               V(pT_t, pi * 512, [(1, nq)]),
                               start=(ki == 0), stop=(ki == nk - 1))
                            if ki == nk - 1:
                                for dl, fn in post(h, m):
                                    delayed.append((idx + 3 + (dl if nk >= 8 else 0), fn))
                                delayed.sort(key=lambda x: x[0])
                    for _, fn in delayed:
                        fn()
                    for k_ in sorted(sched_at):
                        for f in sched_at[k_]:
                            f()

            wsA1 = next_wslot()
            wsA2 = next_wslot()
            wsB1 = next_wslot()
            chains_ = [stage1_chain(0, tiles[0::2]), stage1_chain(1, tiles[1::2]),
                       attn_branch("A", "p1e", wsA1), attn_branch("A", "p1o", wsA1),
                       attn_branch("B", "p1e", wsB1), attn_branch("B", "p1o", wsB1)]
            if b == 0 and l + 1 < depth:
                chains_.append(mods_gen(l + 1, y32_t, lambda: ps[4], "pool"))
            run_chains(chains_)
            wrelease()
            chk("A1")
            attn_branch("A", "p2", wsA2)
            wrelease()
            chk("A")
            wrelease()
            chk("B1")
            wsB2 = next_wslot()
            attn_branch("B", "p2", wsB2)
            wrelease()
            chk("B")

            pbanks[:] = [0, 2, 3]
            ws1 = next_wslot()
            ws2 = next_wslot()

            def sgu_chain(par, tl):
                pt, pz, po = (ps[0], ps[2], ps[4]) if par == 0 else (ps[3], ps[5], ps[6])
                sb_ = 32 + 48 * par
                vnb = V(vnb_t) if par == 0 else V(fpb_t, 2 * 4608, [(1, 256)])
                uu = (lambda off, n: V(uu_t, off, [(1, n)])) if par == 0 else (lambda off, n: V(fp_t, 4096 + off, [(1, n)]))
                for t in tl:
                    for kc in range(8):
                        mm(V(pt, 0, [(1, 512)]), V(hT_t, kc * 2304 + t * 128, [(1, 128)]), ring_ap(ws1, kc, 0, 512),
                           start=(kc == 0), stop=(kc == 7))
                    yield
                    for kc in range(8):
                        mm(V(pz, 0, [(1, 256)]), V(hT_t, kc * 2304 + t * 128, [(1, 128)]), ring_ap(ws2, kc, 0, 256),
                           start=(kc == 0), stop=(kc == 7))
                    yield
                    S.op("dve", lambda e, pt=pt, sb_=sb_: e.bn_stats(out=V(st_t, sb_, [(1, 6)]), in_=V(pt, 256, [(1, 256)])),
                         reads=[V(pt, 256, [(1, 256)])], writes=[V(st_t, sb_, [(1, 6)])])
                    cp(uu(0, 256), V(pt, 0, [(1, 256)]), eng="act")
                    yield
                    S.op("dve", lambda e, sb_=sb_: e.bn_aggr(out=V(st_t, sb_ + 8, [(1, 2)]), in_=V(st_t, sb_, [(1, 6)])),
                         reads=[V(st_t, sb_, [(1, 6)])], writes=[V(st_t, sb_ + 8, [(1, 2)])])
                    act(sAp(par, 512, 256), V(pz, 0, [(1, 256)]), AF.Exp, scale=-1.0)
                    yield
                    act(stat(sb_ + 10), stat(sb_ + 9), AF.Ln, scale=1.0, bias=float(EPS))
                    act(sAp(par, 512, 256), sAp(par, 512, 256), AF.Ln, bias=1.0)
                    yield
                    act(stat(sb_ + 10), stat(sb_ + 10), AF.Exp, scale=-0.5)
                    act(sAp(par, 512, 256), sAp(par, 512, 256), AF.Exp, scale=-1.0)
                    yield
                    stt(stat(sb_ + 11), stat(sb_ + 8), -1.0, stat(sb_ + 10), ALU.mult, ALU.mult)
                    yield
                    ts(sAp(par, 0, 256), V(pt, 256, [(1, 256)]), stat(sb_ + 10), stat(sb_ + 11), ALU.mult, ALU.add)
                    yield
                    tt(sAp(par, 0, 256), sAp(par, 0, 256), pb("lng", 256), ALU.mult)
                    yield
                    tt(vnb, sAp(par, 0, 256), pb("lnb", 256), ALU.add)
                    yield
                    for g in range(4):
                        mm(V(po, g * 64, [(1, 64)]), V(wsT_t, g * 128, [(1, 128)]),
                           bass.AP(vnb.tensor, vnb.offset + g * 64, [list(vnb.ap[0]), [1, 64]]), start=True, stop=True)
                    tt(sAp(par, 256, 256), sAp(par, 512, 256), V(pz, 0, [(1, 256)]), ALU.mult)
                    yield
                    for g in range(4):
                        stt(uu(256 + g * 64, 64), V(po, g * 64, [(1, 64)]), V(bsT_t, g, [(1, 1)]),
                            uu(g * 64, 64), ALU.add, ALU.mult)
                    yield
                    tt(V(y_t, t * 1024 + 512, [(1, 256)]), uu(256, 256), sAp(par, 256, 256), ALU.mult)
                    yield

            run_chains([sgu_chain(0, out_tiles[0::2]), sgu_chain(1, out_tiles[1::2])])
            wrelease(2)
            chk("C")
            pbanks[:] = [0]
            wd1 = next_wslot()
            wd2 = next_wslot()

            class _B:
                pass

            bF = _B()
            bF.P, bF.S, bF.O = ps[0], ps[2], ps[6]
            bF.rTb, bF.qeb, bF.keb, bF.vb, bF.qeT = V(rTb_t, 0, [(1, 128)], 0, 32), V(qeb_t), V(keb_t), V(vb_t), V(qeT_t)
            bF.kTm = lambda off, dims: V(kTm_t, off, dims)
            bF.am = lambda off, dims: V(am_t, off, dims)
            bF.Sst, bF.Sb, bF.Wt = V(Sst_t), (lambda off, n: V(Sb_t, off, [(1, n)])), V(Wt_t)
            bF.gl = lambda off, n=128: V(gl_t, off, [(1, n)])
            bF.E = stat(48)
            bB = _B()
            bB.P, bB.S, bB.O = ps[5], ps[3], ps[7]
            FB = 7168 * 2
            bB.rTb, bB.qeb, bB.keb = V(fpb_t, FB, [(1, 128)], 0, 32), V(fpb_t, FB + 128, [(1, 128)]), V(fpb_t, FB + 256, [(1, 128)])
            bB.vb, bB.qeT = V(fpb_t, FB + 384, [(1, 256)]), V(fpb_t, FB + 640, [(1, 128)])
            bB.kTm = lambda off, dims: V(fpb_t, FB + 768 + off, dims)
            bB.am = lambda off, dims: V(fpb_t, FB + 1280 + off, dims)
            bB.Sb = lambda off, n: V(fpb_t, FB + 1792 + off, [(1, n)])
            bB.Sst, bB.Wt = V(fp_t, 6144, [(1, 256)]), V(fp_t, 6400, [(1, 256)])
            bB.gl = lambda off, n=128: V(fp_t, 6656 + off, [(1, n)])
            bB.E = stat(49)

            bF.rTb128 = V(rTb_t)
            bB.rTb128 = V(fpb_t, FB, [(1, 128)])
            bF.qeT2 = [V(qeT_t), V(pT_t, 0, [(1, 128)])]
            bF.vb2 = [V(vb_t), V(pT_t, 256, [(1, 256)])]
            bF.am2 = [lambda off, dims: V(am_t, off, dims), lambda off, dims: V(pT_t, 512 + off, dims)]
            bF.E2 = [stat(48), stat(50)]
            bF.Wt2 = [V(Wt_t), V(oT_t, 0, [(1, 256)])]
            bB.qeT2 = [bB.qeT, V(pT_t, 1024, [(1, 128)])]
            bB.vb2 = [bB.vb, V(pT_t, 1024 + 256, [(1, 256)])]
            bB.am2 = [bB.am, lambda off, dims: V(pT_t, 1024 + 512 + off, dims)]
            bB.E2 = [stat(49), stat(51)]
            bB.Wt2 = [bB.Wt, V(oT_t, 256, [(1, 256)])]
            gdone = {}

            def qkv(t, off, n):
                if t < 9:
                    return V(kTA_t, t * 512 + off, [(1, n)])
                return V(VA_t, (t - 9) * 512 + off, [(1, n)])

            def gla_prep(sweep, order, B, C_, kp, ks, par):
                Ltri = LtriF if sweep == 0 else LtriB
                for i, t in enumerate(order):
                    if i % 2 != par:
                        continue
                    while gdone.get(ks, 0) < i - 1:
                        yield
                    s_ = i % 2
                    qeT, vb, am, E, Wt = B.qeT2[s_], B.vb2[s_], B.am2[s_], B.E2[s_], B.Wt2[s_]
                    need_out = (t in out_tiles)
                    p1 = C_.bank
                    vb = qkv(t, 256, 256)
                    mm(V(C_.bank, 0, [(1, 128)]), V(kTB_t, t * 128, [(1, 128)], 0, 32), V(w2bd_t, sweep * 128, [(1, 128)], 0, 32),
                       start=True, stop=True)
                    yield
                    tt(C_.gl(384), V(C_.bank, 0, [(1, 128)]), pb("gb", 128, sweep * 128), ALU.add)
                    yield
                    act(C_.gl(384), C_.gl(384), AF.Exp, scale=-1.0)
                    yield
                    act(C_.gl(0), C_.gl(384), AF.Ln, bias=1.0)
                    yield
                    mm(V(C_.bank, 0, [(1, 128)]), Ltri, C_.gl(0), start=True, stop=True)
                    mm(V(C_.bank, 128, [(1, 1)]), C_.gl(0), ones_c, start=True, stop=True)
                    yield
                    act(E, V(C_.bank, 128, [(1, 1)]), AF.Exp, scale=float(-1.0 / 16))
                    act(C_.gl(128), V(C_.bank, 0, [(1, 128)]), AF.Exp, scale=float(-1.0 / 16))
                    act(C_.gl(256), V(C_.bank, 0, [(1, 128)]), AF.Exp, scale=float(1.0 / 16))
                    yield
                    tt(C_.keb, qkv(t, 128, 128), C_.gl(256), ALU.mult)
                    if need_out:
                        stt(C_.qeb, qkv(t, 0, 128), float(32 ** -0.5), C_.gl(128), ALU.mult, ALU.mult)
                    yield
                    mm(V(p1, 0, [(1, 256)]), C_.keb, vb, start=True, stop=True)
                    yield
                    stt(Wt, V(p1, 0, [(1, 256)]), E, blockmask, ALU.mult, ALU.mult)
                    yield
                    if need_out:
                        ptq = psT()
                        tr(V(ptq, 0, [(1, 128)]), C_.qeb)
                        tr(V(ptq, 128, [(1, 128)]), C_.keb)
                        cp(qeT, V(ptq, 0, [(1, 128)]), eng="dve")
                        tt(C_.kTm(0, [(128, 4), (1, 128)]), V(ptq, 0, [(0, 4), (1, 128)]),
                           V(cf_t, 512, [(1, 4), (0, 128)]), ALU.mult)
                        cp(C_.keT, V(ptq, 128, [(1, 128)]), eng="act")
                        yield
                        mm(V(C_.bank, 0, [(1, 512)]), C_.keT, C_.kTm(0, [(1, 512)]), start=True, stop=True)
                        yield
                        tt(am(0, [(128, 4), (1, 128)]), V(C_.bank, 0, [(128, 4), (1, 128)]),
                           V(cf_t, 0 if sweep == 0 else 128, [(0, 4), (1, 128)]), ALU.mult)
                        yield
                    pdone[kp].add(i)

            def gla_seq(sweep, order, B, kp, ks):
                memset(B.Sst, 0.0)
                memset(B.Sb(0, 256), 0.0)
                yield
                for i, t in enumerate(order):
                    while i not in pdone[kp]:
                        yield
                    s_ = i % 2
                    qeT, am, E, Wt = B.qeT2[s_], B.am2[s_], B.E2[s_], B.Wt2[s_]
                    vb = qkv(t, 256, 256)
                    if t in out_tiles:
                        py = ps[4]
                        mm(V(py, 0, [(1, 256)]), qeT, B.Sb(0, 256), start=True, stop=False)
                        for h in range(4):
                            mm(V(py, h * 64, [(1, 64)]), am(h * 128, [(1, 128)]),
                               bass.AP(vb.tensor, vb.offset + h * 64, [list(vb.ap[0]), [1, 64]]), start=False, stop=True)
                        if sweep == 0:
                            cp(V(y_t, t * 1024 + 768, [(1, 256)]), V(py, 0, [(1, 256)]), eng="act")
                        else:
                            cp(V(fpb_t, t * 256, [(1, 256)]), V(py, 0, [(1, 256)]), eng="act")
                        yield
                    stt(B.Sst, B.Sst, E, Wt, ALU.mult, ALU.add)
                    yield
                    cp(B.Sb(0, 256), B.Sst, eng="dve")
                    yield
                    gdone[ks] = i + 1

            for cg in range(5):
                ncol = 512 if cg < 4 else 256
                pr_ = ps[(2, 3)[cg % 2]]
                for kc in range(8):
                    mm(V(pr_, 0, [(1, ncol)], 0, 32), ring_ap(wd2, kc, 256, 32), V(hT_t, kc * 2304 + cg * 512, [(1, ncol)]),
                       start=(kc == 0), stop=(kc == 7))
                cp(V(kTB_t, cg * 512, [(1, ncol)], 0, 32), V(pr_, 0, [(1, ncol)], 0, 32), eng=("act" if cg % 2 else "dve"))
            qbanks = (ps[0], ps[5], ps[4], ps[6])
            for t in tiles:
                pq_ = qbanks[t % 4]
                for kc in range(8):
                    mm(V(pq_, 0, [(1, 512)]), V(hT_t, kc * 2304 + t * 128, [(1, 128)]), ring_ap(wd1, kc, 0, 512),
                       start=(kc == 0), stop=(kc == 7))
                cp(qkv(t, 0, 512), V(pq_, 0, [(1, 512)]), eng=("act" if t % 2 else "dve"))
            ordB = [1, 0] + list(range(NT - 1, 1, -1))
            pdone = {"pF": set(), "pB": set()}
            cF0, cF1, cB0, cB1 = _B(), _B(), _B(), _B()
            cF0.bank, cF1.bank, cB0.bank, cB1.bank = ps[0], ps[2], ps[3], ps[5]
            cF0.gl, cF0.qeb, cF0.keb, cF0.kTm, cF0.keT = bF.gl, bF.qeb, bF.keb, bF.kTm, bF.rTb128
            cB0.gl, cB0.qeb, cB0.keb, cB0.kTm, cB0.keT = bB.gl, bB.qeb, bB.keb, bB.kTm, bB.rTb128
            for c_, fo, ho in ((cF1, 4096, 0), (cB1, 4608, 1024)):
                c_.gl = lambda off, n=128, fo=fo: V(fp_t, fo + off, [(1, n)])
                c_.qeb = V(hbf_t, ho, [(1, 128)])
                c_.keb = V(hbf_t, ho + 128, [(1, 128)])
                c_.keT = V(hbf_t, ho + 256, [(1, 128)])
                c_.kTm = lambda off, dims, ho=ho: V(hbf_t, ho + 384 + off, dims)
            run_chains([gla_prep(0, tiles, bF, cF0, "pF", "sF", 0), gla_prep(1, ordB, bB, cB0, "pB", "sB", 0),
                        gla_prep(0, tiles, bF, cF1, "pF", "sF", 1), gla_prep(1, ordB, bB, cB1, "pB", "sB", 1),
                        gla_seq(0, tiles, bF, "pF", "sF"), gla_seq(1, ordB, bB, "pB", "sB")])
            def dfin_chain(fs, tl):
                pz = ps[0] if fs == 0 else ps[2]
                o_ = lambda off, dims: V(fp_t, 6144 + fs * 1024 + off, dims)
                for t in tl:
                    yD = V(y_t, t * 1024 + 768, [(1, 256)])
                    for kc in range(8):
                        mm(V(pz, 0, [(1, 256)]), V(hT_t, kc * 2304 + t * 128, [(1, 128)]), ring_ap(wd2, kc, 0, 256),
                           start=(kc == 0), stop=(kc == 7))
                    tt(o_(0, [(1, 256)]), V(fpb_t, t * 256, [(1, 256)]), yD, ALU.add)
                    yield
                    act(o_(256, [(1, 256)]), o_(0, [(1, 256)]), AF.Square)
                    act(o_(512, [(1, 256)]), V(pz, 0, [(1, 256)]), AF.Exp, scale=-1.0)
                    yield
                    red(stat(52 + 4 * fs, 4), o_(256, [(64, 4), (1, 64)]))
                    act(o_(512, [(1, 256)]), o_(512, [(1, 256)]), AF.Ln, bias=1.0)
                    yield
                    act(stat(52 + 4 * fs, 4), stat(52 + 4 * fs, 4), AF.Ln, scale=float(1.0 / 64), bias=float(EPS))
                    act(o_(512, [(1, 256)]), o_(512, [(1, 256)]), AF.Exp, scale=-1.0)
                    yield
                    act(stat(52 + 4 * fs, 4), stat(52 + 4 * fs, 4), AF.Exp, scale=-0.5)
                    tt(o_(256, [(1, 256)]), o_(512, [(1, 256)]), V(pz, 0, [(1, 256)]), ALU.mult)
                    yield
                    tt(o_(0, [(64, 4), (1, 64)]), o_(0, [(64, 4), (1, 64)]), V(st_t, 52 + 4 * fs, [(1, 4), (0, 64)]), ALU.mult)
                    yield
                    tt(o_(0, [(64, 4), (1, 64)]), o_(0, [(64, 4), (1, 64)]), V(pbc_t, PO["gng"], [(0, 4), (1, 64)]), ALU.mult)
                    yield
                    tt(yD, o_(0, [(1, 256)]), o_(256, [(1, 256)]), ALU.mult)
                    yield

            run_chains([dfin_chain(0, out_tiles[0::2]), dfin_chain(1, out_tiles[1::2])])

            memset(V(VA_t, 64, [(65, NT * 4)]), 1.0)
            wrelease(2)
            chk("D")
            if debug_y and b == 0 and l == 0:
                for t in tiles:
                    cp(sA(), V(y_t, t * 1024, [(1, 1024)]), eng="dve")
                    dma("sp", dbg_d[t * 128:(t + 1) * 128, :], sA(), reads=[sA()], writes=[("dbg", t, t + 1, 0, 1)])

            for slot, row in ((0, 2), (1, b)):
                src = bass.AP(mods_d.tensor, l * 9 * D + row * 3 * D + 2 * D, [[0, 128], [1, D]])
                dma("sp", mods(slot), src, reads=[("modsd", l, l + 1, 0, 1)], writes=[mods(slot)])
            if last:
                dma("sp", mods(2), bass.AP(fng_d.tensor, 0, [[0, 128], [1, D]]), writes=[mods(2)])
            pbanks[:] = [0, 2, 3]
            wo = [next_wslot(), next_wslot()]

            def stageW_chain(par, tl):
                banks = (ps[0], ps[2]) if par == 0 else (ps[3], ps[4])
                for t in tl:
                    ys = par
                    pt = psT()
                    for kc in range(8):
                        tr(V(pt, kc * 128, [(1, 128)]), V(y_t, t * 1024 + kc * 128, [(1, 128)]), inc=(kc == 7))
                    cp(V(ytT_t, ys * 1024, [(1, 1024)]), V(pt, 0, [(1, 1024)]), eng=("dve" if par else "act"))
                    sl = par
                    if l == 0:
                        src = ctx_d[b, t * 128:(t + 1) * 128, :] if t < 2 else x_d[b, (t - 2) * 128:(t - 1) * 128, :]
                        dma("sp", xt(sl), src, writes=[xt(sl)])
                    else:
                        dma("sp", xt(sl), xres_d[b, t * 128:(t + 1) * 128, :], reads=[("xres", b * NT + t, b * NT + t + 1, 0, 1)],
                            writes=[xt(sl)])
                    yield
                    gsl = 0 if t < 2 else 1
                    for cb in range(2):
                        po = banks[cb]
                        for kc in range(8):
                            mm(V(po, 0, [(1, 512)]), V(ytT_t, ys * 1024 + kc * 128, [(1, 128)]), ring_ap(wo[cb], kc, 0, 512),
                               start=(kc == 0), stop=(kc == 7))
                        yield
                    for cb in range(2):
                        po = banks[cb]
                        tt(sAp(par, cb * 512, 512), V(po, 0, [(1, 512)]), mods(gsl, cb * 512, 512), ALU.mult)
                        yield
                        tt(V(fp_t, 4096 + sl * 1024 + cb * 512, [(1, 512)]), V(fp_t, 4096 + sl * 1024 + cb * 512, [(1, 512)]),
                           sAp(par, cb * 512, 512), ALU.add)
                        yield
                    if not last:
                        dma("sp", xres_d[b, t * 128:(t + 1) * 128, :], xt(sl), reads=[xt(sl)],
                            writes=[("xres", b * NT + t, b * NT + t + 1, 0, 1)])
                    else:
                        act(sAp(par), xt(sl), AF.Square, accum=stat(8 + 2 * par))
                        yield
                        act(stat(9 + 2 * par), stat(8 + 2 * par), AF.Ln, scale=float(1.0 / D), bias=float(EPS))
                        yield
                        act(stat(9 + 2 * par), stat(9 + 2 * par), AF.Exp, scale=-0.5)
                        yield
                        stt(xt(sl), xt(sl), stat(9 + 2 * par), mods(2), ALU.mult, ALU.mult)
                        tok = S.op("sp", lambda e, t=t, sl=sl, b=b: e.dma_start(out=out_d[b, (t - 2) * 128:(t - 1) * 128, :], in_=xt(sl)),
                                   reads=[xt(sl)], writes=[], dma=True)
                        out_toks.append(tok)
                    yield

            run_chains([stageW_chain(0, out_tiles[0::2]), stageW_chain(1, out_tiles[1::2])])
            wrelease(2)
            chk("W")

    try:
        main_loops()
    except _Stop:
        if debug_y and stop_at == "stage1":
            for kc in range(8):
                stg = V(fp_t, (kc % 2) * 2304, [(1, 2304)])
                cp(stg, V(hT_t, kc * 2304, [(1, 2304)]), eng="dve")
                dma("sp", bass.AP(dbg_d.tensor, kc * 128 * 2304, [[2304, 128], [1, 2304]]), stg, reads=[stg],
                    writes=[("dbg", kc, kc + 1, 0, 1)])
        elif debug_y and stop_at == "A1":
            for c in range(2):
                stg = V(fp_t, c * 2304, [(1, 2304)])
                cp(stg, V(kTA_t, c * 2304, [(1, 2304)]), eng="dve")
                dma("sp", bass.AP(dbg_d.tensor, c * 128 * 2304, [[2304, 128], [1, 2304]]), stg, reads=[stg],
                    writes=[("dbg", c, c + 1, 0, 1)])
        elif debug_y:
            for t in range(NT):
                cp(sA(), V(y_t, t * 1024, [(1, 1024)]), eng="dve")
                dma("sp", dbg_d[t * 128:(t + 1) * 128, :], sA(), reads=[sA()], writes=[("dbg", t, t + 1, 0, 1)])

    final = {}
    for tok in out_toks:
        final[tok[0]] = max(final.get(tok[0], 0), tok[1])
    for q in ("sp", "pool"):
        for i, lt_ in enumerate(S.dq[q]["last"]):
            if lt_ is not None:
                final[lt_[0]] = max(final.get(lt_[0], 0), lt_[1])
    S.ops["sp"].append((list(final.items()), None, None))

    engmap = {"pe": "tensor", "act": "scalar", "dve": "vector", "pool": "gpsimd", "sp": "sync"}

    def replay(eng, e):
        for waits, fn, tok in S.ops[eng]:
            for k, v in waits:
                e.wait_ge(sems[k], v)
            if fn is None:
                continue
            ins = fn(e)
            if tok is not None:
                ins.then_inc(sems[tok[0]], 16 if tok[0][0] == "d" else 1)

    with nc.Block() as block:
        @block.tensor
        def _(e):
            replay("pe", e)

        @block.scalar
        def _(e):
            replay("act", e)

        @block.vector
        def _(e):
            replay("dve", e)

        @block.gpsimd
        def _(e):
            replay("pool", e)

        @block.sync
        def _(e):
            replay("sp", e)

    st.close()
    return nc, S


def _const_tables():
    bf = ml_dtypes.bfloat16
    cf = np.zeros((128, 647), np.float32)
    j = np.arange(128)[:, None]
    i = np.arange(128)[None, :]
    cf[:, 0:128] = (j <= i)
    cf[:, 128:256] = (j >= i)
    hc = np.arange(128) // 32
    hv = np.arange(256) // 64
    cf[:, 256:512] = (hc[:, None] == hv[None, :])
    cf[:, 512:516] = (hc[:, None] == np.arange(4)[None, :])
    cf[:, 516] = 1.0
    cf[:, 517:645] = np.eye(128)
    cf[:, 645:647] = ((np.arange(128) // 64)[:, None] == np.arange(2)[None, :])
    cb = np.zeros((128, 3200), np.float32)
    cb[:, 0:128] = np.eye(128)
    tt_ = np.arange(TL)
    rows = (tt_ // 64).astype(np.float32)
    cols = (tt_ % 64).astype(np.float32)

    def tables(half):
        inv = (np.float32(10000.0) ** (-np.arange(0, half, 2, dtype=np.float32) / np.float32(half))).astype(np.float32)
        ar = rows[:, None] * inv[None, :]
        ac = cols[:, None] * inv[None, :]
        F = half // 2
        C = np.zeros((TL, 2, 2, F), np.float32)
        Sg = np.zeros((TL, 2, 2, F), np.float32)
        for rc, a in enumerate((ar, ac)):
            C[:, rc, 0] = np.cos(a)
            C[:, rc, 1] = np.cos(a)
            Sg[:, rc, 0] = -np.sin(a)
            Sg[:, rc, 1] = np.sin(a)
        W = 4 * F
        C = C.reshape(16, 128, W).transpose(1, 0, 2).reshape(128, 16 * W)
        Sg = Sg.reshape(16, 128, W).transpose(1, 0, 2).reshape(128, 16 * W)
        return C, Sg

    CA, SA = tables(16)
    CB, SB = tables(32)
    cb[:, 128:640] = CA
    cb[:, 640:1152] = SA
    cb[:, 1152:2176] = CB
    cb[:, 2176:3200] = SB
    return cf, cb.astype(bf)


_CACHE = {}


def _prep(x, c, ctx, c_ctx, ada_w, ada_b, norm_g, w_in, da_lq1, da_lk1, da_lq2, da_lk2,
          da_subln_g, gq_qnorm_g, gq_knorm_g, sg_ln_g, sg_ln_b, sg_w, sg_b,
          gla_w2_f, gla_b_f, gla_w2_b, gla_b_b, gla_norm_g, w_out, final_norm_g, build=True):
    f = lambda a: np.ascontiguousarray(np.asarray(a, dtype=np.float32))
    x, c, ctx, c_ctx = f(x), f(c), f(ctx), f(c_ctx)
    ncores = 8
    nc = None
    if build:
        if "nc" not in _CACHE:
            _CACHE["nc"] = build_program()[0]
        nc = _CACHE["nc"]
    cf, cb = _const_tables()
    L = DEPTH
    pvec = np.concatenate([f(sg_ln_g), f(sg_ln_b), f(gla_b_f), f(gla_b_b), f(da_subln_g), f(gq_qnorm_g),
                           f(gq_knorm_g), f(gla_norm_g), f(da_lq1), f(da_lq2), f(da_lk1), f(da_lk2)], axis=1)
    assert pvec.shape == (L, NPV)
    sgbT = np.ascontiguousarray(f(sg_b).transpose(0, 2, 1))
    sgwT = np.ascontiguousarray(f(sg_w).transpose(0, 3, 1, 2))
    w2bd = np.zeros((L, 32, 256), np.float32)
    w2bd[:, 0:16, 0:128] = f(gla_w2_f)
    w2bd[:, 16:32, 128:256] = f(gla_w2_b)
    shared = dict(ada_w=f(ada_w), ada_b=f(ada_b), norm_g=f(norm_g), w_in=f(w_in), w_out=f(w_out), pvec=pvec,
                  sgbT=sgbT, sgwT=sgwT, w2bd=w2bd, fng=f(final_norm_g), cf32=cf, cbf=cb)
    in_maps = []
    for i in range(ncores):
        rows = np.stack([c[2 * i], c[2 * i + 1], c_ctx], axis=0)
        cT = np.ascontiguousarray(rows.reshape(3, 8, 128).transpose(2, 1, 0))
        m = dict(shared)
        m["x"] = x[2 * i:2 * i + 2]
        m["ctx"] = ctx[2 * i:2 * i + 2]
        m["cT"] = cT
        in_maps.append(m)
    return nc, in_maps


def kernel(**inputs):
    nc, in_maps = _prep(**inputs)
    res = run_bass_kernel_spmd(nc, in_maps, core_ids=list(range(len(in_maps))))
    out = np.concatenate([r["out"] for r in res.results], axis=0)
    return out.astype(np.float32)
```

```python
import math
from contextlib import ExitStack

import numpy as np
import ml_dtypes

import concourse.bass as bass
import concourse.mybir as mybir
from concourse.bass_utils import run_bass_kernel_spmd

F32 = mybir.dt.float32
BF16 = mybir.dt.bfloat16
AF = mybir.ActivationFunctionType
ALU = mybir.AluOpType
AX = mybir.AxisListType

D = 1024
DEPTH = 4
NB = 2
TL = 2048
TC = 256
NT = 18
P_IN = 3360
EPS = 1e-6
NPV = 1152


def _esz(dt):
    return 2 if dt == BF16 else 4


class Sched:
    ENGS = ("pe", "act", "dve", "pool", "sp")

    def __init__(self, same_sync=True, n_dma=10):
        self.ops = {e: [] for e in self.ENGS}
        self.cnt = {e: 0 for e in self.ENGS}
        self.clock = {e: {} for e in self.ENGS}
        self.recs = {}
        self.snap = {}
        self.same_sync = same_sync
        self.n_dma = n_dma
        self.dq = {e: dict(next=0, cnt=[0] * n_dma, last=[None] * n_dma) for e in ("sp", "pool", "act")}
        self.nwaits = 0

    @staticmethod
    def box(a):
        if isinstance(a, tuple):
            return a
        t = a.tensor
        ps = 1
        for s in t.shape[1:]:
            ps *= s
        esz = _esz(a.dtype)
        off = a.offset
        p0 = off // ps
        f0 = off % ps
        apl = a.ap
        npart = apl[0][1]
        ext = 1
        for st, c in apl[1:]:
            ext += (c - 1) * abs(st)
        if t.name.startswith("ps"):
            return (t.name, 0, 128, 0, 1 << 20)
        return (t.name, p0, p0 + npart, f0 * esz, (f0 + ext) * esz)

    def op(self, eng, fn, reads=(), writes=(), dma=False, inc=True):
        deps = {}

        def add(tok):
            k, v = tok
            if deps.get(k, 0) < v:
                deps[k] = v

        rb = [self.box(a) for a in reads]
        wb = [self.box(a) for a in writes]
        own_k = ("e", eng)
        for b in rb:
            isps = b[0].startswith("ps")
            for r in self.recs.get(b[0], ()):
                if (r[5] or (isps and r[6][0] != own_k)) and r[1] < b[2] and b[1] < r[2] and r[3] < b[4] and b[3] < r[4]:
                    add(r[6])
        for b in wb:
            for r in self.recs.get(b[0], ()):
                if r[1] < b[2] and b[1] < r[2] and r[3] < b[4] and b[3] < r[4]:
                    add(r[6])
        if dma:
            q = self.dq[eng]
            i = q["next"]
            q["next"] = (i + 1) % self.n_dma
            if q["last"][i] is not None:
                add(q["last"][i])
            q["cnt"][i] += 1
            tok = (("d", eng, i), q["cnt"][i] * 16)
            q["last"][i] = tok
        else:
            if inc:
                self.cnt[eng] += 1
                tok = (("e", eng), self.cnt[eng])
            else:
                tok = (("e", eng), self.cnt[eng] + 1)
        clk = self.clock[eng]
        waits = []
        own = ("e", eng)
        for k, v in deps.items():
            if k == own and (eng == "pe" or not self.same_sync):
                continue
            if clk.get(k, 0) >= v:
                continue
            waits.append((k, v))
            sn = self.snap.get((k, v))
            if sn:
                for kk, vv in sn.items():
                    if clk.get(kk, 0) < vv:
                        clk[kk] = vv
            clk[k] = v
        self.nwaits += len(waits)
        if inc or dma:
            self.snap[tok] = dict(clk)
        self.ops[eng].append((waits, fn, tok if (inc or dma) else None))
        for b in wb:
            lst = self.recs.setdefault(b[0], [])
            lst[:] = [r for r in lst if not (b[1] <= r[1] and r[2] <= b[2] and b[3] <= r[3] and r[4] <= b[4])]
            lst.append([b[0], b[1], b[2], b[3], b[4], True, tok])
        for b in rb:
            lst = self.recs.setdefault(b[0], [])
            found = False
            for r in lst:
                if (not r[5]) and r[1] == b[1] and r[2] == b[2] and r[3] == b[3] and r[4] == b[4] and r[6][0] == tok[0]:
                    if r[6][1] < tok[1]:
                        r[6] = tok
                    found = True
                    break
            if not found:
                lst.append([b[0], b[1], b[2], b[3], b[4], False, tok])
        return tok

    def replay(self, eng, e, sems):
        for waits, fn, tok in self.ops[eng]:
            for k, v in waits:
                e.wait_ge(sems[k], v)
            ins = fn(e)
            if tok is not None:
                ins.then_inc(sems[tok[0]], 16 if tok[0][0] == "d" else 1)

    def final_wait(self, eng, toks):
        ws = [(k, v) for (k, v) in toks]
        self.ops[eng].append((ws, None, None))


class _Stop(Exception):
    pass


def build_program(depth=DEPTH, nb=NB, debug_y=False, same_sync=True, stop_at=None):
    nc = bass.Bass("TRN2", target_bir_lowering=False)
    S = Sched(same_sync=same_sync)
    st = ExitStack()

    def din(name, shape, dt=F32):
        return nc.dram_tensor(name, list(shape), dt, kind="ExternalInput").ap()

    x_d = din("x", [nb, TL, D])
    ctx_d = din("ctx", [nb, TC, D])
    cT_d = din("cT", [128, 8, 3])
    adaw_d = din("ada_w", [DEPTH, D, 3 * D])
    adab_d = din("ada_b", [DEPTH, 3 * D])
    ng_d = din("norm_g", [DEPTH, D])
    win_d = din("w_in", [DEPTH, D, P_IN])
    wout_d = din("w_out", [DEPTH, D, D])
    pvec_d = din("pvec", [DEPTH, NPV])
    sgbT_d = din("sgbT", [DEPTH, 128, 4])
    sgwT_d = din("sgwT", [DEPTH, 128, 4, 128])
    w2bd_d = din("w2bd", [DEPTH, 32, 256])
    fng_d = din("fng", [D])
    cf32_d = din("cf32", [128, 647])
    cbf_d = din("cbf", [128, 3200], BF16)
    out_d = nc.dram_tensor("out", [nb, TL, D], F32, kind="ExternalOutput").ap()
    if debug_y:
        dbg_d = nc.dram_tensor("dbg", [NT * 128, D], F32, kind="ExternalOutput").ap()
    xres_d = nc.dram_tensor("xres", [nb, NT * 128, D], F32).ap()
    mods_d = nc.dram_tensor("modsd", [DEPTH, 3, 3, D], F32).ap()

    def sb(name, free, dt):
        return st.enter_context(nc.sbuf_tensor("s_" + name, [128, free], dt))

    ident_t = sb("ident", 128, BF16)
    rope_t = sb("rope", 3072, BF16)
    cf_t = sb("cf", 647, F32)
    pbc_t = sb("pbc", NPV, F32)
    bsT_t = sb("bsT", 4, F32)
    w2bd_t = sb("w2bd", 256, BF16)
    wsT_t = sb("wsT", 512, BF16)
    sm_t = sb("sm", 64, F32)
    sgeff_t = sb("sgeff", 64, F32)
    fp_t = sb("fpool", 8192, F32)
    fpb_t = fp_t.bitcast(BF16)
    hT_t = sb("hT", 8 * 2304, BF16)
    y_t = sb("y", NT * 1024, BF16)
    NRING = 3
    ring_t = sb("ring", NRING * 4096, BF16)
    kTA_t = sb("kTA", 2 * 2304, BF16)
    VA_t = sb("VA", NT * 4 * 65 + 64, BF16)
    kTB_t = sb("kTB", 2304, BF16)
    VB_t = sb("VB", NT * 2 * 65 + 64, BF16)
    hbf_t = sb("hbf", 2048, BF16)
    gate_t = sb("gate", 2 * 1024, BF16)
    pT_t = sb("pT", 4 * 512, BF16)
    ytT_t = sb("ytT", 2 * 1024, BF16)
    qr_t = sb("qr", 2 * 256, BF16)
    oT_t = sb("oT", 2 * 512, F32)
    cmb_t = sb("cmb", 512, F32)
    st_t = sb("stats", 128, F32)
    rTb_t = sb("rTb", 128, BF16)
    qeb_t = sb("qeb", 128, BF16)
    keb_t = sb("keb", 128, BF16)
    vb_t = sb("vb", 256, BF16)
    qeT_t = sb("qeT", 128, BF16)
    kTm_t = sb("kTm", 512, BF16)
    am_t = sb("am", 512, BF16)
    Sst_t = sb("Sst", 256, F32)
    Sb_t = sb("Sb", 256, BF16)
    Wt_t = sb("Wt", 256, F32)
    gl_t = sb("gl", 512, F32)
    vnb_t = sb("vnb", 256, BF16)
    uu_t = sb("uu", 512, F32)
    scT_t = sb("scT", 48, F32)

    ps = []
    for i in range(8):
        ps.append(st.enter_context(nc.psum_tensor(f"ps{i}", [128, 512], F32)))
    ps1b = ps[1].bitcast(BF16)

    sems = {}
    for e in Sched.ENGS:
        sems[("e", e)] = st.enter_context(nc.semaphore(f"s_{e}"))
    for e in ("sp", "pool", "act"):
        for i in range(S.n_dma):
            sems[("d", e, i)] = st.enter_context(nc.semaphore(f"d_{e}{i}"))

    def V(t, off=0, dims=None, p0=0, npart=128):
        ps_ = 1
        for s in t.shape[1:]:
            ps_ *= s
        if dims is None:
            dims = [(1, ps_ - off)]
        return bass.AP(t, p0 * ps_ + off, [[ps_, npart]] + [[s, c] for (s, c) in dims])

    def dma(eng, out, in_, reads=(), writes=()):
        S.op(eng, lambda e: e.dma_start(out=out, in_=in_), reads=list(reads), writes=list(writes), dma=True)

    def mm(out, lhsT, rhs, start, stop, **kw):
        S.op("pe", lambda e: e.matmul(out, lhsT, rhs, start=start, stop=stop, **kw),
             reads=[lhsT, rhs], writes=[out], inc=bool(stop))

    def tr(out, in_, inc=True):
        S.op("pe", lambda e: e.transpose(out, in_, V(ident_t, 0, [(1, 128)])), reads=[in_], writes=[out], inc=inc)

    def act(out, in_, func, scale=1.0, bias=None, accum=None):
        rd = [in_]
        wr = [out]
        kw = {}
        if bias is not None:
            kw["bias"] = bias
            if not isinstance(bias, float):
                rd.append(bias)
        if not isinstance(scale, float):
            rd.append(scale)
        if accum is not None:
            kw["accum_out"] = accum
            wr.append(accum)
        S.op("act", lambda e: e.activation(out=out, in_=in_, func=func, scale=scale, **kw), reads=rd, writes=wr)

    def tt(out, in0, in1, op, eng="dve"):
        S.op(eng, lambda e: e.tensor_tensor(out=out, in0=in0, in1=in1, op=op), reads=[in0, in1], writes=[out])

    def ts(out, in0, s1, s2, op0, op1=None, eng="dve"):
        rd = [in0]
        if not isinstance(s1, float):
            rd.append(s1)
        if s2 is not None and not isinstance(s2, float):
            rd.append(s2)
        if op1 is None:
            S.op(eng, lambda e: e.tensor_scalar(out=out, in0=in0, scalar1=s1, scalar2=None, op0=op0), reads=rd, writes=[out])
        else:
            S.op(eng, lambda e: e.tensor_scalar(out=out, in0=in0, scalar1=s1, scalar2=s2, op0=op0, op1=op1), reads=rd, writes=[out])

    def stt(out, in0, scalar, in1, op0, op1):
        rd = [in0, in1]
        if not isinstance(scalar, float):
            rd.append(scalar)
        S.op("dve", lambda e: e.scalar_tensor_tensor(out=out, in0=in0, scalar=scalar, in1=in1, op0=op0, op1=op1),
             reads=rd, writes=[out])

    def cp(out, in_, eng="dve"):
        if eng == "act":
            S.op("act", lambda e: e.copy(out=out, in_=in_), reads=[in_], writes=[out])
        else:
            S.op(eng, lambda e: e.tensor_copy(out=out, in_=in_), reads=[in_], writes=[out])

    def memset(ap, val, eng="dve"):
        S.op(eng, lambda e: e.memset(ap, val), writes=[ap])

    def ttr(out, in0, in1, accum):
        S.op("dve", lambda e: e.tensor_tensor_reduce(out=out, in0=in0, in1=in1, scale=1.0, scalar=0.0,
                                                    op0=ALU.mult, op1=ALU.add, accum_out=accum),
             reads=[in0, in1], writes=[out, accum])

    def red(out, in_):
        S.op("dve", lambda e: e.tensor_reduce(out=out, in_=in_, axis=AX.X, op=ALU.add), reads=[in_], writes=[out])

    def recip(out, in_):
        S.op("dve", lambda e: e.reciprocal(out=out, in_=in_), reads=[in_], writes=[out])

    def rstd_from(out, ssum, k):
        act(out, ssum, AF.Ln, scale=float(k), bias=float(EPS))
        act(out, out, AF.Exp, scale=-0.5)

    rot = {}

    def nxt(key, n):
        i = rot.get(key, 0)
        rot[key] = (i + 1) % n
        return i

    pbanks = [0]

    def psP():
        return ps[pbanks[nxt("P", len(pbanks)) % len(pbanks)]]

    def psT():
        return ps1b

    def psS():
        return ps[(0, 2, 3)[nxt("S", 3)]]

    def psO():
        return ps[(5, 6, 7)[nxt("O", 3)]]

    def mods(slot, off=0, n=1024):
        return V(fp_t, slot * 1024 + off, [(1, n)])

    def xt(slot):
        return V(fp_t, 4096 + slot * 1024, [(1, 1024)])

    sA = lambda off=0, n=1024: V(fp_t, 6144 + off, [(1, n)])
    sB = lambda off=0, n=512: V(fp_t, 7168 + off, [(1, n)])
    sC = lambda off=0, n=512: V(fp_t, 7680 + off, [(1, n)])

    def stat(i, n=1):
        return V(st_t, i, [(1, n)])

    PO = dict(lng=0, lnb=256, gb=512, subg=768, qng=832, kng=896, gng=960, lq=1024, lk=1088)

    def pb(name, n, off=0):
        return V(pbc_t, PO[name] + off, [(1, n)])

    LtriF = V(cf_t, 0, [(1, 128)])
    LtriB = V(cf_t, 128, [(1, 128)])
    blockmask = V(cf_t, 256, [(1, 256)])
    ones_c = V(cf_t, 516, [(1, 1)])

    wreq = []
    wstate = dict(issued=0, used=0)

    def wblock_aps(kind, l, c0, ncol, dst_off):
        src = win_d if kind == "in" else wout_d
        ncols_total = P_IN if kind == "in" else D
        in_ap = bass.AP(src.tensor, l * D * ncols_total + c0, [[ncols_total, 128], [128 * ncols_total, 8], [1, ncol]])
        return dst_off, in_ap, ncol

    def plan_weights():
        plan = []
        for b in range(nb):
            for l in range(depth):
                plan.append([("in", l, 256, 512, 0)])
                plan.append([("in", l, 0, 256, 0), ("in", l, 768, 256, 256)])
                plan.append([("in", l, 1280, 256, 0)])
                plan.append([("in", l, 1024, 256, 0), ("in", l, 1536, 256, 256)])
                plan.append([("in", l, 1792, 512, 0)])
                plan.append([("in", l, 2304, 256, 0)])
                plan.append([("in", l, 2560, 512, 0)])
                plan.append([("in", l, 3072, 288, 0)])
                plan.append([("out", l, 0, 512, 0)])
                plan.append([("out", l, 512, 512, 0)])
        return plan

    wplan = plan_weights()

    def ring_ap(slot, kc, c0, n):
        return V(ring_t, slot * 4096 + kc * 512 + c0, [(1, n)])

    def issue_block(n):
        slot = n % NRING
        for (kind, l, c0, ncol, dst) in wplan[n]:
            _, in_ap, _ = wblock_aps(kind, l, c0, ncol, dst)
            out_ap = V(ring_t, slot * 4096 + dst, [(512, 8), (1, ncol)])
            dma("pool", out_ap, in_ap, writes=[out_ap])

    def issue_weights(upto):
        while wstate["issued"] < min(upto, len(wplan)):
            issue_block(wstate["issued"])
            wstate["issued"] += 1

    wstate["released"] = 0

    def next_wslot():
        n = wstate["used"]
        wstate["used"] += 1
        assert wstate["issued"] > n, "weight block not prefetched"
        return n % NRING

    def wrelease(k=1):
        for _ in range(k):
            wstate["released"] += 1
            issue_weights(wstate["released"] + NRING)

    dma("sp", V(ident_t), cbf_d[:, 0:128], writes=[V(ident_t)])
    dma("sp", V(rope_t), cbf_d[:, 128:3200], writes=[V(rope_t)])
    dma("sp", V(cf_t), cf32_d[:, :], writes=[V(cf_t)])
    dma("sp", V(scT_t, 0, [(1, 24)]), cT_d.rearrange("p a b -> p (a b)"), writes=[V(scT_t, 0, [(1, 24)])])
    memset(V(VA_t), 1.0)
    memset(V(VB_t), 1.0)
    act(V(scT_t, 24, [(1, 24)]), V(scT_t, 0, [(1, 24)]), AF.Exp, scale=-1.0)
    act(V(scT_t, 24, [(1, 24)]), V(scT_t, 24, [(1, 24)]), AF.Ln, bias=1.0)
    act(V(scT_t, 24, [(1, 24)]), V(scT_t, 24, [(1, 24)]), AF.Exp, scale=-1.0)
    tt(V(scT_t, 0, [(1, 24)]), V(scT_t, 24, [(1, 24)]), V(scT_t, 0, [(1, 24)]), ALU.mult)

    pbanks[:] = [0, 2, 3]

    y32_t = y_t.bitcast(F32)

    def mods_gen(l, T_, bank_fn, dq):
        def R3(off, n):
            return V(T_, 4096 + off, [(1, n)], 0, 3)
        adab_bc = bass.AP(adab_d.tensor, l * 3 * D, [[0, 3], [1, 3 * D]])
        dma("sp", R3(0, 3072), adab_bc, writes=[R3(0, 3072)])
        ng_bc = bass.AP(ng_d.tensor, l * D, [[0, 3], [1, D]])
        dma("sp", R3(3072, 1024), ng_bc, writes=[R3(3072, 1024)])
        yield
        for cb in range(12):
            stg = cb % 2
            stg_ap = V(T_, stg * 2048, [(256, 8), (1, 256)])
            in_ap = bass.AP(adaw_d.tensor, l * D * 3 * D + cb * 256, [[3 * D, 128], [128 * 3 * D, 8], [1, 256]])
            dma(dq, stg_ap, in_ap, writes=[stg_ap])
            yield
            pt = bank_fn()
            for kc in range(8):
                mm(V(pt, 0, [(1, 256)], 0, 3), V(scT_t, kc * 3, [(1, 3)]), V(T_, stg * 2048 + kc * 256, [(1, 256)]),
                   start=(kc == 0), stop=(kc == 7))
            tt(R3(cb * 256, 256), V(pt, 0, [(1, 256)], 0, 3), R3(cb * 256, 256), ALU.add)
            yield
        stt(R3(1024, 1024), R3(1024, 1024), 1.0, R3(3072, 1024), ALU.add, ALU.mult)
        yield
        for j, src_off in ((0, 1024), (1, 0), (2, 2048)):
            o_ap = bass.AP(mods_d.tensor, l * 9 * D + j * D, [[3 * D, 3], [1, D]])
            dma("sp", o_ap, R3(src_off, 1024), reads=[R3(src_off, 1024)], writes=[("modsd", l, l + 1, 0, 1)])
        yield

    for _ in mods_gen(0, fp_t, psP, "sp"):
        pass

    issue_weights(NRING)
    nonlocal_dummy = None

    def proj(t, slot, c0, ncol, pt=None, kparts=128):
        if pt is None:
            pt = psP()
        for kc in range(8):
            mm(V(pt, 0, [(1, ncol)]), V(hT_t, kc * 2304 + t * 128, [(1, 128)]), ring_ap(slot, kc, c0, ncol),
               start=(kc == 0), stop=(kc == 7))
        return pt

    def rope(dst, src, lt, which, G, t1o=6144, t2o=7168, scr=None):
        scr = fp_t if scr is None else scr
        if which == "A":
            W, Fq, cb, sbo = 32, 8, 0, 512
        else:
            W, Fq, cb, sbo = 64, 16, 1024, 2048
        cos_bc = V(rope_t, cb + lt * W, [(0, G), (1, W)])
        t1 = V(scr, t1o, [(1, G * W)])
        t1v = V(scr, t1o, [(W, G), (1, W)])
        t2 = V(scr, t2o, [(1, G * W)])
        in0v = src(0, [(W, G), (1, W)])
        S.op("dve", lambda e, in0v=in0v, t1v=t1v, cos_bc=cos_bc: e.tensor_tensor(out=t1v, in0=in0v, in1=cos_bc, op=ALU.mult),
             reads=[src(0, [(1, G * W)]), cos_bc], writes=[t1])
        for half in range(2):
            o = V(scr, t2o + half * Fq, [(W, G), (2 * Fq, 2), (1, Fq)])
            i0 = src((1 - half) * Fq, [(W, G), (2 * Fq, 2), (1, Fq)])
            i1 = V(rope_t, sbo + lt * W + half * Fq, [(0, G), (2 * Fq, 2), (1, Fq)])
            S.op("dve", lambda e, o=o, i0=i0, i1=i1: e.tensor_tensor(out=o, in0=i0, in1=i1, op=ALU.mult),
                 reads=[src(0, [(1, G * W)]), V(rope_t, sbo + lt * W, [(1, W)])], writes=[t2])
        dst(t1, t2)

    def silu_gate(out_bf, z_ps, n, tmp):
        act(tmp, z_ps, AF.Exp, scale=-1.0)
        act(tmp, tmp, AF.Ln, bias=1.0)
        act(tmp, tmp, AF.Exp, scale=-1.0)
        tt(out_bf, tmp, z_ps, ALU.mult)

    out_toks = []

    S.marks = []

    def chk(name):
        if len(name) <= 8 and not any(ch.isdigit() for ch in name[2:]):
            S.marks.append((name, len(S.ops["pe"])))
        if stop_at == name:
            raise _Stop()

    def main_loops():
      for b in range(nb):
        for l in range(depth):
            chk("prologue")
            last = (l == DEPTH - 1)
            lam_init = 0.8 - 0.6 * math.exp(-0.3 * l)
            tiles = list(range(NT))
            out_tiles = list(range(2, NT)) if last else tiles

            pv_bc = bass.AP(pvec_d.tensor, l * NPV, [[0, 128], [1, NPV]])
            dma("sp", V(pbc_t), pv_bc, writes=[V(pbc_t)])
            dma("sp", V(bsT_t), sgbT_d[l], writes=[V(bsT_t)])
            dma("pool", V(wsT_t), sgwT_d[l].rearrange("p a b -> p (a b)"), writes=[V(wsT_t)])
            dma("pool", V(w2bd_t, 0, [(1, 256)], 0, 32), w2bd_d[l], writes=[V(w2bd_t, 0, [(1, 256)], 0, 32)])
            tt(V(sm_t, 0, [(1, 64)]), pb("lq", 64), pb("lk", 64), ALU.mult)
            red(stat(4, 2), V(sm_t, 0, [(32, 2), (1, 32)]))
            act(stat(0, 2), stat(4, 2), AF.Exp)
            tt(stat(2), stat(0), stat(1), ALU.subtract)
            ts(stat(3), stat(2), float(lam_init), -1.0, ALU.add, ALU.mult)
            ts(V(sgeff_t), pb("subg", 64), float(1.0 - lam_init), None, ALU.mult)
            neglam = stat(3)

            for slot, (row, j) in enumerate(((2, 0), (2, 1), (b, 0), (b, 1))):
                src = bass.AP(mods_d.tensor, l * 9 * D + row * 3 * D + j * D, [[0, 128], [1, D]])
                dma("sp", mods(slot), src, reads=[("modsd", l, l + 1, 0, 1)], writes=[mods(slot)])

            chk("params")
            def stagger(g_, n):
                for _ in range(n):
                    yield
                yield from g_

            def run_chains(gens):
                gens = list(gens)
                while gens:
                    for g_ in list(gens):
                        try:
                            next(g_)
                        except StopIteration:
                            gens.remove(g_)

            def sAp(par, off=0, n=1024):
                return V(fp_t, 6144 + par * 1024 + off, [(1, n)])

            def stage1_chain(par, tl):
                for t in tl:
                    sl = par
                    if l == 0:
                        src = ctx_d[b, t * 128:(t + 1) * 128, :] if t < 2 else x_d[b, (t - 2) * 128:(t - 1) * 128, :]
                        dma("sp", xt(sl), src, writes=[xt(sl)])
                    else:
                        dma("sp", xt(sl), xres_d[b, t * 128:(t + 1) * 128, :], reads=[("xres", b * NT + t, b * NT + t + 1, 0, 1)],
                            writes=[xt(sl)])
                    yield
                    act(sAp(par), xt(sl), AF.Square, accum=stat(8 + 2 * par))
                    yield
                    act(stat(9 + 2 * par), stat(8 + 2 * par), AF.Ln, scale=float(1.0 / D), bias=float(EPS))
                    yield
                    act(stat(9 + 2 * par), stat(9 + 2 * par), AF.Exp, scale=-0.5)
                    yield
                    ms = 0 if t < 2 else 2
                    stt(sAp(par), xt(sl), stat(9 + 2 * par), mods(ms), ALU.mult, ALU.mult)
                    yield
                    hb = V(hbf_t, par * 1024, [(1, 1024)])
                    tt(hb, sAp(par), mods(ms + 1), ALU.add)
                    yield
                    pt = psT()
                    for kc in range(8):
                        tr(V(pt, kc * 128, [(1, 128)]), V(hbf_t, par * 1024 + kc * 128, [(1, 128)]), inc=(kc == 7))
                    cp(V(hT_t, t * 128, [(2304, 8), (1, 128)]), V(pt, 0, [(128, 8), (1, 128)]),
                       eng=("dve" if par else "act"))
                    s1done.add(t)
                    yield

            s1done = set()

            chk("stage1")
            def attn_branch(which, phase, ws_in):
                isA = which == "A"
                kT_t = kTA_t if isA else kTB_t
                V_t = VA_t if isA else VB_t
                nkc = 2 if isA else 1
                nvh = 4 if isA else 2
                ycol = 0 if isA else 256
                scale = float(32 ** -0.5) if isA else float(64 ** -0.5)
                def pass1_chain(par):
                    kw = 256 if isA else 128
                    vw = 256 if isA else 128
                    if isA:
                        pbk = ps[0] if par == 0 else ps[3]
                        krt, kro = (qr_t, 0) if par == 0 else (vnb_t, 0)
                        scr, t1o, t2o, nrm = (oT_t, 0, 256, 0) if par == 0 else (cmb_t, 0, 256, 0)
                        sti = 16
                    else:
                        pbk = ps[2] if par == 0 else ps[5]
                        krt, kro = (qr_t, 256) if par == 0 else (qr_t, 384)
                        scr, t1o, t2o, nrm = (oT_t, 512, 640, 768) if par == 0 else (uu_t, 0, 128, 256)
                        sti = 16 if par == 0 else 18
                    for t in tiles[par::2]:
                        while t not in s1done:
                            yield
                        for kc in range(8):
                            mm(V(pbk, 0, [(1, kw + vw)]), V(hT_t, kc * 2304 + t * 128, [(1, 128)]), ring_ap(ws_in, kc, 0, kw + vw),
                               start=(kc == 0), stop=(kc == 7))
                        yield
                        cp(V(V_t, t * nvh * 65, [(65, nvh), (1, 64)]), V(pbk, kw, [(64, nvh), (1, 64)]), eng="act")
                        kr = V(krt, kro, [(1, kw)])
                        if isA:
                            ksrc = lambda off, dims, pbk=pbk: V(pbk, off, dims)
                        else:
                            act(V(scr, nrm, [(1, 128)]), V(pbk, 0, [(1, 128)]), AF.Square)
                            yield
                            red(stat(sti, 2), V(scr, nrm, [(64, 2), (1, 64)]))
                            yield
                            act(stat(sti, 2), stat(sti, 2), AF.Ln, scale=float(1.0 / 64), bias=float(EPS))
                            yield
                            act(stat(sti, 2), stat(sti, 2), AF.Exp, scale=-0.5)
                            yield
                            tt(V(scr, nrm, [(64, 2), (1, 64)]), V(pbk, 0, [(64, 2), (1, 64)]),
                               V(st_t, sti, [(1, 2), (0, 64)]), ALU.mult)
                            yield
                            tt(V(scr, nrm, [(64, 2), (1, 64)]), V(scr, nrm, [(64, 2), (1, 64)]),
                               V(pbc_t, PO["kng"], [(0, 2), (1, 64)]), ALU.mult)
                            ksrc = lambda off, dims, scr=scr, nrm=nrm: V(scr, nrm + off, dims)
                        yield
                        if t >= 2:
                            G = kw // (32 if isA else 64)
                            rope(lambda a, bb_, kr=kr: tt(kr, a, bb_, ALU.add), ksrc, t - 2, which, G, t1o, t2o, scr)
                        else:
                            cp(kr, ksrc(0, [(1, kw)]), eng="dve")
                        yield
                        ptt = psT()
                        for c in range(nkc):
                            tr(V(ptt, c * 128, [(1, 128)]), V(krt, kro + c * 128, [(1, 128)]), inc=(c == nkc - 1))
                        cp(V(kT_t, t * 128, [(2304, nkc), (1, 128)]), V(ptt, 0, [(128, nkc), (1, 128)]), eng="act")
                        yield

                if phase in ("p1e", "p1o"):
                    return pass1_chain(0 if phase == "p1e" else 1)
                ws = ws_in
                pbanks[:] = [1]
                groups = ([] if last else [[0, 1]]) + [[2 + 4 * g + j for j in range(4)] for g in range(4)]
                gslots = [nxt("qTm", 2) for _ in groups]

                def qproj_stages(gs, j, t):
                    cx = {}

                    def st_proj(lo, hi):
                        def f():
                            if lo == 0:
                                cx["pt"] = psP()
                                cx["qs"] = nxt("qr", 2)
                            pt = cx["pt"]
                            for kc in range(lo, hi):
                                mm(V(pt, 0, [(1, 512)]), V(hT_t, kc * 2304 + t * 128, [(1, 128)]), ring_ap(ws, kc, 0, 512),
                                   start=(kc == 0), stop=(kc == 7))
                        return f

                    def st_e1():
                        pt = cx["pt"]
                        gtmp = V(uu_t, 0, [(1, 256)])
                        act(gtmp, V(pt, 256, [(1, 256)]), AF.Exp, scale=-1.0)
                        act(gtmp, gtmp, AF.Ln, bias=1.0)
                        act(gtmp, gtmp, AF.Exp, scale=-1.0)
                        if not isA:
                            act(sC(0, 256), V(pt, 0, [(1, 256)]), AF.Square)

                    def st_e2():
                        if not isA:
                            red(stat(16, 4), V(fp_t, 7680, [(64, 4), (1, 64)]))

                    def st_e3():
                        if not isA:
                            rstd_from(stat(16, 4), stat(16, 4), 1.0 / 64)

                    def st_e4():
                        pt = cx["pt"]
                        qs = cx["qs"]
                        qrv = V(qr_t, qs * 256, [(1, 256)])
                        if isA:
                            qsrc = lambda off, dims: V(pt, off, dims)
                            if t >= 2:
                                rope(lambda a, bb_: tt(qrv, a, bb_, ALU.add), qsrc, t - 2, "A", 8)
                            else:
                                cp(qrv, V(pt, 0, [(1, 256)]), eng="dve")
                        else:
                            tt(V(fp_t, 7680, [(64, 4), (1, 64)]), V(pt, 0, [(64, 4), (1, 64)]),
                               V(st_t, 16, [(1, 4), (0, 64)]), ALU.mult)
                            tt(V(fp_t, 7680, [(64, 4), (1, 64)]), V(fp_t, 7680, [(64, 4), (1, 64)]),
                               V(pbc_t, PO["qng"], [(0, 4), (1, 64)]), ALU.mult)
                            qsrc = lambda off, dims: V(fp_t, 7680 + off, dims)
                            qperm = V(qr_t, qs * 256, [(64, 2), (128, 2), (1, 64)])
                            if t >= 2:
                                rope(lambda a, bb_, qperm=qperm: tt(qperm, V(fp_t, 6144, [(128, 2), (64, 2), (1, 64)]),
                                                                   V(fp_t, 7168, [(128, 2), (64, 2), (1, 64)]), ALU.add),
                                     qsrc, t - 2, "B", 4)
                            else:
                                cp(qperm, V(fp_t, 7680, [(128, 2), (64, 2), (1, 64)]), eng="dve")
                        tt(V(gate_t, gs * 1024 + j * 256, [(1, 256)]), V(uu_t, 0, [(1, 256)]), V(pt, 256, [(1, 256)]), ALU.mult)

                    def st_tr():
                        qs = cx["qs"]
                        ptt = psT()
                        cx["ptt"] = ptt
                        for c in range(2):
                            tr(V(ptt, c * 128, [(1, 128)]), V(qr_t, qs * 256 + c * 128, [(1, 128)]), inc=(c == 1))

                    def st_mask():
                        ptt = cx["ptt"]
                        for c in range(2):
                            if isA:
                                tt(V(fpb_t, gs * 4096 + (4 * c) * 512 + j * 128, [(512, 4), (1, 128)]),
                                   V(ptt, c * 128, [(0, 4), (1, 128)]), V(cf_t, 512, [(1, 4), (0, 128)]), ALU.mult)
                            else:
                                tt(V(fpb_t, gs * 4096 + c * 512 + j * 128, [(1024, 2), (1, 128)]),
                                   V(ptt, c * 128, [(0, 2), (1, 128)]), V(cf_t, 645, [(1, 2), (0, 128)]), ALU.mult)

                    return [st_proj(0, 4), st_proj(4, 8), st_e1, st_e2, st_e3, st_e4, st_tr, st_mask]

                def qproj_tile(gs, j, t):
                    for f in qproj_stages(gs, j, t):
                        f()

                for gi, grp in enumerate(groups):
                    isctx = grp[0] < 2
                    nq = len(grp) * 128
                    kts = [0, 1] if isctx else list(range(NT))
                    gs = gslots[gi]
                    if gi == 0:
                        for j, t in enumerate(grp):
                            qproj_tile(gs, j, t)
                    nxt_calls = []
                    if gi + 1 < len(groups):
                        nxt_calls = [(gslots[gi + 1], j, t) for j, t in enumerate(groups[gi + 1])]
                    nj = len(grp)
                    if isA:
                        items = [(h, m, ki, kt) for h in range(4) for m in range(2) for ki, kt in enumerate(kts)]
                    else:
                        items = [(hq, 0, ki, kt) for hq in range(4) for ki, kt in enumerate(kts)]
                    nk = len(kts)
                    obank = {}

                    def post(h, m):
                        ob = obank[(h, m)]
                        osl = nxt("oT", 2)
                        cp(V(oT_t, osl * 512, [(1, nq)], 0, 65), V(ob, 0, [(1, nq)], 0, 65), eng="dve")
                        return post2(h, m, osl)

                    def post2(h, m, osl):
                        def s_tr():
                            for j in range(nj):
                                if isA:
                                    cbk = ps[6 + j // 2]
                                    col = (m * 2 + j % 2) * 65
                                else:
                                    cbk = ps[6 + h % 2]
                                    col = j * 65
                                S.op("pe", lambda e, cbk=cbk, col=col, osl=osl, j=j: e.transpose(
                                    V(cbk, col, [(1, 65)]), V(oT_t, osl * 512 + j * 128, [(1, 128)], 0, 65),
                                    V(cf_t, 517, [(1, 65)], 0, 65)),
                                    reads=[V(oT_t, osl * 512 + j * 128, [(1, 128)], 0, 65)], writes=[V(cbk, col, [(1, 65)])])
                        stages = [(0, s_tr)]
                        if isA and m == 0:
                            return stages
                        t0_ = grp[0]
                        if not isA:
                            def s_b():
                                cbk = ps[6 + h % 2]
                                recip(stat(64, nj), V(cbk, 64, [(65, nj)]))
                                tt(V(cmb_t, 0, [(64, nj), (1, 64)]), V(cbk, 0, [(65, nj), (1, 64)]),
                                   V(st_t, 64, [(1, nj), (0, 64)]), ALU.mult)
                                tt(V(y_t, t0_ * 1024 + ycol + h * 64, [(1024, nj), (1, 64)]), V(cmb_t, 0, [(64, nj), (1, 64)]),
                                   V(gate_t, gs * 1024 + h * 64, [(256, nj), (1, 64)]), ALU.mult)
                            stages.append((2, s_b))
                            return stages
                        nbk = (nj + 1) // 2

                        def s_a1():
                            for bk in range(nbk):
                                cbk = ps[6 + bk]
                                so = 64 + bk * 8
                                oo = bk * 128
                                recip(stat(so, 4), V(cbk, 64, [(65, 4)]))
                                tt(stat(so + 4, 2), stat(so + 2, 2), V(st_t, 3, [(0, 2)]), ALU.mult)
                                tt(V(cmb_t, 256 + oo, [(64, 2), (1, 64)]), V(cbk, 130, [(65, 2), (1, 64)]),
                                   V(st_t, so + 4, [(1, 2), (0, 64)]), ALU.mult)
                                tt(V(cmb_t, oo, [(64, 2), (1, 64)]), V(cbk, 0, [(65, 2), (1, 64)]),
                                   V(st_t, so, [(1, 2), (0, 64)]), ALU.mult)
                                tt(V(cmb_t, oo, [(1, 128)]), V(cmb_t, oo, [(1, 128)]), V(cmb_t, 256 + oo, [(1, 128)]), ALU.add)
                                tt(V(cmb_t, 256 + oo, [(1, 128)]), V(cmb_t, oo, [(1, 128)]), V(cmb_t, oo, [(1, 128)]), ALU.mult)
                                red(stat(so + 6, 2), V(cmb_t, 256 + oo, [(64, 2), (1, 64)]))

                        def s_a2():
                            for bk in range(nbk):
                                so = 64 + bk * 8
                                act(stat(so + 6, 2), stat(so + 6, 2), AF.Ln, scale=float(1.0 / 64), bias=float(EPS))
                                act(stat(so + 6, 2), stat(so + 6, 2), AF.Exp, scale=-0.5)

                        def s_a3():
                            for bk in range(nbk):
                                so = 64 + bk * 8
                                oo = bk * 128
                                tt(V(cmb_t, oo, [(64, 2), (1, 64)]), V(cmb_t, oo, [(64, 2), (1, 64)]),
                                   V(st_t, so + 6, [(1, 2), (0, 64)]), ALU.mult)
                                tt(V(cmb_t, oo, [(64, 2), (1, 64)]), V(cmb_t, oo, [(64, 2), (1, 64)]),
                                   V(sgeff_t, 0, [(0, 2), (1, 64)]), ALU.mult)
                                tj = grp[2 * bk]
                                tt(V(y_t, tj * 1024 + ycol + h * 64, [(1024, 2), (1, 64)]), V(cmb_t, oo, [(64, 2), (1, 64)]),
                                   V(gate_t, gs * 1024 + (2 * bk) * 256 + h * 64, [(256, 2), (1, 64)]), ALU.mult)
                        stages += [(2, s_a1), (6, s_a2), (8, s_a3)]
                        return stages

                    LOOK = 2
                    pending = []
                    nsteps = len(items)
                    sched_at = {}
                    if nxt_calls:
                        spacing = max(1, nsteps // len(nxt_calls))
                        for qi, call in enumerate(nxt_calls):
                            base = 1 + qi * spacing
                            for si, f in enumerate(qproj_stages(*call)):
                                off = (0, 1, 3, 5, 7, 9, 12, 15)[si] if spacing >= 17 else 0
                                sched_at.setdefault(base + off, []).append(f)
                    delayed = []
                    for idx in range(len(items) + LOOK):
                        for f in sched_at.pop(idx, []):
                            f()
                        while delayed and delayed[0][0] <= idx:
                            delayed.pop(0)[1]()
                        if idx < len(items):
                            h, m, ki, kt = items[idx]
                            sbk = psS()
                            if isA:
                                hm = h * 2 + m
                                mm(V(sbk, 0, [(1, nq)]),
                                   V(kT_t, (hm // 4) * 2304 + kt * 128, [(1, 128)]),
                                   V(fpb_t, gs * 4096 + hm * 512, [(1, nq)]),
                                   start=True, stop=True)
                            else:
                                mm(V(sbk, 0, [(1, nq)]),
                                   V(kT_t, kt * 128, [(1, 128)]),
                                   V(fpb_t, gs * 4096 + h * 512, [(1, nq)]),
                                   start=True, stop=True)
                            pi = nxt("pT", 4)
                            act(V(pT_t, pi * 512, [(1, nq)]), V(sbk, 0, [(1, nq)]), AF.Exp, scale=scale)
                            pending.append((h, m, ki, kt, pi))
                        if idx >= LOOK and pending:
                            h, m, ki, kt, pi = pending.pop(0)
                            if ki == 0:
                                obank[(h, m)] = ps[4 + nxt("OT", 2)]
                            hv = h if isA else h // 2
                            mm(V(obank[(h, m)], 0, [(1, nq)]),
                               V(V_t, (kt * nvh + hv) * 65, [(1, 128)]),
                               V(pT_t, pi * 512, [(1, nq)]),
                               start=(ki == 0), stop=(ki == nk - 1))
                            if ki == nk - 1:
                                for dl, fn in post(h, m):
                                    delayed.append((idx + 3 + (dl if nk >= 8 else 0), fn))
                                delayed.sort(key=lambda x: x[0])
                    for _, fn in delayed:
                        fn()
                    for k_ in sorted(sched_at):
                        for f in sched_at[k_]:
                            f()

            wsA1 = next_wslot()
            wsA2 = next_wslot()
            wsB1 = next_wslot()
            chains_ = [stage1_chain(0, tiles[0::2]), stagger(stage1_chain(1, tiles[1::2]), 4),
                       attn_branch("A", "p1e", wsA1), stagger(attn_branch("A", "p1o", wsA1), 3),
                       stagger(attn_branch("B", "p1e", wsB1), 1), stagger(attn_branch("B", "p1o", wsB1), 5)]
            if b == 0 and l + 1 < depth:
                chains_.append(mods_gen(l + 1, y32_t, lambda: ps[4], "pool"))
            run_chains(chains_)
            wrelease()
            chk("A1")
            attn_branch("A", "p2", wsA2)
            wrelease()
            chk("A")
            wrelease()
            chk("B1")
            wsB2 = next_wslot()
            attn_branch("B", "p2", wsB2)
            wrelease()
            chk("B")

            pbanks[:] = [0, 2, 3]
            ws1 = next_wslot()
            ws2 = next_wslot()

            def sgu_chain(par, tl):
                pt, pz, po = (ps[0], ps[2], ps[4]) if par == 0 else (ps[3], ps[5], ps[6])
                sb_ = 32 + 48 * par
                vnb = V(vnb_t) if par == 0 else V(fpb_t, 2 * 4608, [(1, 256)])
                uu = (lambda off, n: V(uu_t, off, [(1, n)])) if par == 0 else (lambda off, n: V(fp_t, 4096 + off, [(1, n)]))
                for t in tl:
                    for kc in range(8):
                        mm(V(pt, 0, [(1, 512)]), V(hT_t, kc * 2304 + t * 128, [(1, 128)]), ring_ap(ws1, kc, 0, 512),
                           start=(kc == 0), stop=(kc == 7))
                    yield
                    for kc in range(8):
                        mm(V(pz, 0, [(1, 256)]), V(hT_t, kc * 2304 + t * 128, [(1, 128)]), ring_ap(ws2, kc, 0, 256),
                           start=(kc == 0), stop=(kc == 7))
                    yield
                    S.op("dve", lambda e, pt=pt, sb_=sb_: e.bn_stats(out=V(st_t, sb_, [(1, 6)]), in_=V(pt, 256, [(1, 256)])),
                         reads=[V(pt, 256, [(1, 256)])], writes=[V(st_t, sb_, [(1, 6)])])
                    cp(uu(0, 256), V(pt, 0, [(1, 256)]), eng="act")
                    yield
                    S.op("dve", lambda e, sb_=sb_: e.bn_aggr(out=V(st_t, sb_ + 8, [(1, 2)]), in_=V(st_t, sb_, [(1, 6)])),
                         reads=[V(st_t, sb_, [(1, 6)])], writes=[V(st_t, sb_ + 8, [(1, 2)])])
                    act(sAp(par, 512, 256), V(pz, 0, [(1, 256)]), AF.Exp, scale=-1.0)
                    yield
                    act(stat(sb_ + 10), stat(sb_ + 9), AF.Ln, scale=1.0, bias=float(EPS))
                    act(sAp(par, 512, 256), sAp(par, 512, 256), AF.Ln, bias=1.0)
                    yield
                    act(stat(sb_ + 10), stat(sb_ + 10), AF.Exp, scale=-0.5)
                    act(sAp(par, 512, 256), sAp(par, 512, 256), AF.Exp, scale=-1.0)
                    yield
                    stt(stat(sb_ + 11), stat(sb_ + 8), -1.0, stat(sb_ + 10), ALU.mult, ALU.mult)
                    yield
                    ts(sAp(par, 0, 256), V(pt, 256, [(1, 256)]), stat(sb_ + 10), stat(sb_ + 11), ALU.mult, ALU.add)
                    yield
                    tt(sAp(par, 0, 256), sAp(par, 0, 256), pb("lng", 256), ALU.mult)
                    yield
                    tt(vnb, sAp(par, 0, 256), pb("lnb", 256), ALU.add)
                    yield
                    for g in range(4):
                        mm(V(po, g * 64, [(1, 64)]), V(wsT_t, g * 128, [(1, 128)]),
                           bass.AP(vnb.tensor, vnb.offset + g * 64, [list(vnb.ap[0]), [1, 64]]), start=True, stop=True)
                    tt(sAp(par, 256, 256), sAp(par, 512, 256), V(pz, 0, [(1, 256)]), ALU.mult)
                    yield
                    for g in range(4):
                        stt(uu(256 + g * 64, 64), V(po, g * 64, [(1, 64)]), V(bsT_t, g, [(1, 1)]),
                            uu(g * 64, 64), ALU.add, ALU.mult)
                    yield
                    tt(V(y_t, t * 1024 + 512, [(1, 256)]), uu(256, 256), sAp(par, 256, 256), ALU.mult)
                    yield

            run_chains([sgu_chain(0, out_tiles[0::2]), stagger(sgu_chain(1, out_tiles[1::2]), 6)])
            wrelease(2)
            chk("C")
            pbanks[:] = [0]
            wd1 = next_wslot()
            wd2 = next_wslot()

            class _B:
                pass

            bF = _B()
            bF.P, bF.S, bF.O = ps[0], ps[2], ps[6]
            bF.rTb, bF.qeb, bF.keb, bF.vb, bF.qeT = V(rTb_t, 0, [(1, 128)], 0, 32), V(qeb_t), V(keb_t), V(vb_t), V(qeT_t)
            bF.kTm = lambda off, dims: V(kTm_t, off, dims)
            bF.am = lambda off, dims: V(am_t, off, dims)
            bF.Sst, bF.Sb, bF.Wt = V(Sst_t), (lambda off, n: V(Sb_t, off, [(1, n)])), V(Wt_t)
            bF.gl = lambda off, n=128: V(gl_t, off, [(1, n)])
            bF.E = stat(48)
            bB = _B()
            bB.P, bB.S, bB.O = ps[5], ps[3], ps[7]
            FB = 7168 * 2
            bB.rTb, bB.qeb, bB.keb = V(fpb_t, FB, [(1, 128)], 0, 32), V(fpb_t, FB + 128, [(1, 128)]), V(fpb_t, FB + 256, [(1, 128)])
            bB.vb, bB.qeT = V(fpb_t, FB + 384, [(1, 256)]), V(fpb_t, FB + 640, [(1, 128)])
            bB.kTm = lambda off, dims: V(fpb_t, FB + 768 + off, dims)
            bB.am = lambda off, dims: V(fpb_t, FB + 1280 + off, dims)
            bB.Sb = lambda off, n: V(fpb_t, FB + 1792 + off, [(1, n)])
            bB.Sst, bB.Wt = V(fp_t, 6144, [(1, 256)]), V(fp_t, 6400, [(1, 256)])
            bB.gl = lambda off, n=128: V(fp_t, 6656 + off, [(1, n)])
            bB.E = stat(49)

            bF.rTb128 = V(rTb_t)
            bB.rTb128 = V(fpb_t, FB, [(1, 128)])
            bF.qeT2 = [V(qeT_t), V(pT_t, 0, [(1, 128)])]
            bF.vb2 = [V(vb_t), V(pT_t, 256, [(1, 256)])]
            bF.am2 = [lambda off, dims: V(am_t, off, dims), lambda off, dims: V(pT_t, 512 + off, dims)]
            bF.E2 = [stat(48), stat(50)]
            bF.Wt2 = [V(Wt_t), V(oT_t, 0, [(1, 256)])]
            bB.qeT2 = [bB.qeT, V(pT_t, 1024, [(1, 128)])]
            bB.vb2 = [bB.vb, V(pT_t, 1024 + 256, [(1, 256)])]
            bB.am2 = [bB.am, lambda off, dims: V(pT_t, 1024 + 512 + off, dims)]
            bB.E2 = [stat(49), stat(51)]
            bB.Wt2 = [bB.Wt, V(oT_t, 256, [(1, 256)])]
            gdone = {}

            def qkv(t, off, n):
                if t < 9:
                    return V(kTA_t, t * 512 + off, [(1, n)])
                return V(VA_t, (t - 9) * 512 + off, [(1, n)])

            def gla_prep(sweep, order, B, C_, kp, ks, par):
                Ltri = LtriF if sweep == 0 else LtriB
                for i, t in enumerate(order):
                    if i % 2 != par:
                        continue
                    while gdone.get(ks, 0) < i - 1:
                        yield
                    s_ = i % 2
                    qeT, vb, am, E, Wt = B.qeT2[s_], B.vb2[s_], B.am2[s_], B.E2[s_], B.Wt2[s_]
                    need_out = (t in out_tiles)
                    p1 = C_.bank
                    vb = qkv(t, 256, 256)
                    mm(V(C_.bank, 0, [(1, 128)]), V(kTB_t, t * 128, [(1, 128)], 0, 32), V(w2bd_t, sweep * 128, [(1, 128)], 0, 32),
                       start=True, stop=True)
                    yield
                    tt(C_.gl(384), V(C_.bank, 0, [(1, 128)]), pb("gb", 128, sweep * 128), ALU.add)
                    yield
                    act(C_.gl(384), C_.gl(384), AF.Exp, scale=-1.0)
                    yield
                    act(C_.gl(0), C_.gl(384), AF.Ln, bias=1.0)
                    yield
                    mm(V(C_.bank, 0, [(1, 128)]), Ltri, C_.gl(0), start=True, stop=True)
                    mm(V(C_.bank, 128, [(1, 1)]), C_.gl(0), ones_c, start=True, stop=True)
                    yield
                    act(E, V(C_.bank, 128, [(1, 1)]), AF.Exp, scale=float(-1.0 / 16))
                    act(C_.gl(128), V(C_.bank, 0, [(1, 128)]), AF.Exp, scale=float(-1.0 / 16))
                    act(C_.gl(256), V(C_.bank, 0, [(1, 128)]), AF.Exp, scale=float(1.0 / 16))
                    yield
                    tt(C_.keb, qkv(t, 128, 128), C_.gl(256), ALU.mult)
                    if need_out:
                        stt(C_.qeb, qkv(t, 0, 128), float(32 ** -0.5), C_.gl(128), ALU.mult, ALU.mult)
                    yield
                    mm(V(p1, 0, [(1, 256)]), C_.keb, vb, start=True, stop=True)
                    yield
                    stt(Wt, V(p1, 0, [(1, 256)]), E, blockmask, ALU.mult, ALU.mult)
                    yield
                    if need_out:
                        ptq = psT()
                        tr(V(ptq, 0, [(1, 128)]), C_.qeb)
                        tr(V(ptq, 128, [(1, 128)]), C_.keb)
                        cp(qeT, V(ptq, 0, [(1, 128)]), eng="dve")
                        tt(C_.kTm(0, [(128, 4), (1, 128)]), V(ptq, 0, [(0, 4), (1, 128)]),
                           V(cf_t, 512, [(1, 4), (0, 128)]), ALU.mult)
                        cp(C_.keT, V(ptq, 128, [(1, 128)]), eng="act")
                        yield
                        mm(V(C_.bank, 0, [(1, 512)]), C_.keT, C_.kTm(0, [(1, 512)]), start=True, stop=True)
                        yield
                        tt(am(0, [(128, 4), (1, 128)]), V(C_.bank, 0, [(128, 4), (1, 128)]),
                           V(cf_t, 0 if sweep == 0 else 128, [(0, 4), (1, 128)]), ALU.mult)
                        yield
                    pdone[kp].add(i)

            def gla_seq(sweep, order, B, kp, ks):
                memset(B.Sst, 0.0)
                memset(B.Sb(0, 256), 0.0)
                yield
                for i, t in enumerate(order):
                    while i not in pdone[kp]:
                        yield
                    s_ = i % 2
                    qeT, am, E, Wt = B.qeT2[s_], B.am2[s_], B.E2[s_], B.Wt2[s_]
                    vb = qkv(t, 256, 256)
                    if t in out_tiles:
                        py = ps[4]
                        mm(V(py, 0, [(1, 256)]), qeT, B.Sb(0, 256), start=True, stop=False)
                        for h in range(4):
                            mm(V(py, h * 64, [(1, 64)]), am(h * 128, [(1, 128)]),
                               bass.AP(vb.tensor, vb.offset + h * 64, [list(vb.ap[0]), [1, 64]]), start=False, stop=True)
                        if sweep == 0:
                            cp(V(y_t, t * 1024 + 768, [(1, 256)]), V(py, 0, [(1, 256)]), eng="act")
                        else:
                            cp(V(fpb_t, t * 256, [(1, 256)]), V(py, 0, [(1, 256)]), eng="act")
                        yield
                    stt(B.Sst, B.Sst, E, Wt, ALU.mult, ALU.add)
                    yield
                    cp(B.Sb(0, 256), B.Sst, eng="dve")
                    yield
                    gdone[ks] = i + 1

            for cg in range(5):
                ncol = 512 if cg < 4 else 256
                pr_ = ps[(2, 3)[cg % 2]]
                for kc in range(8):
                    mm(V(pr_, 0, [(1, ncol)], 0, 32), ring_ap(wd2, kc, 256, 32), V(hT_t, kc * 2304 + cg * 512, [(1, ncol)]),
                       start=(kc == 0), stop=(kc == 7))
                cp(V(kTB_t, cg * 512, [(1, ncol)], 0, 32), V(pr_, 0, [(1, ncol)], 0, 32), eng=("act" if cg % 2 else "dve"))
            qbanks = (ps[0], ps[5], ps[4], ps[6])
            for t in tiles:
                pq_ = qbanks[t % 4]
                for kc in range(8):
                    mm(V(pq_, 0, [(1, 512)]), V(hT_t, kc * 2304 + t * 128, [(1, 128)]), ring_ap(wd1, kc, 0, 512),
                       start=(kc == 0), stop=(kc == 7))
                cp(qkv(t, 0, 512), V(pq_, 0, [(1, 512)]), eng=("act" if t % 2 else "dve"))
            ordB = [1, 0] + list(range(NT - 1, 1, -1))
            pdone = {"pF": set(), "pB": set()}
            cF0, cF1, cB0, cB1 = _B(), _B(), _B(), _B()
            cF0.bank, cF1.bank, cB0.bank, cB1.bank = ps[0], ps[2], ps[3], ps[5]
            cF0.gl, cF0.qeb, cF0.keb, cF0.kTm, cF0.keT = bF.gl, bF.qeb, bF.keb, bF.kTm, bF.rTb128
            cB0.gl, cB0.qeb, cB0.keb, cB0.kTm, cB0.keT = bB.gl, bB.qeb, bB.keb, bB.kTm, bB.rTb128
            for c_, fo, ho in ((cF1, 4096, 0), (cB1, 4608, 1024)):
                c_.gl = lambda off, n=128, fo=fo: V(fp_t, fo + off, [(1, n)])
                c_.qeb = V(hbf_t, ho, [(1, 128)])
                c_.keb = V(hbf_t, ho + 128, [(1, 128)])
                c_.keT = V(hbf_t, ho + 256, [(1, 128)])
                c_.kTm = lambda off, dims, ho=ho: V(hbf_t, ho + 384 + off, dims)
            run_chains([gla_prep(0, tiles, bF, cF0, "pF", "sF", 0), stagger(gla_prep(1, ordB, bB, cB0, "pB", "sB", 0), 3),
                        stagger(gla_prep(0, tiles, bF, cF1, "pF", "sF", 1), 6),
                        stagger(gla_prep(1, ordB, bB, cB1, "pB", "sB", 1), 9),
                        gla_seq(0, tiles, bF, "pF", "sF"), gla_seq(1, ordB, bB, "pB", "sB")])
            def dfin_chain(fs, tl):
                pz = ps[0] if fs == 0 else ps[2]
                o_ = lambda off, dims: V(fp_t, 6144 + fs * 1024 + off, dims)
                for t in tl:
                    yD = V(y_t, t * 1024 + 768, [(1, 256)])
                    for kc in range(8):
                        mm(V(pz, 0, [(1, 256)]), V(hT_t, kc * 2304 + t * 128, [(1, 128)]), ring_ap(wd2, kc, 0, 256),
                           start=(kc == 0), stop=(kc == 7))
                    tt(o_(0, [(1, 256)]), V(fpb_t, t * 256, [(1, 256)]), yD, ALU.add)
                    yield
                    act(o_(256, [(1, 256)]), o_(0, [(1, 256)]), AF.Square)
                    act(o_(512, [(1, 256)]), V(pz, 0, [(1, 256)]), AF.Exp, scale=-1.0)
                    yield
                    red(stat(52 + 4 * fs, 4), o_(256, [(64, 4), (1, 64)]))
                    act(o_(512, [(1, 256)]), o_(512, [(1, 256)]), AF.Ln, bias=1.0)
                    yield
                    act(stat(52 + 4 * fs, 4), stat(52 + 4 * fs, 4), AF.Ln, scale=float(1.0 / 64), bias=float(EPS))
                    act(o_(512, [(1, 256)]), o_(512, [(1, 256)]), AF.Exp, scale=-1.0)
                    yield
                    act(stat(52 + 4 * fs, 4), stat(52 + 4 * fs, 4), AF.Exp, scale=-0.5)
                    tt(o_(256, [(1, 256)]), o_(512, [(1, 256)]), V(pz, 0, [(1, 256)]), ALU.mult)
                    yield
                    tt(o_(0, [(64, 4), (1, 64)]), o_(0, [(64, 4), (1, 64)]), V(st_t, 52 + 4 * fs, [(1, 4), (0, 64)]), ALU.mult)
                    yield
                    tt(o_(0, [(64, 4), (1, 64)]), o_(0, [(64, 4), (1, 64)]), V(pbc_t, PO["gng"], [(0, 4), (1, 64)]), ALU.mult)
                    yield
                    tt(yD, o_(0, [(1, 256)]), o_(256, [(1, 256)]), ALU.mult)
                    yield

            run_chains([dfin_chain(0, out_tiles[0::2]), stagger(dfin_chain(1, out_tiles[1::2]), 4)])

            memset(V(VA_t, 64, [(65, NT * 4)]), 1.0)
            wrelease(2)
            chk("D")
            if debug_y and b == 0 and l == 0:
                for t in tiles:
                    cp(sA(), V(y_t, t * 1024, [(1, 1024)]), eng="dve")
                    dma("sp", dbg_d[t * 128:(t + 1) * 128, :], sA(), reads=[sA()], writes=[("dbg", t, t + 1, 0, 1)])

            for slot, row in ((0, 2), (1, b)):
                src = bass.AP(mods_d.tensor, l * 9 * D + row * 3 * D + 2 * D, [[0, 128], [1, D]])
                dma("sp", mods(slot), src, reads=[("modsd", l, l + 1, 0, 1)], writes=[mods(slot)])
            if last:
                dma("sp", mods(2), bass.AP(fng_d.tensor, 0, [[0, 128], [1, D]]), writes=[mods(2)])
            pbanks[:] = [0, 2, 3]
            wo = [next_wslot(), next_wslot()]

            def stageW_chain(par, tl):
                banks = (ps[0], ps[2]) if par == 0 else (ps[3], ps[4])
                for t in tl:
                    ys = par
                    pt = psT()
                    for kc in range(8):
                        tr(V(pt, kc * 128, [(1, 128)]), V(y_t, t * 1024 + kc * 128, [(1, 128)]), inc=(kc == 7))
                    cp(V(ytT_t, ys * 1024, [(1, 1024)]), V(pt, 0, [(1, 1024)]), eng=("dve" if par else "act"))
                    sl = par
                    if l == 0:
                        src = ctx_d[b, t * 128:(t + 1) * 128, :] if t < 2 else x_d[b, (t - 2) * 128:(t - 1) * 128, :]
                        dma("sp", xt(sl), src, writes=[xt(sl)])
                    else:
                        dma("sp", xt(sl), xres_d[b, t * 128:(t + 1) * 128, :], reads=[("xres", b * NT + t, b * NT + t + 1, 0, 1)],
                            writes=[xt(sl)])
                    yield
                    gsl = 0 if t < 2 else 1
                    for cb in range(2):
                        po = banks[cb]
                        for kc in range(8):
                            mm(V(po, 0, [(1, 512)]), V(ytT_t, ys * 1024 + kc * 128, [(1, 128)]), ring_ap(wo[cb], kc, 0, 512),
                               start=(kc == 0), stop=(kc == 7))
                        yield
                    for cb in range(2):
                        po = banks[cb]
                        tt(sAp(par, cb * 512, 512), V(po, 0, [(1, 512)]), mods(gsl, cb * 512, 512), ALU.mult)
                        yield
                        tt(V(fp_t, 4096 + sl * 1024 + cb * 512, [(1, 512)]), V(fp_t, 4096 + sl * 1024 + cb * 512, [(1, 512)]),
                           sAp(par, cb * 512, 512), ALU.add)
                        yield
                    if not last:
                        dma("sp", xres_d[b, t * 128:(t + 1) * 128, :], xt(sl), reads=[xt(sl)],
                            writes=[("xres", b * NT + t, b * NT + t + 1, 0, 1)])
                    else:
                        act(sAp(par), xt(sl), AF.Square, accum=stat(8 + 2 * par))
                        yield
                        act(stat(9 + 2 * par), stat(8 + 2 * par), AF.Ln, scale=float(1.0 / D), bias=float(EPS))
                        yield
                        act(stat(9 + 2 * par), stat(9 + 2 * par), AF.Exp, scale=-0.5)
                        yield
                        stt(xt(sl), xt(sl), stat(9 + 2 * par), mods(2), ALU.mult, ALU.mult)
                        tok = S.op("sp", lambda e, t=t, sl=sl, b=b: e.dma_start(out=out_d[b, (t - 2) * 128:(t - 1) * 128, :], in_=xt(sl)),
                                   reads=[xt(sl)], writes=[], dma=True)
                        out_toks.append(tok)
                    yield

            run_chains([stageW_chain(0, out_tiles[0::2]), stagger(stageW_chain(1, out_tiles[1::2]), 3)])
            wrelease(2)
            chk("W")

    try:
        main_loops()
    except _Stop:
        if debug_y and stop_at == "stage1":
            for kc in range(8):
                stg = V(fp_t, (kc % 2) * 2304, [(1, 2304)])
                cp(stg, V(hT_t, kc * 2304, [(1, 2304)]), eng="dve")
                dma("sp", bass.AP(dbg_d.tensor, kc * 128 * 2304, [[2304, 128], [1, 2304]]), stg, reads=[stg],
                    writes=[("dbg", kc, kc + 1, 0, 1)])
        elif debug_y and stop_at == "A1":
            for c in range(2):
                stg = V(fp_t, c * 2304, [(1, 2304)])
                cp(stg, V(kTA_t, c * 2304, [(1, 2304)]), eng="dve")
                dma("sp", bass.AP(dbg_d.tensor, c * 128 * 2304, [[2304, 128], [1, 2304]]), stg, reads=[stg],
                    writes=[("dbg", c, c + 1, 0, 1)])
        elif debug_y:
            for t in range(NT):
                cp(sA(), V(y_t, t * 1024, [(1, 1024)]), eng="dve")
                dma("sp", dbg_d[t * 128:(t + 1) * 128, :], sA(), reads=[sA()], writes=[("dbg", t, t + 1, 0, 1)])

    final = {}
    for tok in out_toks:
        final[tok[0]] = max(final.get(tok[0], 0), tok[1])
    for q in ("sp", "pool"):
        for i, lt_ in enumerate(S.dq[q]["last"]):
            if lt_ is not None:
                final[lt_[0]] = max(final.get(lt_[0], 0), lt_[1])
    S.ops["sp"].append((list(final.items()), None, None))

    engmap = {"pe": "tensor", "act": "scalar", "dve": "vector", "pool": "gpsimd", "sp": "sync"}

    def replay(eng, e):
        for waits, fn, tok in S.ops[eng]:
            for k, v in waits:
                e.wait_ge(sems[k], v)
            if fn is None:
                continue
            ins = fn(e)
            if tok is not None:
                ins.then_inc(sems[tok[0]], 16 if tok[0][0] == "d" else 1)

    with nc.Block() as block:
        @block.tensor
        def _(e):
            replay("pe", e)

        @block.scalar
        def _(e):
            replay("act", e)

        @block.vector
        def _(e):
            replay("dve", e)

        @block.gpsimd
        def _(e):
            replay("pool", e)

        @block.sync
        def _(e):
            replay("sp", e)

    st.close()
    return nc, S


def _const_tables():
    bf = ml_dtypes.bfloat16
    cf = np.zeros((128, 647), np.float32)
    j = np.arange(128)[:, None]
    i = np.arange(128)[None, :]
    cf[:, 0:128] = (j <= i)
    cf[:, 128:256] = (j >= i)
    hc = np.arange(128) // 32
    hv = np.arange(256) // 64
    cf[:, 256:512] = (hc[:, None] == hv[None, :])
    cf[:, 512:516] = (hc[:, None] == np.arange(4)[None, :])
    cf[:, 516] = 1.0
    cf[:, 517:645] = np.eye(128)
    cf[:, 645:647] = ((np.arange(128) // 64)[:, None] == np.arange(2)[None, :])
    cb = np.zeros((128, 3200), np.float32)
    cb[:, 0:128] = np.eye(128)
    tt_ = np.arange(TL)
    rows = (tt_ // 64).astype(np.float32)
    cols = (tt_ % 64).astype(np.float32)

    def tables(half):
        inv = (np.float32(10000.0) ** (-np.arange(0, half, 2, dtype=np.float32) / np.float32(half))).astype(np.float32)
        ar = rows[:, None] * inv[None, :]
        ac = cols[:, None] * inv[None, :]
        F = half // 2
        C = np.zeros((TL, 2, 2, F), np.float32)
        Sg = np.zeros((TL, 2, 2, F), np.float32)
        for rc, a in enumerate((ar, ac)):
            C[:, rc, 0] = np.cos(a)
            C[:, rc, 1] = np.cos(a)
            Sg[:, rc, 0] = -np.sin(a)
            Sg[:, rc, 1] = np.sin(a)
        W = 4 * F
        C = C.reshape(16, 128, W).transpose(1, 0, 2).reshape(128, 16 * W)
        Sg = Sg.reshape(16, 128, W).transpose(1, 0, 2).reshape(128, 16 * W)
        return C, Sg

    CA, SA = tables(16)
    CB, SB = tables(32)
    cb[:, 128:640] = CA
    cb[:, 640:1152] = SA
    cb[:, 1152:2176] = CB
    cb[:, 2176:3200] = SB
    return cf, cb.astype(bf)


_CACHE = {}


def _prep(x, c, ctx, c_ctx, ada_w, ada_b, norm_g, w_in, da_lq1, da_lk1, da_lq2, da_lk2,
          da_subln_g, gq_qnorm_g, gq_knorm_g, sg_ln_g, sg_ln_b, sg_w, sg_b,
          gla_w2_f, gla_b_f, gla_w2_b, gla_b_b, gla_norm_g, w_out, final_norm_g, build=True):
    f = lambda a: np.ascontiguousarray(np.asarray(a, dtype=np.float32))
    x, c, ctx, c_ctx = f(x), f(c), f(ctx), f(c_ctx)
    ncores = 8
    nc = None
    if build:
        if "nc" not in _CACHE:
            _CACHE["nc"] = build_program()[0]
        nc = _CACHE["nc"]
    cf, cb = _const_tables()
    L = DEPTH
    pvec = np.concatenate([f(sg_ln_g), f(sg_ln_b), f(gla_b_f), f(gla_b_b), f(da_subln_g), f(gq_qnorm_g),
                           f(gq_knorm_g), f(gla_norm_g), f(da_lq1), f(da_lq2), f(da_lk1), f(da_lk2)], axis=1)
    assert pvec.shape == (L, NPV)
    sgbT = np.ascontiguousarray(f(sg_b).transpose(0, 2, 1))
    sgwT = np.ascontiguousarray(f(sg_w).transpose(0, 3, 1, 2))
    w2bd = np.zeros((L, 32, 256), np.float32)
    w2bd[:, 0:16, 0:128] = f(gla_w2_f)
    w2bd[:, 16:32, 128:256] = f(gla_w2_b)
    shared = dict(ada_w=f(ada_w), ada_b=f(ada_b), norm_g=f(norm_g), w_in=f(w_in), w_out=f(w_out), pvec=pvec,
                  sgbT=sgbT, sgwT=sgwT, w2bd=w2bd, fng=f(final_norm_g), cf32=cf, cbf=cb)
    in_maps = []
    for i in range(ncores):
        rows = np.stack([c[2 * i], c[2 * i + 1], c_ctx], axis=0)
        cT = np.ascontiguousarray(rows.reshape(3, 8, 128).transpose(2, 1, 0))
        m = dict(shared)
        m["x"] = x[2 * i:2 * i + 2]
        m["ctx"] = ctx[2 * i:2 * i + 2]
        m["cT"] = cT
        in_maps.append(m)
    return nc, in_maps


def kernel(**inputs):
    nc, in_maps = _prep(**inputs)
    res = run_bass_kernel_spmd(nc, in_maps, core_ids=list(range(len(in_maps))))
    out = np.concatenate([r["out"] for r in res.results], axis=0)
    return out.astype(np.float32)
```

```python
import math
from contextlib import ExitStack

import numpy as np
import ml_dtypes

import concourse.bass as bass
import concourse.mybir as mybir
from concourse.bass_utils import run_bass_kernel_spmd

F32 = mybir.dt.float32
BF16 = mybir.dt.bfloat16
AF = mybir.ActivationFunctionType
ALU = mybir.AluOpType
AX = mybir.AxisListType

D = 1024
DEPTH = 4
NB = 2
TL = 2048
TC = 256
NT = 18
P_IN = 3360
EPS = 1e-6
NPV = 1152


def _esz(dt):
    return 2 if dt == BF16 else 4


class Sched:
    ENGS = ("pe", "act", "dve", "pool", "sp")

    def __init__(self, same_sync=True, n_dma=10):
        self.ops = {e: [] for e in self.ENGS}
        self.cnt = {e: 0 for e in self.ENGS}
        self.clock = {e: {} for e in self.ENGS}
        self.recs = {}
        self.snap = {}
        self.same_sync = same_sync
        self.n_dma = n_dma
        self.dq = {e: dict(next=0, cnt=[0] * n_dma, last=[None] * n_dma) for e in ("sp", "pool", "act")}
        self.nwaits = 0

    @staticmethod
    def box(a):
        if isinstance(a, tuple):
            return a
        t = a.tensor
        ps = 1
        for s in t.shape[1:]:
            ps *= s
        esz = _esz(a.dtype)
        off = a.offset
        p0 = off // ps
        f0 = off % ps
        apl = a.ap
        npart = apl[0][1]
        ext = 1
        for st, c in apl[1:]:
            ext += (c - 1) * abs(st)
        if t.name.startswith("ps"):
            return (t.name, 0, 128, 0, 1 << 20)
        return (t.name, p0, p0 + npart, f0 * esz, (f0 + ext) * esz)

    def op(self, eng, fn, reads=(), writes=(), dma=False, inc=True):
        deps = {}

        def add(tok):
            k, v = tok
            if deps.get(k, 0) < v:
                deps[k] = v

        rb = [self.box(a) for a in reads]
        wb = [self.box(a) for a in writes]
        own_k = ("e", eng)
        for b in rb:
            isps = b[0].startswith("ps")
            for r in self.recs.get(b[0], ()):
                if (r[5] or (isps and r[6][0] != own_k)) and r[1] < b[2] and b[1] < r[2] and r[3] < b[4] and b[3] < r[4]:
                    add(r[6])
        for b in wb:
            for r in self.recs.get(b[0], ()):
                if r[1] < b[2] and b[1] < r[2] and r[3] < b[4] and b[3] < r[4]:
                    add(r[6])
        if dma:
            q = self.dq[eng]
            i = q["next"]
            q["next"] = (i + 1) % self.n_dma
            if q["last"][i] is not None:
                add(q["last"][i])
            q["cnt"][i] += 1
            tok = (("d", eng, i), q["cnt"][i] * 16)
            q["last"][i] = tok
        else:
            if inc:
                self.cnt[eng] += 1
                tok = (("e", eng), self.cnt[eng])
            else:
                tok = (("e", eng), self.cnt[eng] + 1)
        clk = self.clock[eng]
        waits = []
        own = ("e", eng)
        for k, v in deps.items():
            if k == own and (eng == "pe" or not self.same_sync):
                continue
            if clk.get(k, 0) >= v:
                continue
            waits.append((k, v))
            sn = self.snap.get((k, v))
            if sn:
                for kk, vv in sn.items():
                    if clk.get(kk, 0) < vv:
                        clk[kk] = vv
            clk[k] = v
        self.nwaits += len(waits)
        if inc or dma:
            self.snap[tok] = dict(clk)
        self.ops[eng].append((waits, fn, tok if (inc or dma) else None))
        for b in wb:
            lst = self.recs.setdefault(b[0], [])
            lst[:] = [r for r in lst if not (b[1] <= r[1] and r[2] <= b[2] and b[3] <= r[3] and r[4] <= b[4])]
            lst.append([b[0], b[1], b[2], b[3], b[4], True, tok])
        for b in rb:
            lst = self.recs.setdefault(b[0], [])
            found = False
            for r in lst:
                if (not r[5]) and r[1] == b[1] and r[2] == b[2] and r[3] == b[3] and r[4] == b[4] and r[6][0] == tok[0]:
                    if r[6][1] < tok[1]:
                        r[6] = tok
                    found = True
                    break
            if not found:
                lst.append([b[0], b[1], b[2], b[3], b[4], False, tok])
        return tok

    def replay(self, eng, e, sems):
        for waits, fn, tok in self.ops[eng]:
            for k, v in waits:
                e.wait_ge(sems[k], v)
            ins = fn(e)
            if tok is not None:
                ins.then_inc(sems[tok[0]], 16 if tok[0][0] == "d" else 1)

    def final_wait(self, eng, toks):
        ws = [(k, v) for (k, v) in toks]
        self.ops[eng].append((ws, None, None))


class _Stop(Exception):
    pass


def build_program(depth=DEPTH, nb=NB, debug_y=False, same_sync=True, stop_at=None):
    nc = bass.Bass("TRN2", target_bir_lowering=False)
    S = Sched(same_sync=same_sync)
    st = ExitStack()

    def din(name, shape, dt=F32):
        return nc.dram_tensor(name, list(shape), dt, kind="ExternalInput").ap()

    x_d = din("x", [nb, TL, D])
    ctx_d = din("ctx", [nb, TC, D])
    cT_d = din("cT", [128, 8, 3])
    adaw_d = din("ada_w", [DEPTH, D, 3 * D])
    adab_d = din("ada_b", [DEPTH, 3 * D])
    ng_d = din("norm_g", [DEPTH, D])
    win_d = din("w_in", [DEPTH, D, P_IN])
    wout_d = din("w_out", [DEPTH, D, D])
    pvec_d = din("pvec", [DEPTH, NPV])
    sgbT_d = din("sgbT", [DEPTH, 128, 4])
    sgwT_d = din("sgwT", [DEPTH, 128, 4, 128])
    w2bd_d = din("w2bd", [DEPTH, 32, 256])
    fng_d = din("fng", [D])
    cf32_d = din("cf32", [128, 647])
    cbf_d = din("cbf", [128, 3200], BF16)
    out_d = nc.dram_tensor("out", [nb, TL, D], F32, kind="ExternalOutput").ap()
    if debug_y:
        dbg_d = nc.dram_tensor("dbg", [NT * 128, D], F32, kind="ExternalOutput").ap()
    xres_d = nc.dram_tensor("xres", [nb, NT * 128, D], F32).ap()
    mods_d = nc.dram_tensor("modsd", [DEPTH, 3, 3, D], F32).ap()

    def sb(name, free, dt):
        return st.enter_context(nc.sbuf_tensor("s_" + name, [128, free], dt))

    ident_t = sb("ident", 128, BF16)
    rope_t = sb("rope", 3072, BF16)
    cf_t = sb("cf", 647, F32)
    pbc_t = sb("pbc", NPV, F32)
    bsT_t = sb("bsT", 4, F32)
    w2bd_t = sb("w2bd", 256, BF16)
    wsT_t = sb("wsT", 512, BF16)
    sm_t = sb("sm", 64, F32)
    sgeff_t = sb("sgeff", 64, F32)
    fp_t = sb("fpool", 8192, F32)
    fpb_t = fp_t.bitcast(BF16)
    hT_t = sb("hT", 8 * 2304, BF16)
    y_t = sb("y", NT * 1024, BF16)
    NRING = 3
    ring_t = sb("ring", NRING * 4096, BF16)
    kTA_t = sb("kTA", 2 * 2304, BF16)
    VA_t = sb("VA", NT * 4 * 65 + 64, BF16)
    kTB_t = sb("kTB", 2304, BF16)
    VB_t = sb("VB", NT * 2 * 65 + 64, BF16)
    hbf_t = sb("hbf", 2048, BF16)
    gate_t = sb("gate", 2 * 1024, BF16)
    pT_t = sb("pT", 4 * 512, BF16)
    ytT_t = sb("ytT", 2 * 1024, BF16)
    qr_t = sb("qr", 2 * 256, BF16)
    oT_t = sb("oT", 2 * 512, F32)
    cmb_t = sb("cmb", 512, F32)
    st_t = sb("stats", 128, F32)
    rTb_t = sb("rTb", 128, BF16)
    qeb_t = sb("qeb", 128, BF16)
    keb_t = sb("keb", 128, BF16)
    vb_t = sb("vb", 256, BF16)
    qeT_t = sb("qeT", 128, BF16)
    kTm_t = sb("kTm", 512, BF16)
    am_t = sb("am", 512, BF16)
    Sst_t = sb("Sst", 256, F32)
    Sb_t = sb("Sb", 256, BF16)
    Wt_t = sb("Wt", 256, F32)
    gl_t = sb("gl", 512, F32)
    vnb_t = sb("vnb", 256, BF16)
    uu_t = sb("uu", 512, F32)
    scT_t = sb("scT", 48, F32)

    ps = []
    for i in range(8):
        ps.append(st.enter_context(nc.psum_tensor(f"ps{i}", [128, 512], F32)))
    ps1b = ps[1].bitcast(BF16)

    sems = {}
    for e in Sched.ENGS:
        sems[("e", e)] = st.enter_context(nc.semaphore(f"s_{e}"))
    for e in ("sp", "pool", "act"):
        for i in range(S.n_dma):
            sems[("d", e, i)] = st.enter_context(nc.semaphore(f"d_{e}{i}"))

    def V(t, off=0, dims=None, p0=0, npart=128):
        ps_ = 1
        for s in t.shape[1:]:
            ps_ *= s
        if dims is None:
            dims = [(1, ps_ - off)]
        return bass.AP(t, p0 * ps_ + off, [[ps_, npart]] + [[s, c] for (s, c) in dims])

    def dma(eng, out, in_, reads=(), writes=()):
        S.op(eng, lambda e: e.dma_start(out=out, in_=in_), reads=list(reads), writes=list(writes), dma=True)

    def mm(out, lhsT, rhs, start, stop, **kw):
        S.op("pe", lambda e: e.matmul(out, lhsT, rhs, start=start, stop=stop, **kw),
             reads=[lhsT, rhs], writes=[out], inc=bool(stop))

    def tr(out, in_, inc=True):
        S.op("pe", lambda e: e.transpose(out, in_, V(ident_t, 0, [(1, 128)])), reads=[in_], writes=[out], inc=inc)

    def act(out, in_, func, scale=1.0, bias=None, accum=None):
        rd = [in_]
        wr = [out]
        kw = {}
        if bias is not None:
            kw["bias"] = bias
            if not isinstance(bias, float):
                rd.append(bias)
        if not isinstance(scale, float):
            rd.append(scale)
        if accum is not None:
            kw["accum_out"] = accum
            wr.append(accum)
        S.op("act", lambda e: e.activation(out=out, in_=in_, func=func, scale=scale, **kw), reads=rd, writes=wr)

    def tt(out, in0, in1, op, eng="dve"):
        S.op(eng, lambda e: e.tensor_tensor(out=out, in0=in0, in1=in1, op=op), reads=[in0, in1], writes=[out])

    def ts(out, in0, s1, s2, op0, op1=None, eng="dve"):
        rd = [in0]
        if not isinstance(s1, float):
            rd.append(s1)
        if s2 is not None and not isinstance(s2, float):
            rd.append(s2)
        if op1 is None:
            S.op(eng, lambda e: e.tensor_scalar(out=out, in0=in0, scalar1=s1, scalar2=None, op0=op0), reads=rd, writes=[out])
        else:
            S.op(eng, lambda e: e.tensor_scalar(out=out, in0=in0, scalar1=s1, scalar2=s2, op0=op0, op1=op1), reads=rd, writes=[out])

    def stt(out, in0, scalar, in1, op0, op1):
        rd = [in0, in1]
        if not isinstance(scalar, float):
            rd.append(scalar)
        S.op("dve", lambda e: e.scalar_tensor_tensor(out=out, in0=in0, scalar=scalar, in1=in1, op0=op0, op1=op1),
             reads=rd, writes=[out])

    def cp(out, in_, eng="dve"):
        if eng == "act":
            S.op("act", lambda e: e.copy(out=out, in_=in_), reads=[in_], writes=[out])
        else:
            S.op(eng, lambda e: e.tensor_copy(out=out, in_=in_), reads=[in_], writes=[out])

    def memset(ap, val, eng="dve"):
        S.op(eng, lambda e: e.memset(ap, val), writes=[ap])

    def ttr(out, in0, in1, accum):
        S.op("dve", lambda e: e.tensor_tensor_reduce(out=out, in0=in0, in1=in1, scale=1.0, scalar=0.0,
                                                    op0=ALU.mult, op1=ALU.add, accum_out=accum),
             reads=[in0, in1], writes=[out, accum])

    def red(out, in_):
        S.op("dve", lambda e: e.tensor_reduce(out=out, in_=in_, axis=AX.X, op=ALU.add), reads=[in_], writes=[out])

    def recip(out, in_):
        S.op("dve", lambda e: e.reciprocal(out=out, in_=in_), reads=[in_], writes=[out])

    def rstd_from(out, ssum, k):
        act(out, ssum, AF.Ln, scale=float(k), bias=float(EPS))
        act(out, out, AF.Exp, scale=-0.5)

    rot = {}

    def nxt(key, n):
        i = rot.get(key, 0)
        rot[key] = (i + 1) % n
        return i

    pbanks = [0]

    def psP():
        return ps[pbanks[nxt("P", len(pbanks)) % len(pbanks)]]

    def psT():
        return ps1b

    def psS():
        return ps[(0, 2, 3)[nxt("S", 3)]]

    def psO():
        return ps[(5, 6, 7)[nxt("O", 3)]]

    def mods(slot, off=0, n=1024):
        return V(fp_t, slot * 1024 + off, [(1, n)])

    def xt(slot):
        return V(fp_t, 4096 + slot * 1024, [(1, 1024)])

    sA = lambda off=0, n=1024: V(fp_t, 6144 + off, [(1, n)])
    sB = lambda off=0, n=512: V(fp_t, 7168 + off, [(1, n)])
    sC = lambda off=0, n=512: V(fp_t, 7680 + off, [(1, n)])

    def stat(i, n=1):
        return V(st_t, i, [(1, n)])

    PO = dict(lng=0, lnb=256, gb=512, subg=768, qng=832, kng=896, gng=960, lq=1024, lk=1088)

    def pb(name, n, off=0):
        return V(pbc_t, PO[name] + off, [(1, n)])

    LtriF = V(cf_t, 0, [(1, 128)])
    LtriB = V(cf_t, 128, [(1, 128)])
    blockmask = V(cf_t, 256, [(1, 256)])
    ones_c = V(cf_t, 516, [(1, 1)])

    wreq = []
    wstate = dict(issued=0, used=0)

    def wblock_aps(kind, l, c0, ncol, dst_off):
        src = win_d if kind == "in" else wout_d
        ncols_total = P_IN if kind == "in" else D
        in_ap = bass.AP(src.tensor, l * D * ncols_total + c0, [[ncols_total, 128], [128 * ncols_total, 8], [1, ncol]])
        return dst_off, in_ap, ncol

    def plan_weights():
        plan = []
        for b in range(nb):
            for l in range(depth):
                plan.append([("in", l, 256, 512, 0)])
                plan.append([("in", l, 0, 256, 0), ("in", l, 768, 256, 256)])
                plan.append([("in", l, 1280, 256, 0)])
                plan.append([("in", l, 1024, 256, 0), ("in", l, 1536, 256, 256)])
                plan.append([("in", l, 1792, 512, 0)])
                plan.append([("in", l, 2304, 256, 0)])
                plan.append([("in", l, 2560, 512, 0)])
                plan.append([("in", l, 3072, 288, 0)])
                plan.append([("out", l, 0, 512, 0)])
                plan.append([("out", l, 512, 512, 0)])
        return plan

    wplan = plan_weights()

    def ring_ap(slot, kc, c0, n):
        return V(ring_t, slot * 4096 + kc * 512 + c0, [(1, n)])

    def issue_block(n):
        slot = n % NRING
        for (kind, l, c0, ncol, dst) in wplan[n]:
            _, in_ap, _ = wblock_aps(kind, l, c0, ncol, dst)
            out_ap = V(ring_t, slot * 4096 + dst, [(512, 8), (1, ncol)])
            dma("pool", out_ap, in_ap, writes=[out_ap])

    def issue_weights(upto):
        while wstate["issued"] < min(upto, len(wplan)):
            issue_block(wstate["issued"])
            wstate["issued"] += 1

    wstate["released"] = 0

    def next_wslot():
        n = wstate["used"]
        wstate["used"] += 1
        assert wstate["issued"] > n, "weight block not prefetched"
        return n % NRING

    def wrelease(k=1):
        for _ in range(k):
            wstate["released"] += 1
            issue_weights(wstate["released"] + NRING)

    dma("sp", V(ident_t), cbf_d[:, 0:128], writes=[V(ident_t)])
    dma("sp", V(rope_t), cbf_d[:, 128:3200], writes=[V(rope_t)])
    dma("sp", V(cf_t), cf32_d[:, :], writes=[V(cf_t)])
    dma("sp", V(scT_t, 0, [(1, 24)]), cT_d.rearrange("p a b -> p (a b)"), writes=[V(scT_t, 0, [(1, 24)])])
    memset(V(VA_t), 1.0)
    memset(V(VB_t), 1.0)
    act(V(scT_t, 24, [(1, 24)]), V(scT_t, 0, [(1, 24)]), AF.Exp, scale=-1.0)
    act(V(scT_t, 24, [(1, 24)]), V(scT_t, 24, [(1, 24)]), AF.Ln, bias=1.0)
    act(V(scT_t, 24, [(1, 24)]), V(scT_t, 24, [(1, 24)]), AF.Exp, scale=-1.0)
    tt(V(scT_t, 0, [(1, 24)]), V(scT_t, 24, [(1, 24)]), V(scT_t, 0, [(1, 24)]), ALU.mult)

    pbanks[:] = [0, 2, 3]

    y32_t = y_t.bitcast(F32)

    def mods_gen(l, T_, bank_fn, dq):
        def R3(off, n):
            return V(T_, 4096 + off, [(1, n)], 0, 3)
        adab_bc = bass.AP(adab_d.tensor, l * 3 * D, [[0, 3], [1, 3 * D]])
        dma("sp", R3(0, 3072), adab_bc, writes=[R3(0, 3072)])
        ng_bc = bass.AP(ng_d.tensor, l * D, [[0, 3], [1, D]])
        dma("sp", R3(3072, 1024), ng_bc, writes=[R3(3072, 1024)])
        yield
        for cb in range(12):
            stg = cb % 2
            stg_ap = V(T_, stg * 2048, [(256, 8), (1, 256)])
            in_ap = bass.AP(adaw_d.tensor, l * D * 3 * D + cb * 256, [[3 * D, 128], [128 * 3 * D, 8], [1, 256]])
            dma(dq, stg_ap, in_ap, writes=[stg_ap])
            yield
            pt = bank_fn()
            for kc in range(8):
                mm(V(pt, 0, [(1, 256)], 0, 3), V(scT_t, kc * 3, [(1, 3)]), V(T_, stg * 2048 + kc * 256, [(1, 256)]),
                   start=(kc == 0), stop=(kc == 7))
            tt(R3(cb * 256, 256), V(pt, 0, [(1, 256)], 0, 3), R3(cb * 256, 256), ALU.add)
            yield
        stt(R3(1024, 1024), R3(1024, 1024), 1.0, R3(3072, 1024), ALU.add, ALU.mult)
        yield
        for j, src_off in ((0, 1024), (1, 0), (2, 2048)):
            o_ap = bass.AP(mods_d.tensor, l * 9 * D + j * D, [[3 * D, 3], [1, D]])
            dma("sp", o_ap, R3(src_off, 1024), reads=[R3(src_off, 1024)], writes=[("modsd", l, l + 1, 0, 1)])
        yield

    for _ in mods_gen(0, fp_t, psP, "sp"):
        pass

    issue_weights(NRING)
    nonlocal_dummy = None

    def proj(t, slot, c0, ncol, pt=None, kparts=128):
        if pt is None:
            pt = psP()
        for kc in range(8):
            mm(V(pt, 0, [(1, ncol)]), V(hT_t, kc * 2304 + t * 128, [(1, 128)]), ring_ap(slot, kc, c0, ncol),
               start=(kc == 0), stop=(kc == 7))
        return pt

    def rope(dst, src, lt, which, G, t1o=6144, t2o=7168, scr=None):
        scr = fp_t if scr is None else scr
        if which == "A":
            W, Fq, cb, sbo = 32, 8, 0, 512
        else:
            W, Fq, cb, sbo = 64, 16, 1024, 2048
        cos_bc = V(rope_t, cb + lt * W, [(0, G), (1, W)])
        t1 = V(scr, t1o, [(1, G * W)])
        t1v = V(scr, t1o, [(W, G), (1, W)])
        t2 = V(scr, t2o, [(1, G * W)])
        in0v = src(0, [(W, G), (1, W)])
        S.op("dve", lambda e, in0v=in0v, t1v=t1v, cos_bc=cos_bc: e.tensor_tensor(out=t1v, in0=in0v, in1=cos_bc, op=ALU.mult),
             reads=[src(0, [(1, G * W)]), cos_bc], writes=[t1])
        for half in range(2):
            o = V(scr, t2o + half * Fq, [(W, G), (2 * Fq, 2), (1, Fq)])
            i0 = src((1 - half) * Fq, [(W, G), (2 * Fq, 2), (1, Fq)])
            i1 = V(rope_t, sbo + lt * W + half * Fq, [(0, G), (2 * Fq, 2), (1, Fq)])
            S.op("dve", lambda e, o=o, i0=i0, i1=i1: e.tensor_tensor(out=o, in0=i0, in1=i1, op=ALU.mult),
                 reads=[src(0, [(1, G * W)]), V(rope_t, sbo + lt * W, [(1, W)])], writes=[t2])
        dst(t1, t2)

    def silu_gate(out_bf, z_ps, n, tmp):
        act(tmp, z_ps, AF.Exp, scale=-1.0)
        act(tmp, tmp, AF.Ln, bias=1.0)
        act(tmp, tmp, AF.Exp, scale=-1.0)
        tt(out_bf, tmp, z_ps, ALU.mult)

    out_toks = []

    S.marks = []

    def chk(name):
        if len(name) <= 8 and not any(ch.isdigit() for ch in name[2:]):
            S.marks.append((name, len(S.ops["pe"])))
        if stop_at == name:
            raise _Stop()

    def main_loops():
      for b in range(nb):
        for l in range(depth):
            chk("prologue")
            last = (l == DEPTH - 1)
            lam_init = 0.8 - 0.6 * math.exp(-0.3 * l)
            tiles = list(range(NT))
            out_tiles = list(range(2, NT)) if last else tiles

            pv_bc = bass.AP(pvec_d.tensor, l * NPV, [[0, 128], [1, NPV]])
            dma("sp", V(pbc_t), pv_bc, writes=[V(pbc_t)])
            dma("sp", V(bsT_t), sgbT_d[l], writes=[V(bsT_t)])
            dma("pool", V(wsT_t), sgwT_d[l].rearrange("p a b -> p (a b)"), writes=[V(wsT_t)])
            dma("pool", V(w2bd_t, 0, [(1, 256)], 0, 32), w2bd_d[l], writes=[V(w2bd_t, 0, [(1, 256)], 0, 32)])
            tt(V(sm_t, 0, [(1, 64)]), pb("lq", 64), pb("lk", 64), ALU.mult)
            red(stat(4, 2), V(sm_t, 0, [(32, 2), (1, 32)]))
            act(stat(0, 2), stat(4, 2), AF.Exp)
            tt(stat(2), stat(0), stat(1), ALU.subtract)
            ts(stat(3), stat(2), float(lam_init), -1.0, ALU.add, ALU.mult)
            ts(V(sgeff_t), pb("subg", 64), float(1.0 - lam_init), None, ALU.mult)
            neglam = stat(3)

            for slot, (row, j) in enumerate(((2, 0), (2, 1), (b, 0), (b, 1))):
                src = bass.AP(mods_d.tensor, l * 9 * D + row * 3 * D + j * D, [[0, 128], [1, D]])
                dma("sp", mods(slot), src, reads=[("modsd", l, l + 1, 0, 1)], writes=[mods(slot)])

            chk("params")
            def stagger(g_, n):
                for _ in range(n):
                    yield
                yield from g_

            def run_chains(gens):
                gens = list(gens)
                while gens:
                    for g_ in list(gens):
                        try:
                            next(g_)
                        except StopIteration:
                            gens.remove(g_)

            def sAp(par, off=0, n=1024):
                return V(fp_t, 6144 + par * 1024 + off, [(1, n)])

            def stage1_chain(par, tl):
                for t in tl:
                    sl = par
                    if l == 0:
                        src = ctx_d[b, t * 128:(t + 1) * 128, :] if t < 2 else x_d[b, (t - 2) * 128:(t - 1) * 128, :]
                        dma("sp", xt(sl), src, writes=[xt(sl)])
                    else:
                        dma("sp", xt(sl), xres_d[b, t * 128:(t + 1) * 128, :], reads=[("xres", b * NT + t, b * NT + t + 1, 0, 1)],
                            writes=[xt(sl)])
                    yield
                    act(sAp(par), xt(sl), AF.Square, accum=stat(8 + 2 * par))
                    yield
                    act(stat(9 + 2 * par), stat(8 + 2 * par), AF.Ln, scale=float(1.0 / D), bias=float(EPS))
                    yield
                    act(stat(9 + 2 * par), stat(9 + 2 * par), AF.Exp, scale=-0.5)
                    yield
                    ms = 0 if t < 2 else 2
                    stt(sAp(par), xt(sl), stat(9 + 2 * par), mods(ms), ALU.mult, ALU.mult)
                    yield
                    hb = V(hbf_t, par * 1024, [(1, 1024)])
                    tt(hb, sAp(par), mods(ms + 1), ALU.add)
                    yield
                    pt = psT()
                    for kc in range(8):
                        tr(V(pt, kc * 128, [(1, 128)]), V(hbf_t, par * 1024 + kc * 128, [(1, 128)]), inc=(kc == 7))
                    cp(V(hT_t, t * 128, [(2304, 8), (1, 128)]), V(pt, 0, [(128, 8), (1, 128)]),
                       eng=("dve" if par else "act"))
                    s1done.add(t)
                    yield

            s1done = set()

            chk("stage1")
            def attn_branch(which, phase, ws_in):
                isA = which == "A"
                kT_t = kTA_t if isA else kTB_t
                V_t = VA_t if isA else VB_t
                nkc = 2 if isA else 1
                nvh = 4 if isA else 2
                ycol = 0 if isA else 256
                scale = float(32 ** -0.5) if isA else float(64 ** -0.5)
                def pass1_chain(par):
                    kw = 256 if isA else 128
                    vw = 256 if isA else 128
                    if isA:
                        pbk = ps[0] if par == 0 else ps[3]
                        krt, kro = (qr_t, 0) if par == 0 else (vnb_t, 0)
                        scr, t1o, t2o, nrm = (oT_t, 0, 256, 0) if par == 0 else (cmb_t, 0, 256, 0)
                        sti = 16
                    else:
                        pbk = ps[2] if par == 0 else ps[5]
                        krt, kro = (qr_t, 256) if par == 0 else (qr_t, 384)
                        scr, t1o, t2o, nrm = (oT_t, 512, 640, 768) if par == 0 else (uu_t, 0, 128, 256)
                        sti = 16 if par == 0 else 18
                    for t in tiles[par::2]:
                        while t not in s1done:
                            yield
                        for kc in range(8):
                            mm(V(pbk, 0, [(1, kw + vw)]), V(hT_t, kc * 2304 + t * 128, [(1, 128)]), ring_ap(ws_in, kc, 0, kw + vw),
                               start=(kc == 0), stop=(kc == 7))
                        yield
                        cp(V(V_t, t * nvh * 65, [(65, nvh), (1, 64)]), V(pbk, kw, [(64, nvh), (1, 64)]), eng="act")
                        kr = V(krt, kro, [(1, kw)])
                        if isA:
                            ksrc = lambda off, dims, pbk=pbk: V(pbk, off, dims)
                        else:
                            act(V(scr, nrm, [(1, 128)]), V(pbk, 0, [(1, 128)]), AF.Square)
                            yield
                            red(stat(sti, 2), V(scr, nrm, [(64, 2), (1, 64)]))
                            yield
                            act(stat(sti, 2), stat(sti, 2), AF.Ln, scale=float(1.0 / 64), bias=float(EPS))
                            yield
                            act(stat(sti, 2), stat(sti, 2), AF.Exp, scale=-0.5)
                            yield
                            tt(V(scr, nrm, [(64, 2), (1, 64)]), V(pbk, 0, [(64, 2), (1, 64)]),
                               V(st_t, sti, [(1, 2), (0, 64)]), ALU.mult)
                            yield
                            tt(V(scr, nrm, [(64, 2), (1, 64)]), V(scr, nrm, [(64, 2), (1, 64)]),
                               V(pbc_t, PO["kng"], [(0, 2), (1, 64)]), ALU.mult)
                            ksrc = lambda off, dims, scr=scr, nrm=nrm: V(scr, nrm + off, dims)
                        yield
                        if t >= 2:
                            G = kw // (32 if isA else 64)
                            rope(lambda a, bb_, kr=kr: tt(kr, a, bb_, ALU.add), ksrc, t - 2, which, G, t1o, t2o, scr)
                        else:
                            cp(kr, ksrc(0, [(1, kw)]), eng="dve")
                        yield
                        ptt = psT()
                        for c in range(nkc):
                            tr(V(ptt, c * 128, [(1, 128)]), V(krt, kro + c * 128, [(1, 128)]), inc=(c == nkc - 1))
                        cp(V(kT_t, t * 128, [(2304, nkc), (1, 128)]), V(ptt, 0, [(128, nkc), (1, 128)]), eng="act")
                        yield

                if phase in ("p1e", "p1o"):
                    return pass1_chain(0 if phase == "p1e" else 1)
                ws = ws_in
                pbanks[:] = [1]
                groups = ([] if last else [[0, 1]]) + [[2 + 4 * g + j for j in range(4)] for g in range(4)]
                gslots = [nxt("qTm", 2) for _ in groups]

                def qproj_stages(gs, j, t):
                    cx = {}

                    def st_proj(lo, hi):
                        def f():
                            if lo == 0:
                                cx["pt"] = psP()
                                cx["qs"] = nxt("qr", 2)
                            pt = cx["pt"]
                            for kc in range(lo, hi):
                                mm(V(pt, 0, [(1, 512)]), V(hT_t, kc * 2304 + t * 128, [(1, 128)]), ring_ap(ws, kc, 0, 512),
                                   start=(kc == 0), stop=(kc == 7))
                        return f

                    def st_e1():
                        pt = cx["pt"]
                        gtmp = V(uu_t, 0, [(1, 256)])
                        act(gtmp, V(pt, 256, [(1, 256)]), AF.Exp, scale=-1.0)
                        act(gtmp, gtmp, AF.Ln, bias=1.0)
                        act(gtmp, gtmp, AF.Exp, scale=-1.0)
                        if not isA:
                            act(sC(0, 256), V(pt, 0, [(1, 256)]), AF.Square)

                    def st_e2():
                        if not isA:
                            red(stat(16, 4), V(fp_t, 7680, [(64, 4), (1, 64)]))

                    def st_e3():
                        if not isA:
                            rstd_from(stat(16, 4), stat(16, 4), 1.0 / 64)

                    def st_e4():
                        pt = cx["pt"]
                        qs = cx["qs"]
                        qrv = V(qr_t, qs * 256, [(1, 256)])
                        if isA:
                            qsrc = lambda off, dims: V(pt, off, dims)
                            if t >= 2:
                                rope(lambda a, bb_: tt(qrv, a, bb_, ALU.add), qsrc, t - 2, "A", 8)
                            else:
                                cp(qrv, V(pt, 0, [(1, 256)]), eng="dve")
                        else:
                            tt(V(fp_t, 7680, [(64, 4), (1, 64)]), V(pt, 0, [(64, 4), (1, 64)]),
                               V(st_t, 16, [(1, 4), (0, 64)]), ALU.mult)
                            tt(V(fp_t, 7680, [(64, 4), (1, 64)]), V(fp_t, 7680, [(64, 4), (1, 64)]),
                               V(pbc_t, PO["qng"], [(0, 4), (1, 64)]), ALU.mult)
                            qsrc = lambda off, dims: V(fp_t, 7680 + off, dims)
                            qperm = V(qr_t, qs * 256, [(64, 2), (128, 2), (1, 64)])
                            if t >= 2:
                                rope(lambda a, bb_, qperm=qperm: tt(qperm, V(fp_t, 6144, [(128, 2), (64, 2), (1, 64)]),
                                                                   V(fp_t, 7168, [(128, 2), (64, 2), (1, 64)]), ALU.add),
                                     qsrc, t - 2, "B", 4)
                            else:
                                cp(qperm, V(fp_t, 7680, [(128, 2), (64, 2), (1, 64)]), eng="dve")
                        tt(V(gate_t, gs * 1024 + j * 256, [(1, 256)]), V(uu_t, 0, [(1, 256)]), V(pt, 256, [(1, 256)]), ALU.mult)

                    def st_tr():
                        qs = cx["qs"]
                        ptt = psT()
                        cx["ptt"] = ptt
                        for c in range(2):
                            tr(V(ptt, c * 128, [(1, 128)]), V(qr_t, qs * 256 + c * 128, [(1, 128)]), inc=(c == 1))

                    def st_mask():
                        ptt = cx["ptt"]
                        for c in range(2):
                            if isA:
                                tt(V(fpb_t, gs * 4096 + (4 * c) * 512 + j * 128, [(512, 4), (1, 128)]),
                                   V(ptt, c * 128, [(0, 4), (1, 128)]), V(cf_t, 512, [(1, 4), (0, 128)]), ALU.mult)
                            else:
                                tt(V(fpb_t, gs * 4096 + c * 512 + j * 128, [(1024, 2), (1, 128)]),
                                   V(ptt, c * 128, [(0, 2), (1, 128)]), V(cf_t, 645, [(1, 2), (0, 128)]), ALU.mult)

                    return [st_proj(0, 4), st_proj(4, 8), st_e1, st_e2, st_e3, st_e4, st_tr, st_mask]

                def qproj_tile(gs, j, t):
                    for f in qproj_stages(gs, j, t):
                        f()

                for gi, grp in enumerate(groups):
                    isctx = grp[0] < 2
                    nq = len(grp) * 128
                    kts = [0, 1] if isctx else list(range(NT))
                    gs = gslots[gi]
                    if gi == 0:
                        for j, t in enumerate(grp):
                            qproj_tile(gs, j, t)
                    nxt_calls = []
                    if gi + 1 < len(groups):
                        nxt_calls = [(gslots[gi + 1], j, t) for j, t in enumerate(groups[gi + 1])]
                    nj = len(grp)
                    if isA:
                        items = [(h, m, ki, kt) for h in range(4) for m in range(2) for ki, kt in enumerate(kts)]
                    else:
                        items = [(hq, 0, ki, kt) for hq in range(4) for ki, kt in enumerate(kts)]
                    nk = len(kts)
                    obank = {}

                    def post(h, m):
                        ob = obank[(h, m)]
                        osl = nxt("oT", 2)
                        cp(V(oT_t, osl * 512, [(1, nq)], 0, 65), V(ob, 0, [(1, nq)], 0, 65), eng="dve")
                        return post2(h, m, osl)

                    def post2(h, m, osl):
                        def s_tr():
                            for j in range(nj):
                                if isA:
                                    cbk = ps[6 + j // 2]
                                    col = (m * 2 + j % 2) * 65
                                else:
                                    cbk = ps[6 + h % 2]
                                    col = j * 65
                                S.op("pe", lambda e, cbk=cbk, col=col, osl=osl, j=j: e.transpose(
                                    V(cbk, col, [(1, 65)]), V(oT_t, osl * 512 + j * 128, [(1, 128)], 0, 65),
                                    V(cf_t, 517, [(1, 65)], 0, 65)),
                                    reads=[V(oT_t, osl * 512 + j * 128, [(1, 128)], 0, 65)], writes=[V(cbk, col, [(1, 65)])])
                        stages = [(0, s_tr)]
                        if isA and m == 0:
                            return stages
                        t0_ = grp[0]
                        if not isA:
                            def s_b():
                                cbk = ps[6 + h % 2]
                                recip(stat(64, nj), V(cbk, 64, [(65, nj)]))
                                tt(V(cmb_t, 0, [(64, nj), (1, 64)]), V(cbk, 0, [(65, nj), (1, 64)]),
                                   V(st_t, 64, [(1, nj), (0, 64)]), ALU.mult)
                                tt(V(y_t, t0_ * 1024 + ycol + h * 64, [(1024, nj), (1, 64)]), V(cmb_t, 0, [(64, nj), (1, 64)]),
                                   V(gate_t, gs * 1024 + h * 64, [(256, nj), (1, 64)]), ALU.mult)
                            stages.append((2, s_b))
                            return stages
                        nbk = (nj + 1) // 2

                        def s_a1():
                            for bk in range(nbk):
                                cbk = ps[6 + bk]
                                so = 64 + bk * 8
                                oo = bk * 128
                                recip(stat(so, 4), V(cbk, 64, [(65, 4)]))
                                tt(stat(so + 4, 2), stat(so + 2, 2), V(st_t, 3, [(0, 2)]), ALU.mult)
                                tt(V(cmb_t, 256 + oo, [(64, 2), (1, 64)]), V(cbk, 130, [(65, 2), (1, 64)]),
                                   V(st_t, so + 4, [(1, 2), (0, 64)]), ALU.mult)
                                tt(V(cmb_t, oo, [(64, 2), (1, 64)]), V(cbk, 0, [(65, 2), (1, 64)]),
                                   V(st_t, so, [(1, 2), (0, 64)]), ALU.mult)
                                tt(V(cmb_t, oo, [(1, 128)]), V(cmb_t, oo, [(1, 128)]), V(cmb_t, 256 + oo, [(1, 128)]), ALU.add)
                                tt(V(cmb_t, 256 + oo, [(1, 128)]), V(cmb_t, oo, [(1, 128)]), V(cmb_t, oo, [(1, 128)]), ALU.mult)
                                red(stat(so + 6, 2), V(cmb_t, 256 + oo, [(64, 2), (1, 64)]))

                        def s_a2():
                            for bk in range(nbk):
                                so = 64 + bk * 8
                                act(stat(so + 6, 2), stat(so + 6, 2), AF.Ln, scale=float(1.0 / 64), bias=float(EPS))
                                act(stat(so + 6, 2), stat(so + 6, 2), AF.Exp, scale=-0.5)

                        def s_a3():
                            for bk in range(nbk):
                                so = 64 + bk * 8
                                oo = bk * 128
                                tt(V(cmb_t, oo, [(64, 2), (1, 64)]), V(cmb_t, oo, [(64, 2), (1, 64)]),
                                   V(st_t, so + 6, [(1, 2), (0, 64)]), ALU.mult)
                                tt(V(cmb_t, oo, [(64, 2), (1, 64)]), V(cmb_t, oo, [(64, 2), (1, 64)]),
                                   V(sgeff_t, 0, [(0, 2), (1, 64)]), ALU.mult)
                                tj = grp[2 * bk]
                                tt(V(y_t, tj * 1024 + ycol + h * 64, [(1024, 2), (1, 64)]), V(cmb_t, oo, [(64, 2), (1, 64)]),
                                   V(gate_t, gs * 1024 + (2 * bk) * 256 + h * 64, [(256, 2), (1, 64)]), ALU.mult)
                        stages += [(2, s_a1), (6, s_a2), (8, s_a3)]
                        return stages

                    LOOK = 2
                    pending = []
                    nsteps = len(items)
                    sched_at = {}
                    if nxt_calls:
                        spacing = max(1, nsteps // len(nxt_calls))
                        for qi, call in enumerate(nxt_calls):
                            base = 1 + qi * spacing
                            for si, f in enumerate(qproj_stages(*call)):
                                off = (0, 1, 3, 5, 7, 9, 12, 15)[si] if spacing >= 17 else 0
                                sched_at.setdefault(base + off, []).append(f)
                    delayed = []
                    for idx in range(len(items) + LOOK):
                        for f in sched_at.pop(idx, []):
                            f()
                        while delayed and delayed[0][0] <= idx:
                            delayed.pop(0)[1]()
                        if idx < len(items):
                            h, m, ki, kt = items[idx]
                            sbk = psS()
                            if isA:
                                hm = h * 2 + m
                                mm(V(sbk, 0, [(1, nq)]),
                                   V(kT_t, (hm // 4) * 2304 + kt * 128, [(1, 128)]),
                                   V(fpb_t, gs * 4096 + hm * 512, [(1, nq)]),
                                   start=True, stop=True)
                            else:
                                mm(V(sbk, 0, [(1, nq)]),
                                   V(kT_t, kt * 128, [(1, 128)]),
                                   V(fpb_t, gs * 4096 + h * 512, [(1, nq)]),
                                   start=True, stop=True)
                            pi = nxt("pT", 4)
                            act(V(pT_t, pi * 512, [(1, nq)]), V(sbk, 0, [(1, nq)]), AF.Exp, scale=scale)
                            pending.append((h, m, ki, kt, pi))
                        if idx >= LOOK and pending:
                            h, m, ki, kt, pi = pending.pop(0)
                            if ki == 0:
                                obank[(h, m)] = ps[4 + nxt("OT", 2)]
                            hv = h if isA else h // 2
                            mm(V(obank[(h, m)], 0, [(1, nq)]),
                               V(V_t, (kt * nvh + hv) * 65, [(1, 128)]),
                               V(pT_t, pi * 512, [(1, nq)]),
                               start=(ki == 0), stop=(ki == nk - 1))
                            if ki == nk - 1:
                                for dl, fn in post(h, m):
                                    delayed.append((idx + 3 + (dl if nk >= 8 else 0), fn))
                                delayed.sort(key=lambda x: x[0])
                    for _, fn in delayed:
                        fn()
                    for k_ in sorted(sched_at):
                        for f in sched_at[k_]:
                            f()

            wsA1 = next_wslot()
            wsA2 = next_wslot()
            wsB1 = next_wslot()
            chains_ = [stage1_chain(0, tiles[0::2]), stagger(stage1_chain(1, tiles[1::2]), 3),
                       attn_branch("A", "p1e", wsA1), stagger(attn_branch("A", "p1o", wsA1), 2),
                       stagger(attn_branch("B", "p1e", wsB1), 4), stagger(attn_branch("B", "p1o", wsB1), 8)]
            if b == 0 and l + 1 < depth:
                chains_.append(mods_gen(l + 1, y32_t, lambda: ps[4], "pool"))
            run_chains(chains_)
            wrelease()
            chk("A1")
            attn_branch("A", "p2", wsA2)
            wrelease()
            chk("A")
            wrelease()
            chk("B1")
            wsB2 = next_wslot()
            attn_branch("B", "p2", wsB2)
            wrelease()
            chk("B")

            pbanks[:] = [0, 2, 3]
            ws1 = next_wslot()
            ws2 = next_wslot()

            def sgu_chain(par, tl):
                pt, pz, po = (ps[0], ps[2], ps[4]) if par == 0 else (ps[3], ps[5], ps[6])
                sb_ = 32 + 48 * par
                vnb = V(vnb_t) if par == 0 else V(fpb_t, 2 * 4608, [(1, 256)])
                uu = (lambda off, n: V(uu_t, off, [(1, n)])) if par == 0 else (lambda off, n: V(fp_t, 4096 + off, [(1, n)]))
                for t in tl:
                    for kc in range(8):
                        mm(V(pt, 0, [(1, 512)]), V(hT_t, kc * 2304 + t * 128, [(1, 128)]), ring_ap(ws1, kc, 0, 512),
                           start=(kc == 0), stop=(kc == 7))
                    yield
                    for kc in range(8):
                        mm(V(pz, 0, [(1, 256)]), V(hT_t, kc * 2304 + t * 128, [(1, 128)]), ring_ap(ws2, kc, 0, 256),
                           start=(kc == 0), stop=(kc == 7))
                    yield
                    S.op("dve", lambda e, pt=pt, sb_=sb_: e.bn_stats(out=V(st_t, sb_, [(1, 6)]), in_=V(pt, 256, [(1, 256)])),
                         reads=[V(pt, 256, [(1, 256)])], writes=[V(st_t, sb_, [(1, 6)])])
                    cp(uu(0, 256), V(pt, 0, [(1, 256)]), eng="act")
                    yield
                    S.op("dve", lambda e, sb_=sb_: e.bn_aggr(out=V(st_t, sb_ + 8, [(1, 2)]), in_=V(st_t, sb_, [(1, 6)])),
                         reads=[V(st_t, sb_, [(1, 6)])], writes=[V(st_t, sb_ + 8, [(1, 2)])])
                    act(sAp(par, 512, 256), V(pz, 0, [(1, 256)]), AF.Exp, scale=-1.0)
                    yield
                    act(stat(sb_ + 10), stat(sb_ + 9), AF.Ln, scale=1.0, bias=float(EPS))
                    act(sAp(par, 512, 256), sAp(par, 512, 256), AF.Ln, bias=1.0)
                    yield
                    act(stat(sb_ + 10), stat(sb_ + 10), AF.Exp, scale=-0.5)
                    act(sAp(par, 512, 256), sAp(par, 512, 256), AF.Exp, scale=-1.0)
                    yield
                    stt(stat(sb_ + 11), stat(sb_ + 8), -1.0, stat(sb_ + 10), ALU.mult, ALU.mult)
                    yield
                    ts(sAp(par, 0, 256), V(pt, 256, [(1, 256)]), stat(sb_ + 10), stat(sb_ + 11), ALU.mult, ALU.add)
                    yield
                    tt(sAp(par, 0, 256), sAp(par, 0, 256), pb("lng", 256), ALU.mult)
                    yield
                    tt(vnb, sAp(par, 0, 256), pb("lnb", 256), ALU.add)
                    yield
                    for g in range(4):
                        mm(V(po, g * 64, [(1, 64)]), V(wsT_t, g * 128, [(1, 128)]),
                           bass.AP(vnb.tensor, vnb.offset + g * 64, [list(vnb.ap[0]), [1, 64]]), start=True, stop=True)
                    tt(sAp(par, 256, 256), sAp(par, 512, 256), V(pz, 0, [(1, 256)]), ALU.mult)
                    yield
                    for g in range(4):
                        stt(uu(256 + g * 64, 64), V(po, g * 64, [(1, 64)]), V(bsT_t, g, [(1, 1)]),
                            uu(g * 64, 64), ALU.add, ALU.mult)
                    yield
                    tt(V(y_t, t * 1024 + 512, [(1, 256)]), uu(256, 256), sAp(par, 256, 256), ALU.mult)
                    yield

            run_chains([sgu_chain(0, out_tiles[0::2]), stagger(sgu_chain(1, out_tiles[1::2]), 7)])
            wrelease(2)
            chk("C")
            pbanks[:] = [0]
            wd1 = next_wslot()
            wd2 = next_wslot()

            class _B:
                pass

            bF = _B()
            bF.P, bF.S, bF.O = ps[0], ps[2], ps[6]
            bF.rTb, bF.qeb, bF.keb, bF.vb, bF.qeT = V(rTb_t, 0, [(1, 128)], 0, 32), V(qeb_t), V(keb_t), V(vb_t), V(qeT_t)
            bF.kTm = lambda off, dims: V(kTm_t, off, dims)
            bF.am = lambda off, dims: V(am_t, off, dims)
            bF.Sst, bF.Sb, bF.Wt = V(Sst_t), (lambda off, n: V(Sb_t, off, [(1, n)])), V(Wt_t)
            bF.gl = lambda off, n=128: V(gl_t, off, [(1, n)])
            bF.E = stat(48)
            bB = _B()
            bB.P, bB.S, bB.O = ps[5], ps[3], ps[7]
            FB = 7168 * 2
            bB.rTb, bB.qeb, bB.keb = V(fpb_t, FB, [(1, 128)], 0, 32), V(fpb_t, FB + 128, [(1, 128)]), V(fpb_t, FB + 256, [(1, 128)])
            bB.vb, bB.qeT = V(fpb_t, FB + 384, [(1, 256)]), V(fpb_t, FB + 640, [(1, 128)])
            bB.kTm = lambda off, dims: V(fpb_t, FB + 768 + off, dims)
            bB.am = lambda off, dims: V(fpb_t, FB + 1280 + off, dims)
            bB.Sb = lambda off, n: V(fpb_t, FB + 1792 + off, [(1, n)])
            bB.Sst, bB.Wt = V(fp_t, 6144, [(1, 256)]), V(fp_t, 6400, [(1, 256)])
            bB.gl = lambda off, n=128: V(fp_t, 6656 + off, [(1, n)])
            bB.E = stat(49)

            bF.rTb128 = V(rTb_t)
            bB.rTb128 = V(fpb_t, FB, [(1, 128)])
            bF.qeT2 = [V(qeT_t), V(pT_t, 0, [(1, 128)])]
            bF.vb2 = [V(vb_t), V(pT_t, 256, [(1, 256)])]
            bF.am2 = [lambda off, dims: V(am_t, off, dims), lambda off, dims: V(pT_t, 512 + off, dims)]
            bF.E2 = [stat(48), stat(50)]
            bF.Wt2 = [V(Wt_t), V(oT_t, 0, [(1, 256)])]
            bB.qeT2 = [bB.qeT, V(pT_t, 1024, [(1, 128)])]
            bB.vb2 = [bB.vb, V(pT_t, 1024 + 256, [(1, 256)])]
            bB.am2 = [bB.am, lambda off, dims: V(pT_t, 1024 + 512 + off, dims)]
            bB.E2 = [stat(49), stat(51)]
            bB.Wt2 = [bB.Wt, V(oT_t, 256, [(1, 256)])]
            gdone = {}

            def qkv(t, off, n):
                if t < 9:
                    return V(kTA_t, t * 512 + off, [(1, n)])
                return V(VA_t, (t - 9) * 512 + off, [(1, n)])

            def gla_prep(sweep, order, B, C_, kp, ks, par):
                Ltri = LtriF if sweep == 0 else LtriB
                for i, t in enumerate(order):
                    if i % 2 != par:
                        continue
                    while gdone.get(ks, 0) < i - 1:
                        yield
                    s_ = i % 2
                    qeT, vb, am, E, Wt = B.qeT2[s_], B.vb2[s_], B.am2[s_], B.E2[s_], B.Wt2[s_]
                    need_out = (t in out_tiles)
                    p1 = C_.bank
                    vb = qkv(t, 256, 256)
                    mm(V(C_.bank, 0, [(1, 128)]), V(kTB_t, t * 128, [(1, 128)], 0, 32), V(w2bd_t, sweep * 128, [(1, 128)], 0, 32),
                       start=True, stop=True)
                    yield
                    tt(C_.gl(384), V(C_.bank, 0, [(1, 128)]), pb("gb", 128, sweep * 128), ALU.add)
                    yield
                    act(C_.gl(384), C_.gl(384), AF.Exp, scale=-1.0)
                    yield
                    act(C_.gl(0), C_.gl(384), AF.Ln, bias=1.0)
                    yield
                    mm(V(C_.bank, 0, [(1, 128)]), Ltri, C_.gl(0), start=True, stop=True)
                    mm(V(C_.bank, 128, [(1, 1)]), C_.gl(0), ones_c, start=True, stop=True)
                    yield
                    act(E, V(C_.bank, 128, [(1, 1)]), AF.Exp, scale=float(-1.0 / 16))
                    act(C_.gl(128), V(C_.bank, 0, [(1, 128)]), AF.Exp, scale=float(-1.0 / 16))
                    act(C_.gl(256), V(C_.bank, 0, [(1, 128)]), AF.Exp, scale=float(1.0 / 16))
                    yield
                    tt(C_.keb, qkv(t, 128, 128), C_.gl(256), ALU.mult)
                    if need_out:
                        stt(C_.qeb, qkv(t, 0, 128), float(32 ** -0.5), C_.gl(128), ALU.mult, ALU.mult)
                    yield
                    mm(V(p1, 0, [(1, 256)]), C_.keb, vb, start=True, stop=True)
                    yield
                    stt(Wt, V(p1, 0, [(1, 256)]), E, blockmask, ALU.mult, ALU.mult)
                    yield
                    if need_out:
                        ptq = psT()
                        tr(V(ptq, 0, [(1, 128)]), C_.qeb)
                        tr(V(ptq, 128, [(1, 128)]), C_.keb)
                        cp(qeT, V(ptq, 0, [(1, 128)]), eng="dve")
                        tt(C_.kTm(0, [(128, 4), (1, 128)]), V(ptq, 0, [(0, 4), (1, 128)]),
                           V(cf_t, 512, [(1, 4), (0, 128)]), ALU.mult)
                        cp(C_.keT, V(ptq, 128, [(1, 128)]), eng="act")
                        yield
                        mm(V(C_.bank, 0, [(1, 512)]), C_.keT, C_.kTm(0, [(1, 512)]), start=True, stop=True)
                        yield
                        tt(am(0, [(128, 4), (1, 128)]), V(C_.bank, 0, [(128, 4), (1, 128)]),
                           V(cf_t, 0 if sweep == 0 else 128, [(0, 4), (1, 128)]), ALU.mult)
                        yield
                    pdone[kp].add(i)

            def gla_seq(sweep, order, B, kp, ks):
                memset(B.Sst, 0.0)
                memset(B.Sb(0, 256), 0.0)
                yield
                for i, t in enumerate(order):
                    while i not in pdone[kp]:
                        yield
                    s_ = i % 2
                    qeT, am, E, Wt = B.qeT2[s_], B.am2[s_], B.E2[s_], B.Wt2[s_]
                    vb = qkv(t, 256, 256)
                    if t in out_tiles:
                        py = ps[4]
                        mm(V(py, 0, [(1, 256)]), qeT, B.Sb(0, 256), start=True, stop=False)
                        for h in range(4):
                            mm(V(py, h * 64, [(1, 64)]), am(h * 128, [(1, 128)]),
                               bass.AP(vb.tensor, vb.offset + h * 64, [list(vb.ap[0]), [1, 64]]), start=False, stop=True)
                        if sweep == 0:
                            cp(V(y_t, t * 1024 + 768, [(1, 256)]), V(py, 0, [(1, 256)]), eng="act")
                        else:
                            cp(V(fpb_t, t * 256, [(1, 256)]), V(py, 0, [(1, 256)]), eng="act")
                        yield
                    stt(B.Sst, B.Sst, E, Wt, ALU.mult, ALU.add)
                    yield
                    cp(B.Sb(0, 256), B.Sst, eng="dve")
                    yield
                    gdone[ks] = i + 1

            for cg in range(5):
                ncol = 512 if cg < 4 else 256
                pr_ = ps[(2, 3)[cg % 2]]
                for kc in range(8):
                    mm(V(pr_, 0, [(1, ncol)], 0, 32), ring_ap(wd2, kc, 256, 32), V(hT_t, kc * 2304 + cg * 512, [(1, ncol)]),
                       start=(kc == 0), stop=(kc == 7))
                cp(V(kTB_t, cg * 512, [(1, ncol)], 0, 32), V(pr_, 0, [(1, ncol)], 0, 32), eng=("act" if cg % 2 else "dve"))
            qbanks = (ps[0], ps[5], ps[4], ps[6])
            for t in tiles:
                pq_ = qbanks[t % 4]
                for kc in range(8):
                    mm(V(pq_, 0, [(1, 512)]), V(hT_t, kc * 2304 + t * 128, [(1, 128)]), ring_ap(wd1, kc, 0, 512),
                       start=(kc == 0), stop=(kc == 7))
                cp(qkv(t, 0, 512), V(pq_, 0, [(1, 512)]), eng=("act" if t % 2 else "dve"))
            ordB = [1, 0] + list(range(NT - 1, 1, -1))
            pdone = {"pF": set(), "pB": set()}
            cF0, cF1, cB0, cB1 = _B(), _B(), _B(), _B()
            cF0.bank, cF1.bank, cB0.bank, cB1.bank = ps[0], ps[2], ps[3], ps[5]
            cF0.gl, cF0.qeb, cF0.keb, cF0.kTm, cF0.keT = bF.gl, bF.qeb, bF.keb, bF.kTm, bF.rTb128
            cB0.gl, cB0.qeb, cB0.keb, cB0.kTm, cB0.keT = bB.gl, bB.qeb, bB.keb, bB.kTm, bB.rTb128
            for c_, fo, ho in ((cF1, 4096, 0), (cB1, 4608, 1024)):
                c_.gl = lambda off, n=128, fo=fo: V(fp_t, fo + off, [(1, n)])
                c_.qeb = V(hbf_t, ho, [(1, 128)])
                c_.keb = V(hbf_t, ho + 128, [(1, 128)])
                c_.keT = V(hbf_t, ho + 256, [(1, 128)])
                c_.kTm = lambda off, dims, ho=ho: V(hbf_t, ho + 384 + off, dims)
            run_chains([gla_prep(0, tiles, bF, cF0, "pF", "sF", 0), stagger(gla_prep(1, ordB, bB, cB0, "pB", "sB", 0), 3),
                        stagger(gla_prep(0, tiles, bF, cF1, "pF", "sF", 1), 6),
                        stagger(gla_prep(1, ordB, bB, cB1, "pB", "sB", 1), 9),
                        gla_seq(0, tiles, bF, "pF", "sF"), gla_seq(1, ordB, bB, "pB", "sB")])
            def dfin_chain(fs, tl):
                pz = ps[0] if fs == 0 else ps[2]
                o_ = lambda off, dims: V(fp_t, 6144 + fs * 1024 + off, dims)
                for t in tl:
                    yD = V(y_t, t * 1024 + 768, [(1, 256)])
                    for kc in range(8):
                        mm(V(pz, 0, [(1, 256)]), V(hT_t, kc * 2304 + t * 128, [(1, 128)]), ring_ap(wd2, kc, 0, 256),
                           start=(kc == 0), stop=(kc == 7))
                    tt(o_(0, [(1, 256)]), V(fpb_t, t * 256, [(1, 256)]), yD, ALU.add)
                    yield
                    act(o_(256, [(1, 256)]), o_(0, [(1, 256)]), AF.Square)
                    act(o_(512, [(1, 256)]), V(pz, 0, [(1, 256)]), AF.Exp, scale=-1.0)
                    yield
                    red(stat(52 + 4 * fs, 4), o_(256, [(64, 4), (1, 64)]))
                    act(o_(512, [(1, 256)]), o_(512, [(1, 256)]), AF.Ln, bias=1.0)
                    yield
                    act(stat(52 + 4 * fs, 4), stat(52 + 4 * fs, 4), AF.Ln, scale=float(1.0 / 64), bias=float(EPS))
                    act(o_(512, [(1, 256)]), o_(512, [(1, 256)]), AF.Exp, scale=-1.0)
                    yield
                    act(stat(52 + 4 * fs, 4), stat(52 + 4 * fs, 4), AF.Exp, scale=-0.5)
                    tt(o_(256, [(1, 256)]), o_(512, [(1, 256)]), V(pz, 0, [(1, 256)]), ALU.mult)
                    yield
                    tt(o_(0, [(64, 4), (1, 64)]), o_(0, [(64, 4), (1, 64)]), V(st_t, 52 + 4 * fs, [(1, 4), (0, 64)]), ALU.mult)
                    yield
                    tt(o_(0, [(64, 4), (1, 64)]), o_(0, [(64, 4), (1, 64)]), V(pbc_t, PO["gng"], [(0, 4), (1, 64)]), ALU.mult)
                    yield
                    tt(yD, o_(0, [(1, 256)]), o_(256, [(1, 256)]), ALU.mult)
                    yield

            run_chains([dfin_chain(0, out_tiles[0::2]), stagger(dfin_chain(1, out_tiles[1::2]), 4)])

            memset(V(VA_t, 64, [(65, NT * 4)]), 1.0)
            wrelease(2)
            chk("D")
            if debug_y and b == 0 and l == 0:
                for t in tiles:
                    cp(sA(), V(y_t, t * 1024, [(1, 1024)]), eng="dve")
                    dma("sp", dbg_d[t * 128:(t + 1) * 128, :], sA(), reads=[sA()], writes=[("dbg", t, t + 1, 0, 1)])

            for slot, row in ((0, 2), (1, b)):
                src = bass.AP(mods_d.tensor, l * 9 * D + row * 3 * D + 2 * D, [[0, 128], [1, D]])
                dma("sp", mods(slot), src, reads=[("modsd", l, l + 1, 0, 1)], writes=[mods(slot)])
            if last:
                dma("sp", mods(2), bass.AP(fng_d.tensor, 0, [[0, 128], [1, D]]), writes=[mods(2)])
            pbanks[:] = [0, 2, 3]
            wo = [next_wslot(), next_wslot()]

            def stageW_chain(par, tl):
                banks = (ps[0], ps[2]) if par == 0 else (ps[3], ps[4])
                for t in tl:
                    ys = par
                    pt = psT()
                    for kc in range(8):
                        tr(V(pt, kc * 128, [(1, 128)]), V(y_t, t * 1024 + kc * 128, [(1, 128)]), inc=(kc == 7))
                    cp(V(ytT_t, ys * 1024, [(1, 1024)]), V(pt, 0, [(1, 1024)]), eng=("dve" if par else "act"))
                    sl = par
                    if l == 0:
                        src = ctx_d[b, t * 128:(t + 1) * 128, :] if t < 2 else x_d[b, (t - 2) * 128:(t - 1) * 128, :]
                        dma("sp", xt(sl), src, writes=[xt(sl)])
                    else:
                        dma("sp", xt(sl), xres_d[b, t * 128:(t + 1) * 128, :], reads=[("xres", b * NT + t, b * NT + t + 1, 0, 1)],
                            writes=[xt(sl)])
                    yield
                    gsl = 0 if t < 2 else 1
                    for cb in range(2):
                        po = banks[cb]
                        for kc in range(8):
                            mm(V(po, 0, [(1, 512)]), V(ytT_t, ys * 1024 + kc * 128, [(1, 128)]), ring_ap(wo[cb], kc, 0, 512),
                               start=(kc == 0), stop=(kc == 7))
                        yield
                    for cb in range(2):
                        po = banks[cb]
                        tt(sAp(par, cb * 512, 512), V(po, 0, [(1, 512)]), mods(gsl, cb * 512, 512), ALU.mult)
                        yield
                        tt(V(fp_t, 4096 + sl * 1024 + cb * 512, [(1, 512)]), V(fp_t, 4096 + sl * 1024 + cb * 512, [(1, 512)]),
                           sAp(par, cb * 512, 512), ALU.add)
                        yield
                    if not last:
                        dma("sp", xres_d[b, t * 128:(t + 1) * 128, :], xt(sl), reads=[xt(sl)],
                            writes=[("xres", b * NT + t, b * NT + t + 1, 0, 1)])
                    else:
                        act(sAp(par), xt(sl), AF.Square, accum=stat(8 + 2 * par))
                        yield
                        act(stat(9 + 2 * par), stat(8 + 2 * par), AF.Ln, scale=float(1.0 / D), bias=float(EPS))
                        yield
                        act(stat(9 + 2 * par), stat(9 + 2 * par), AF.Exp, scale=-0.5)
                        yield
                        stt(xt(sl), xt(sl), stat(9 + 2 * par), mods(2), ALU.mult, ALU.mult)
                        tok = S.op("sp", lambda e, t=t, sl=sl, b=b: e.dma_start(out=out_d[b, (t - 2) * 128:(t - 1) * 128, :], in_=xt(sl)),
                                   reads=[xt(sl)], writes=[], dma=True)
                        out_toks.append(tok)
                    yield

            run_chains([stageW_chain(0, out_tiles[0::2]), stagger(stageW_chain(1, out_tiles[1::2]), 4)])
            wrelease(2)
            chk("W")

    try:
        main_loops()
    except _Stop:
        if debug_y and stop_at == "stage1":
            for kc in range(8):
                stg = V(fp_t, (kc % 2) * 2304, [(1, 2304)])
                cp(stg, V(hT_t, kc * 2304, [(1, 2304)]), eng="dve")
                dma("sp", bass.AP(dbg_d.tensor, kc * 128 * 2304, [[2304, 128], [1, 2304]]), stg, reads=[stg],
                    writes=[("dbg", kc, kc + 1, 0, 1)])
        elif debug_y and stop_at == "A1":
            for c in range(2):
                stg = V(fp_t, c * 2304, [(1, 2304)])
                cp(stg, V(kTA_t, c * 2304, [(1, 2304)]), eng="dve")
                dma("sp", bass.AP(dbg_d.tensor, c * 128 * 2304, [[2304, 128], [1, 2304]]), stg, reads=[stg],
                    writes=[("dbg", c, c + 1, 0, 1)])
        elif debug_y:
            for t in range(NT):
                cp(sA(), V(y_t, t * 1024, [(1, 1024)]), eng="dve")
                dma("sp", dbg_d[t * 128:(t + 1) * 128, :], sA(), reads=[sA()], writes=[("dbg", t, t + 1, 0, 1)])

    final = {}
    for tok in out_toks:
        final[tok[0]] = max(final.get(tok[0], 0), tok[1])
    for q in ("sp", "pool"):
        for i, lt_ in enumerate(S.dq[q]["last"]):
            if lt_ is not None:
                final[lt_[0]] = max(final.get(lt_[0], 0), lt_[1])
    S.ops["sp"].append((list(final.items()), None, None))

    engmap = {"pe": "tensor", "act": "scalar", "dve": "vector", "pool": "gpsimd", "sp": "sync"}

    def replay(eng, e):
        for waits, fn, tok in S.ops[eng]:
            for k, v in waits:
                e.wait_ge(sems[k], v)
            if fn is None:
                continue
            ins = fn(e)
            if tok is not None:
                ins.then_inc(sems[tok[0]], 16 if tok[0][0] == "d" else 1)

    with nc.Block() as block:
        @block.tensor
        def _(e):
            replay("pe", e)

        @block.scalar
        def _(e):
            replay("act", e)

        @block.vector
        def _(e):
            replay("dve", e)

        @block.gpsimd
        def _(e):
            replay("pool", e)

        @block.sync
        def _(e):
            replay("sp", e)

    st.close()
    return nc, S


def _const_tables():
    bf = ml_dtypes.bfloat16
    cf = np.zeros((128, 647), np.float32)
    j = np.arange(128)[:, None]
    i = np.arange(128)[None, :]
    cf[:, 0:128] = (j <= i)
    cf[:, 128:256] = (j >= i)
    hc = np.arange(128) // 32
    hv = np.arange(256) // 64
    cf[:, 256:512] = (hc[:, None] == hv[None, :])
    cf[:, 512:516] = (hc[:, None] == np.arange(4)[None, :])
    cf[:, 516] = 1.0
    cf[:, 517:645] = np.eye(128)
    cf[:, 645:647] = ((np.arange(128) // 64)[:, None] == np.arange(2)[None, :])
    cb = np.zeros((128, 3200), np.float32)
    cb[:, 0:128] = np.eye(128)
    tt_ = np.arange(TL)
    rows = (tt_ // 64).astype(np.float32)
    cols = (tt_ % 64).astype(np.float32)

    def tables(half):
        inv = (np.float32(10000.0) ** (-np.arange(0, half, 2, dtype=np.float32) / np.float32(half))).astype(np.float32)
        ar = rows[:, None] * inv[None, :]
        ac = cols[:, None] * inv[None, :]
        F = half // 2
        C = np.zeros((TL, 2, 2, F), np.float32)
        Sg = np.zeros((TL, 2, 2, F), np.float32)
        for rc, a in enumerate((ar, ac)):
            C[:, rc, 0] = np.cos(a)
            C[:, rc, 1] = np.cos(a)
            Sg[:, rc, 0] = -np.sin(a)
            Sg[:, rc, 1] = np.sin(a)
        W = 4 * F
        C = C.reshape(16, 128, W).transpose(1, 0, 2).reshape(128, 16 * W)
        Sg = Sg.reshape(16, 128, W).transpose(1, 0, 2).reshape(128, 16 * W)
        return C, Sg

    CA, SA = tables(16)
    CB, SB = tables(32)
    cb[:, 128:640] = CA
    cb[:, 640:1152] = SA
    cb[:, 1152:2176] = CB
    cb[:, 2176:3200] = SB
    return cf, cb.astype(bf)


_CACHE = {}


def _prep(x, c, ctx, c_ctx, ada_w, ada_b, norm_g, w_in, da_lq1, da_lk1, da_lq2, da_lk2,
          da_subln_g, gq_qnorm_g, gq_knorm_g, sg_ln_g, sg_ln_b, sg_w, sg_b,
          gla_w2_f, gla_b_f, gla_w2_b, gla_b_b, gla_norm_g, w_out, final_norm_g, build=True):
    f = lambda a: np.ascontiguousarray(np.asarray(a, dtype=np.float32))
    x, c, ctx, c_ctx = f(x), f(c), f(ctx), f(c_ctx)
    ncores = 8
    nc = None
    if build:
        if "nc" not in _CACHE:
            _CACHE["nc"] = build_program()[0]
        nc = _CACHE["nc"]
    cf, cb = _const_tables()
    L = DEPTH
    pvec = np.concatenate([f(sg_ln_g), f(sg_ln_b), f(gla_b_f), f(gla_b_b), f(da_subln_g), f(gq_qnorm_g),
                           f(gq_knorm_g), f(gla_norm_g), f(da_lq1), f(da_lq2), f(da_lk1), f(da_lk2)], axis=1)
    assert pvec.shape == (L, NPV)
    sgbT = np.ascontiguousarray(f(sg_b).transpose(0, 2, 1))
    sgwT = np.ascontiguousarray(f(sg_w).transpose(0, 3, 1, 2))
    w2bd = np.zeros((L, 32, 256), np.float32)
    w2bd[:, 0:16, 0:128] = f(gla_w2_f)
    w2bd[:, 16:32, 128:256] = f(gla_w2_b)
    shared = dict(ada_w=f(ada_w), ada_b=f(ada_b), norm_g=f(norm_g), w_in=f(w_in), w_out=f(w_out), pvec=pvec,
                  sgbT=sgbT, sgwT=sgwT, w2bd=w2bd, fng=f(final_norm_g), cf32=cf, cbf=cb)
    in_maps = []
    for i in range(ncores):
        rows = np.stack([c[2 * i], c[2 * i + 1], c_ctx], axis=0)
        cT = np.ascontiguousarray(rows.reshape(3, 8, 128).transpose(2, 1, 0))
        m = dict(shared)
        m["x"] = x[2 * i:2 * i + 2]
        m["ctx"] = ctx[2 * i:2 * i + 2]
        m["cT"] = cT
        in_maps.append(m)
    return nc, in_maps


def kernel(**inputs):
    nc, in_maps = _prep(**inputs)
    res = run_bass_kernel_spmd(nc, in_maps, core_ids=list(range(len(in_maps))))
    out = np.concatenate([r["out"] for r in res.results], axis=0)
    return out.astype(np.float32)
```

```python
import math
from contextlib import ExitStack

import numpy as np
import ml_dtypes

import concourse.bass as bass
import concourse.mybir as mybir
from concourse.bass_utils import run_bass_kernel_spmd

F32 = mybir.dt.float32
BF16 = mybir.dt.bfloat16
AF = mybir.ActivationFunctionType
ALU = mybir.AluOpType
AX = mybir.AxisListType

D = 1024
DEPTH = 4
NB = 2
TL = 2048
TC = 256
NT = 18
P_IN = 3360
EPS = 1e-6
NPV = 1152


def _esz(dt):
    return 2 if dt == BF16 else 4


class Sched:
    ENGS = ("pe", "act", "dve", "pool", "sp")

    def __init__(self, same_sync=True, n_dma=10):
        self.ops = {e: [] for e in self.ENGS}
        self.cnt = {e: 0 for e in self.ENGS}
        self.clock = {e: {} for e in self.ENGS}
        self.recs = {}
        self.snap = {}
        self.same_sync = same_sync
        self.n_dma = n_dma
        self.dq = {e: dict(next=0, cnt=[0] * n_dma, last=[None] * n_dma) for e in ("sp", "pool", "act")}
        self.nwaits = 0

    @staticmethod
    def box(a):
        if isinstance(a, tuple):
            return a
        t = a.tensor
        ps = 1
        for s in t.shape[1:]:
            ps *= s
        esz = _esz(a.dtype)
        off = a.offset
        p0 = off // ps
        f0 = off % ps
        apl = a.ap
        npart = apl[0][1]
        ext = 1
        for st, c in apl[1:]:
            ext += (c - 1) * abs(st)
        if t.name.startswith("ps"):
            return (t.name, 0, 128, 0, 1 << 20)
        return (t.name, p0, p0 + npart, f0 * esz, (f0 + ext) * esz)

    def op(self, eng, fn, reads=(), writes=(), dma=False, inc=True):
        deps = {}

        def add(tok):
            k, v = tok
            if deps.get(k, 0) < v:
                deps[k] = v

        rb = [self.box(a) for a in reads]
        wb = [self.box(a) for a in writes]
        own_k = ("e", eng)
        for b in rb:
            isps = b[0].startswith("ps")
            for r in self.recs.get(b[0], ()):
                if (r[5] or (isps and r[6][0] != own_k)) and r[1] < b[2] and b[1] < r[2] and r[3] < b[4] and b[3] < r[4]:
                    add(r[6])
        for b in wb:
            for r in self.recs.get(b[0], ()):
                if r[1] < b[2] and b[1] < r[2] and r[3] < b[4] and b[3] < r[4]:
                    add(r[6])
        if dma:
            q = self.dq[eng]
            i = q["next"]
            q["next"] = (i + 1) % self.n_dma
            if q["last"][i] is not None:
                add(q["last"][i])
            q["cnt"][i] += 1
            tok = (("d", eng, i), q["cnt"][i] * 16)
            q["last"][i] = tok
        else:
            if inc:
                self.cnt[eng] += 1
                tok = (("e", eng), self.cnt[eng])
            else:
                tok = (("e", eng), self.cnt[eng] + 1)
        clk = self.clock[eng]
        waits = []
        own = ("e", eng)
        for k, v in deps.items():
            if k == own and (eng == "pe" or not self.same_sync):
                continue
            if clk.get(k, 0) >= v:
                continue
            waits.append((k, v))
            sn = self.snap.get((k, v))
            if sn:
                for kk, vv in sn.items():
                    if clk.get(kk, 0) < vv:
                        clk[kk] = vv
            clk[k] = v
        self.nwaits += len(waits)
        if inc or dma:
            self.snap[tok] = dict(clk)
        self.ops[eng].append((waits, fn, tok if (inc or dma) else None))
        for b in wb:
            lst = self.recs.setdefault(b[0], [])
            lst[:] = [r for r in lst if not (b[1] <= r[1] and r[2] <= b[2] and b[3] <= r[3] and r[4] <= b[4])]
            lst.append([b[0], b[1], b[2], b[3], b[4], True, tok])
        for b in rb:
            lst = self.recs.setdefault(b[0], [])
            found = False
            for r in lst:
                if (not r[5]) and r[1] == b[1] and r[2] == b[2] and r[3] == b[3] and r[4] == b[4] and r[6][0] == tok[0]:
                    if r[6][1] < tok[1]:
                        r[6] = tok
                    found = True
                    break
            if not found:
                lst.append([b[0], b[1], b[2], b[3], b[4], False, tok])
        return tok

    def replay(self, eng, e, sems):
        for waits, fn, tok in self.ops[eng]:
            for k, v in waits:
                e.wait_ge(sems[k], v)
            ins = fn(e)
            if tok is not None:
                ins.then_inc(sems[tok[0]], 16 if tok[0][0] == "d" else 1)

    def final_wait(self, eng, toks):
        ws = [(k, v) for (k, v) in toks]
        self.ops[eng].append((ws, None, None))


class _Stop(Exception):
    pass


def build_program(depth=DEPTH, nb=NB, debug_y=False, same_sync=True, stop_at=None):
    nc = bass.Bass("TRN2", target_bir_lowering=False)
    S = Sched(same_sync=same_sync)
    st = ExitStack()

    def din(name, shape, dt=F32):
        return nc.dram_tensor(name, list(shape), dt, kind="ExternalInput").ap()

    x_d = din("x", [nb, TL, D])
    ctx_d = din("ctx", [nb, TC, D])
    cT_d = din("cT", [128, 8, 3])
    adaw_d = din("ada_w", [DEPTH, D, 3 * D])
    adab_d = din("ada_b", [DEPTH, 3 * D])
    ng_d = din("norm_g", [DEPTH, D])
    win_d = din("w_in", [DEPTH, D, P_IN])
    wout_d = din("w_out", [DEPTH, D, D])
    pvec_d = din("pvec", [DEPTH, NPV])
    sgbT_d = din("sgbT", [DEPTH, 128, 4])
    sgwT_d = din("sgwT", [DEPTH, 128, 4, 128])
    w2bd_d = din("w2bd", [DEPTH, 32, 256])
    fng_d = din("fng", [D])
    cf32_d = din("cf32", [128, 647])
    cbf_d = din("cbf", [128, 3200], BF16)
    out_d = nc.dram_tensor("out", [nb, TL, D], F32, kind="ExternalOutput").ap()
    if debug_y:
        dbg_d = nc.dram_tensor("dbg", [NT * 128, D], F32, kind="ExternalOutput").ap()
    xres_d = nc.dram_tensor("xres", [nb, NT * 128, D], F32).ap()
    mods_d = nc.dram_tensor("modsd", [DEPTH, 3, 3, D], F32).ap()

    def sb(name, free, dt):
        return st.enter_context(nc.sbuf_tensor("s_" + name, [128, free], dt))

    ident_t = sb("ident", 128, BF16)
    rope_t = sb("rope", 3072, BF16)
    cf_t = sb("cf", 647, F32)
    pbc_t = sb("pbc", NPV, F32)
    bsT_t = sb("bsT", 4, F32)
    w2bd_t = sb("w2bd", 256, BF16)
    wsT_t = sb("wsT", 512, BF16)
    sm_t = sb("sm", 64, F32)
    sgeff_t = sb("sgeff", 64, F32)
    fp_t = sb("fpool", 8192, F32)
    fpb_t = fp_t.bitcast(BF16)
    hT_t = sb("hT", 8 * 2304, BF16)
    y_t = sb("y", NT * 1024, BF16)
    NRING = 3
    ring_t = sb("ring", NRING * 4096, BF16)
    kTA_t = sb("kTA", 2 * 2304, BF16)
    VA_t = sb("VA", NT * 4 * 65 + 64, BF16)
    kTB_t = sb("kTB", 2304, BF16)
    VB_t = sb("VB", NT * 2 * 65 + 64, BF16)
    hbf_t = sb("hbf", 2048, BF16)
    gate_t = sb("gate", 2 * 1024, BF16)
    pT_t = sb("pT", 4 * 512, BF16)
    ytT_t = sb("ytT", 2 * 1024, BF16)
    qr_t = sb("qr", 2 * 256, BF16)
    oT_t = sb("oT", 2 * 512, F32)
    cmb_t = sb("cmb", 512, F32)
    st_t = sb("stats", 128, F32)
    rTb_t = sb("rTb", 128, BF16)
    qeb_t = sb("qeb", 128, BF16)
    keb_t = sb("keb", 128, BF16)
    vb_t = sb("vb", 256, BF16)
    qeT_t = sb("qeT", 128, BF16)
    kTm_t = sb("kTm", 512, BF16)
    am_t = sb("am", 512, BF16)
    Sst_t = sb("Sst", 256, F32)
    Sb_t = sb("Sb", 256, BF16)
    Wt_t = sb("Wt", 256, F32)
    gl_t = sb("gl", 512, F32)
    vnb_t = sb("vnb", 256, BF16)
    uu_t = sb("uu", 512, F32)
    scT_t = sb("scT", 48, F32)

    ps = []
    for i in range(8):
        ps.append(st.enter_context(nc.psum_tensor(f"ps{i}", [128, 512], F32)))
    ps1b = ps[1].bitcast(BF16)

    sems = {}
    for e in Sched.ENGS:
        sems[("e", e)] = st.enter_context(nc.semaphore(f"s_{e}"))
    for e in ("sp", "pool", "act"):
        for i in range(S.n_dma):
            sems[("d", e, i)] = st.enter_context(nc.semaphore(f"d_{e}{i}"))

    def V(t, off=0, dims=None, p0=0, npart=128):
        ps_ = 1
        for s in t.shape[1:]:
            ps_ *= s
        if dims is None:
            dims = [(1, ps_ - off)]
        return bass.AP(t, p0 * ps_ + off, [[ps_, npart]] + [[s, c] for (s, c) in dims])

    def dma(eng, out, in_, reads=(), writes=()):
        S.op(eng, lambda e: e.dma_start(out=out, in_=in_), reads=list(reads), writes=list(writes), dma=True)

    def mm(out, lhsT, rhs, start, stop, **kw):
        S.op("pe", lambda e: e.matmul(out, lhsT, rhs, start=start, stop=stop, **kw),
             reads=[lhsT, rhs], writes=[out], inc=bool(stop))

    def tr(out, in_, inc=True):
        S.op("pe", lambda e: e.transpose(out, in_, V(ident_t, 0, [(1, 128)])), reads=[in_], writes=[out], inc=inc)

    def act(out, in_, func, scale=1.0, bias=None, accum=None):
        rd = [in_]
        wr = [out]
        kw = {}
        if bias is not None:
            kw["bias"] = bias
            if not isinstance(bias, float):
                rd.append(bias)
        if not isinstance(scale, float):
            rd.append(scale)
        if accum is not None:
            kw["accum_out"] = accum
            wr.append(accum)
        S.op("act", lambda e: e.activation(out=out, in_=in_, func=func, scale=scale, **kw), reads=rd, writes=wr)

    def tt(out, in0, in1, op, eng="dve"):
        S.op(eng, lambda e: e.tensor_tensor(out=out, in0=in0, in1=in1, op=op), reads=[in0, in1], writes=[out])

    def ts(out, in0, s1, s2, op0, op1=None, eng="dve"):
        rd = [in0]
        if not isinstance(s1, float):
            rd.append(s1)
        if s2 is not None and not isinstance(s2, float):
            rd.append(s2)
        if op1 is None:
            S.op(eng, lambda e: e.tensor_scalar(out=out, in0=in0, scalar1=s1, scalar2=None, op0=op0), reads=rd, writes=[out])
        else:
            S.op(eng, lambda e: e.tensor_scalar(out=out, in0=in0, scalar1=s1, scalar2=s2, op0=op0, op1=op1), reads=rd, writes=[out])

    def stt(out, in0, scalar, in1, op0, op1):
        rd = [in0, in1]
        if not isinstance(scalar, float):
            rd.append(scalar)
        S.op("dve", lambda e: e.scalar_tensor_tensor(out=out, in0=in0, scalar=scalar, in1=in1, op0=op0, op1=op1),
             reads=rd, writes=[out])

    def cp(out, in_, eng="dve"):
        if eng == "act":
            S.op("act", lambda e: e.copy(out=out, in_=in_), reads=[in_], writes=[out])
        else:
            S.op(eng, lambda e: e.tensor_copy(out=out, in_=in_), reads=[in_], writes=[out])

    def memset(ap, val, eng="dve"):
        S.op(eng, lambda e: e.memset(ap, val), writes=[ap])

    def ttr(out, in0, in1, accum):
        S.op("dve", lambda e: e.tensor_tensor_reduce(out=out, in0=in0, in1=in1, scale=1.0, scalar=0.0,
                                                    op0=ALU.mult, op1=ALU.add, accum_out=accum),
             reads=[in0, in1], writes=[out, accum])

    def red(out, in_):
        S.op("dve", lambda e: e.tensor_reduce(out=out, in_=in_, axis=AX.X, op=ALU.add), reads=[in_], writes=[out])

    def recip(out, in_):
        S.op("dve", lambda e: e.reciprocal(out=out, in_=in_), reads=[in_], writes=[out])

    def rstd_from(out, ssum, k):
        act(out, ssum, AF.Ln, scale=float(k), bias=float(EPS))
        act(out, out, AF.Exp, scale=-0.5)

    rot = {}

    def nxt(key, n):
        i = rot.get(key, 0)
        rot[key] = (i + 1) % n
        return i

    pbanks = [0]

    def psP():
        return ps[pbanks[nxt("P", len(pbanks)) % len(pbanks)]]

    def psT():
        return ps1b

    def psS():
        return ps[(0, 2, 3)[nxt("S", 3)]]

    def psO():
        return ps[(5, 6, 7)[nxt("O", 3)]]

    def mods(slot, off=0, n=1024):
        return V(fp_t, slot * 1024 + off, [(1, n)])

    def xt(slot):
        return V(fp_t, 4096 + slot * 1024, [(1, 1024)])

    sA = lambda off=0, n=1024: V(fp_t, 6144 + off, [(1, n)])
    sB = lambda off=0, n=512: V(fp_t, 7168 + off, [(1, n)])
    sC = lambda off=0, n=512: V(fp_t, 7680 + off, [(1, n)])

    def stat(i, n=1):
        return V(st_t, i, [(1, n)])

    PO = dict(lng=0, lnb=256, gb=512, subg=768, qng=832, kng=896, gng=960, lq=1024, lk=1088)

    def pb(name, n, off=0):
        return V(pbc_t, PO[name] + off, [(1, n)])

    LtriF = V(cf_t, 0, [(1, 128)])
    LtriB = V(cf_t, 128, [(1, 128)])
    blockmask = V(cf_t, 256, [(1, 256)])
    ones_c = V(cf_t, 516, [(1, 1)])

    wreq = []
    wstate = dict(issued=0, used=0)

    def wblock_aps(kind, l, c0, ncol, dst_off):
        src = win_d if kind == "in" else wout_d
        ncols_total = P_IN if kind == "in" else D
        in_ap = bass.AP(src.tensor, l * D * ncols_total + c0, [[ncols_total, 128], [128 * ncols_total, 8], [1, ncol]])
        return dst_off, in_ap, ncol

    def plan_weights():
        plan = []
        for b in range(nb):
            for l in range(depth):
                plan.append([("in", l, 256, 512, 0)])
                plan.append([("in", l, 0, 256, 0), ("in", l, 768, 256, 256)])
                plan.append([("in", l, 1280, 256, 0)])
                plan.append([("in", l, 1024, 256, 0), ("in", l, 1536, 256, 256)])
                plan.append([("in", l, 1792, 512, 0)])
                plan.append([("in", l, 2304, 256, 0)])
                plan.append([("in", l, 2560, 512, 0)])
                plan.append([("in", l, 3072, 288, 0)])
                plan.append([("out", l, 0, 512, 0)])
                plan.append([("out", l, 512, 512, 0)])
        return plan

    wplan = plan_weights()

    def ring_ap(slot, kc, c0, n):
        return V(ring_t, slot * 4096 + kc * 512 + c0, [(1, n)])

    def issue_block(n):
        slot = n % NRING
        for (kind, l, c0, ncol, dst) in wplan[n]:
            _, in_ap, _ = wblock_aps(kind, l, c0, ncol, dst)
            out_ap = V(ring_t, slot * 4096 + dst, [(512, 8), (1, ncol)])
            dma("pool", out_ap, in_ap, writes=[out_ap])

    def issue_weights(upto):
        while wstate["issued"] < min(upto, len(wplan)):
            issue_block(wstate["issued"])
            wstate["issued"] += 1

    wstate["released"] = 0

    def next_wslot():
        n = wstate["used"]
        wstate["used"] += 1
        assert wstate["issued"] > n, "weight block not prefetched"
        return n % NRING

    def wrelease(k=1):
        for _ in range(k):
            wstate["released"] += 1
            issue_weights(wstate["released"] + NRING)

    dma("sp", V(ident_t), cbf_d[:, 0:128], writes=[V(ident_t)])
    dma("sp", V(rope_t), cbf_d[:, 128:3200], writes=[V(rope_t)])
    dma("sp", V(cf_t), cf32_d[:, :], writes=[V(cf_t)])
    dma("sp", V(scT_t, 0, [(1, 24)]), cT_d.rearrange("p a b -> p (a b)"), writes=[V(scT_t, 0, [(1, 24)])])
    memset(V(VA_t), 1.0)
    memset(V(VB_t), 1.0)
    act(V(scT_t, 24, [(1, 24)]), V(scT_t, 0, [(1, 24)]), AF.Exp, scale=-1.0)
    act(V(scT_t, 24, [(1, 24)]), V(scT_t, 24, [(1, 24)]), AF.Ln, bias=1.0)
    act(V(scT_t, 24, [(1, 24)]), V(scT_t, 24, [(1, 24)]), AF.Exp, scale=-1.0)
    tt(V(scT_t, 0, [(1, 24)]), V(scT_t, 24, [(1, 24)]), V(scT_t, 0, [(1, 24)]), ALU.mult)

    pbanks[:] = [0, 2, 3]

    y32_t = y_t.bitcast(F32)

    def mods_gen(l, T_, bank_fn, dq):
        def R3(off, n):
            return V(T_, 4096 + off, [(1, n)], 0, 3)
        adab_bc = bass.AP(adab_d.tensor, l * 3 * D, [[0, 3], [1, 3 * D]])
        dma("sp", R3(0, 3072), adab_bc, writes=[R3(0, 3072)])
        ng_bc = bass.AP(ng_d.tensor, l * D, [[0, 3], [1, D]])
        dma("sp", R3(3072, 1024), ng_bc, writes=[R3(3072, 1024)])
        yield
        for cb in range(12):
            stg = cb % 2
            stg_ap = V(T_, stg * 2048, [(256, 8), (1, 256)])
            in_ap = bass.AP(adaw_d.tensor, l * D * 3 * D + cb * 256, [[3 * D, 128], [128 * 3 * D, 8], [1, 256]])
            dma(dq, stg_ap, in_ap, writes=[stg_ap])
            yield
            pt = bank_fn()
            for kc in range(8):
                mm(V(pt, 0, [(1, 256)], 0, 3), V(scT_t, kc * 3, [(1, 3)]), V(T_, stg * 2048 + kc * 256, [(1, 256)]),
                   start=(kc == 0), stop=(kc == 7))
            tt(R3(cb * 256, 256), V(pt, 0, [(1, 256)], 0, 3), R3(cb * 256, 256), ALU.add)
            yield
        stt(R3(1024, 1024), R3(1024, 1024), 1.0, R3(3072, 1024), ALU.add, ALU.mult)
        yield
        for j, src_off in ((0, 1024), (1, 0), (2, 2048)):
            o_ap = bass.AP(mods_d.tensor, l * 9 * D + j * D, [[3 * D, 3], [1, D]])
            dma("sp", o_ap, R3(src_off, 1024), reads=[R3(src_off, 1024)], writes=[("modsd", l, l + 1, 0, 1)])
        yield

    for _ in mods_gen(0, fp_t, psP, "sp"):
        pass

    issue_weights(NRING)
    nonlocal_dummy = None

    def proj(t, slot, c0, ncol, pt=None, kparts=128):
        if pt is None:
            pt = psP()
        for kc in range(8):
            mm(V(pt, 0, [(1, ncol)]), V(hT_t, kc * 2304 + t * 128, [(1, 128)]), ring_ap(slot, kc, c0, ncol),
               start=(kc == 0), stop=(kc == 7))
        return pt

    def rope(dst, src, lt, which, G, t1o=6144, t2o=7168, scr=None):
        scr = fp_t if scr is None else scr
        if which == "A":
            W, Fq, cb, sbo = 32, 8, 0, 512
        else:
            W, Fq, cb, sbo = 64, 16, 1024, 2048
        cos_bc = V(rope_t, cb + lt * W, [(0, G), (1, W)])
        t1 = V(scr, t1o, [(1, G * W)])
        t1v = V(scr, t1o, [(W, G), (1, W)])
        t2 = V(scr, t2o, [(1, G * W)])
        in0v = src(0, [(W, G), (1, W)])
        S.op("dve", lambda e, in0v=in0v, t1v=t1v, cos_bc=cos_bc: e.tensor_tensor(out=t1v, in0=in0v, in1=cos_bc, op=ALU.mult),
             reads=[src(0, [(1, G * W)]), cos_bc], writes=[t1])
        for half in range(2):
            o = V(scr, t2o + half * Fq, [(W, G), (2 * Fq, 2), (1, Fq)])
            i0 = src((1 - half) * Fq, [(W, G), (2 * Fq, 2), (1, Fq)])
            i1 = V(rope_t, sbo + lt * W + half * Fq, [(0, G), (2 * Fq, 2), (1, Fq)])
            S.op("dve", lambda e, o=o, i0=i0, i1=i1: e.tensor_tensor(out=o, in0=i0, in1=i1, op=ALU.mult),
                 reads=[src(0, [(1, G * W)]), V(rope_t, sbo + lt * W, [(1, W)])], writes=[t2])
        dst(t1, t2)

    def silu_gate(out_bf, z_ps, n, tmp):
        act(tmp, z_ps, AF.Exp, scale=-1.0)
        act(tmp, tmp, AF.Ln, bias=1.0)
        act(tmp, tmp, AF.Exp, scale=-1.0)
        tt(out_bf, tmp, z_ps, ALU.mult)

    out_toks = []

    S.marks = []

    def chk(name):
        if len(name) <= 8 and not any(ch.isdigit() for ch in name[2:]):
            S.marks.append((name, len(S.ops["pe"])))
        if stop_at == name:
            raise _Stop()

    def main_loops():
      for b in range(nb):
        for l in range(depth):
            chk("prologue")
            last = (l == DEPTH - 1)
            lam_init = 0.8 - 0.6 * math.exp(-0.3 * l)
            tiles = list(range(NT))
            out_tiles = list(range(2, NT)) if last else tiles

            pv_bc = bass.AP(pvec_d.tensor, l * NPV, [[0, 128], [1, NPV]])
            dma("sp", V(pbc_t), pv_bc, writes=[V(pbc_t)])
            dma("sp", V(bsT_t), sgbT_d[l], writes=[V(bsT_t)])
            dma("pool", V(wsT_t), sgwT_d[l].rearrange("p a b -> p (a b)"), writes=[V(wsT_t)])
            dma("pool", V(w2bd_t, 0, [(1, 256)], 0, 32), w2bd_d[l], writes=[V(w2bd_t, 0, [(1, 256)], 0, 32)])
            tt(V(sm_t, 0, [(1, 64)]), pb("lq", 64), pb("lk", 64), ALU.mult)
            red(stat(4, 2), V(sm_t, 0, [(32, 2), (1, 32)]))
            act(stat(0, 2), stat(4, 2), AF.Exp)
            tt(stat(2), stat(0), stat(1), ALU.subtract)
            ts(stat(3), stat(2), float(lam_init), -1.0, ALU.add, ALU.mult)
            ts(V(sgeff_t), pb("subg", 64), float(1.0 - lam_init), None, ALU.mult)
            neglam = stat(3)

            for slot, (row, j) in enumerate(((2, 0), (2, 1), (b, 0), (b, 1))):
                src = bass.AP(mods_d.tensor, l * 9 * D + row * 3 * D + j * D, [[0, 128], [1, D]])
                dma("sp", mods(slot), src, reads=[("modsd", l, l + 1, 0, 1)], writes=[mods(slot)])

            chk("params")
            def stagger(g_, n):
                for _ in range(n):
                    yield
                yield from g_

            def run_chains(gens):
                gens = list(gens)
                while gens:
                    for g_ in list(gens):
                        try:
                            next(g_)
                        except StopIteration:
                            gens.remove(g_)

            def sAp(par, off=0, n=1024):
                return V(fp_t, 6144 + par * 1024 + off, [(1, n)])

            def stage1_chain(par, tl):
                for t in tl:
                    sl = par
                    if l == 0:
                        src = ctx_d[b, t * 128:(t + 1) * 128, :] if t < 2 else x_d[b, (t - 2) * 128:(t - 1) * 128, :]
                        dma("sp", xt(sl), src, writes=[xt(sl)])
                    else:
                        dma("sp", xt(sl), xres_d[b, t * 128:(t + 1) * 128, :], reads=[("xres", b * NT + t, b * NT + t + 1, 0, 1)],
                            writes=[xt(sl)])
                    yield
                    act(sAp(par), xt(sl), AF.Square, accum=stat(8 + 2 * par))
                    yield
                    act(stat(9 + 2 * par), stat(8 + 2 * par), AF.Ln, scale=float(1.0 / D), bias=float(EPS))
                    yield
                    act(stat(9 + 2 * par), stat(9 + 2 * par), AF.Exp, scale=-0.5)
                    yield
                    ms = 0 if t < 2 else 2
                    stt(sAp(par), xt(sl), stat(9 + 2 * par), mods(ms), ALU.mult, ALU.mult)
                    yield
                    hb = V(hbf_t, par * 1024, [(1, 1024)])
                    tt(hb, sAp(par), mods(ms + 1), ALU.add)
                    yield
                    pt = psT()
                    for kc in range(8):
                        tr(V(pt, kc * 128, [(1, 128)]), V(hbf_t, par * 1024 + kc * 128, [(1, 128)]), inc=(kc == 7))
                    cp(V(hT_t, t * 128, [(2304, 8), (1, 128)]), V(pt, 0, [(128, 8), (1, 128)]),
                       eng=("dve" if par else "act"))
                    s1done.add(t)
                    yield

            s1done = set()

            chk("stage1")
            def attn_branch(which, phase, ws_in):
                isA = which == "A"
                kT_t = kTA_t if isA else kTB_t
                V_t = VA_t if isA else VB_t
                nkc = 2 if isA else 1
                nvh = 4 if isA else 2
                ycol = 0 if isA else 256
                scale = float(32 ** -0.5) if isA else float(64 ** -0.5)
                def pass1_chain(par):
                    kw = 256 if isA else 128
                    vw = 256 if isA else 128
                    if isA:
                        pbk = ps[0] if par == 0 else ps[3]
                        krt, kro = (qr_t, 0) if par == 0 else (vnb_t, 0)
                        scr, t1o, t2o, nrm = (oT_t, 0, 256, 0) if par == 0 else (cmb_t, 0, 256, 0)
                        sti = 16
                    else:
                        pbk = ps[2] if par == 0 else ps[5]
                        krt, kro = (qr_t, 256) if par == 0 else (qr_t, 384)
                        scr, t1o, t2o, nrm = (oT_t, 512, 640, 768) if par == 0 else (uu_t, 0, 128, 256)
                        sti = 16 if par == 0 else 18
                    for t in tiles[par::2]:
                        while t not in s1done:
                            yield
                        for kc in range(8):
                            mm(V(pbk, 0, [(1, kw + vw)]), V(hT_t, kc * 2304 + t * 128, [(1, 128)]), ring_ap(ws_in, kc, 0, kw + vw),
                               start=(kc == 0), stop=(kc == 7))
                        yield
                        cp(V(V_t, t * nvh * 65, [(65, nvh), (1, 64)]), V(pbk, kw, [(64, nvh), (1, 64)]), eng="act")
                        kr = V(krt, kro, [(1, kw)])
                        if isA:
                            ksrc = lambda off, dims, pbk=pbk: V(pbk, off, dims)
                        else:
                            act(V(scr, nrm, [(1, 128)]), V(pbk, 0, [(1, 128)]), AF.Square)
                            yield
                            red(stat(sti, 2), V(scr, nrm, [(64, 2), (1, 64)]))
                            yield
                            act(stat(sti, 2), stat(sti, 2), AF.Ln, scale=float(1.0 / 64), bias=float(EPS))
                            yield
                            act(stat(sti, 2), stat(sti, 2), AF.Exp, scale=-0.5)
                            yield
                            tt(V(scr, nrm, [(64, 2), (1, 64)]), V(pbk, 0, [(64, 2), (1, 64)]),
                               V(st_t, sti, [(1, 2), (0, 64)]), ALU.mult)
                            yield
                            tt(V(scr, nrm, [(64, 2), (1, 64)]), V(scr, nrm, [(64, 2), (1, 64)]),
                               V(pbc_t, PO["kng"], [(0, 2), (1, 64)]), ALU.mult)
                            ksrc = lambda off, dims, scr=scr, nrm=nrm: V(scr, nrm + off, dims)
                        yield
                        if t >= 2:
                            G = kw // (32 if isA else 64)
                            rope(lambda a, bb_, kr=kr: tt(kr, a, bb_, ALU.add), ksrc, t - 2, which, G, t1o, t2o, scr)
                        else:
                            cp(kr, ksrc(0, [(1, kw)]), eng="dve")
                        yield
                        ptt = psT()
                        for c in range(nkc):
                            tr(V(ptt, c * 128, [(1, 128)]), V(krt, kro + c * 128, [(1, 128)]), inc=(c == nkc - 1))
                        cp(V(kT_t, t * 128, [(2304, nkc), (1, 128)]), V(ptt, 0, [(128, nkc), (1, 128)]), eng="act")
                        yield

                if phase in ("p1e", "p1o"):
                    return pass1_chain(0 if phase == "p1e" else 1)
                ws = ws_in
                pbanks[:] = [1]
                groups = ([] if last else [[0, 1]]) + [[2 + 4 * g + j for j in range(4)] for g in range(4)]
                gslots = [nxt("qTm", 2) for _ in groups]

                def qproj_stages(gs, j, t):
                    cx = {}

                    def st_proj(lo, hi):
                        def f():
                            if lo == 0:
                                cx["pt"] = psP()
                                cx["qs"] = nxt("qr", 2)
                            pt = cx["pt"]
                            for kc in range(lo, hi):
                                mm(V(pt, 0, [(1, 512)]), V(hT_t, kc * 2304 + t * 128, [(1, 128)]), ring_ap(ws, kc, 0, 512),
                                   start=(kc == 0), stop=(kc == 7))
                        return f

                    def st_e1():
                        pt = cx["pt"]
                        gtmp = V(uu_t, 0, [(1, 256)])
                        act(gtmp, V(pt, 256, [(1, 256)]), AF.Exp, scale=-1.0)
                        act(gtmp, gtmp, AF.Ln, bias=1.0)
                        act(gtmp, gtmp, AF.Exp, scale=-1.0)
                        if not isA:
                            act(sC(0, 256), V(pt, 0, [(1, 256)]), AF.Square)

                    def st_e2():
                        if not isA:
                            red(stat(16, 4), V(fp_t, 7680, [(64, 4), (1, 64)]))

                    def st_e3():
                        if not isA:
                            rstd_from(stat(16, 4), stat(16, 4), 1.0 / 64)

                    def st_e4():
                        pt = cx["pt"]
                        qs = cx["qs"]
                        qrv = V(qr_t, qs * 256, [(1, 256)])
                        if isA:
                            qsrc = lambda off, dims: V(pt, off, dims)
                            if t >= 2:
                                rope(lambda a, bb_: tt(qrv, a, bb_, ALU.add), qsrc, t - 2, "A", 8)
                            else:
                                cp(qrv, V(pt, 0, [(1, 256)]), eng="dve")
                        else:
                            tt(V(fp_t, 7680, [(64, 4), (1, 64)]), V(pt, 0, [(64, 4), (1, 64)]),
                               V(st_t, 16, [(1, 4), (0, 64)]), ALU.mult)
                            tt(V(fp_t, 7680, [(64, 4), (1, 64)]), V(fp_t, 7680, [(64, 4), (1, 64)]),
                               V(pbc_t, PO["qng"], [(0, 4), (1, 64)]), ALU.mult)
                            qsrc = lambda off, dims: V(fp_t, 7680 + off, dims)
                            qperm = V(qr_t, qs * 256, [(64, 2), (128, 2), (1, 64)])
                            if t >= 2:
                                rope(lambda a, bb_, qperm=qperm: tt(qperm, V(fp_t, 6144, [(128, 2), (64, 2), (1, 64)]),
                                                                   V(fp_t, 7168, [(128, 2), (64, 2), (1, 64)]), ALU.add),
                                     qsrc, t - 2, "B", 4)
                            else:
                                cp(qperm, V(fp_t, 7680, [(128, 2), (64, 2), (1, 64)]), eng="dve")
                        tt(V(gate_t, gs * 1024 + j * 256, [(1, 256)]), V(uu_t, 0, [(1, 256)]), V(pt, 256, [(1, 256)]), ALU.mult)

                    def st_tr():
                        qs = cx["qs"]
                        ptt = psT()
                        cx["ptt"] = ptt
                        for c in range(2):
                            tr(V(ptt, c * 128, [(1, 128)]), V(qr_t, qs * 256 + c * 128, [(1, 128)]), inc=(c == 1))

                    def st_mask():
                        ptt = cx["ptt"]
                        for c in range(2):
                            if isA:
                                tt(V(fpb_t, gs * 4096 + (4 * c) * 512 + j * 128, [(512, 4), (1, 128)]),
                                   V(ptt, c * 128, [(0, 4), (1, 128)]), V(cf_t, 512, [(1, 4), (0, 128)]), ALU.mult)
                            else:
                                tt(V(fpb_t, gs * 4096 + c * 512 + j * 128, [(1024, 2), (1, 128)]),
                                   V(ptt, c * 128, [(0, 2), (1, 128)]), V(cf_t, 645, [(1, 2), (0, 128)]), ALU.mult)

                    return [st_proj(0, 4), st_proj(4, 8), st_e1, st_e2, st_e3, st_e4, st_tr, st_mask]

                def qproj_tile(gs, j, t):
                    for f in qproj_stages(gs, j, t):
                        f()

                for gi, grp in enumerate(groups):
                    isctx = grp[0] < 2
                    nq = len(grp) * 128
                    kts = [0, 1] if isctx else list(range(NT))
                    gs = gslots[gi]
                    if gi == 0:
                        for j, t in enumerate(grp):
                            qproj_tile(gs, j, t)
                    nxt_calls = []
                    if gi + 1 < len(groups):
                        nxt_calls = [(gslots[gi + 1], j, t) for j, t in enumerate(groups[gi + 1])]
                    nj = len(grp)
                    if isA:
                        items = [(h, m, ki, kt) for h in range(4) for m in range(2) for ki, kt in enumerate(kts)]
                    else:
                        items = [(hq, 0, ki, kt) for hq in range(4) for ki, kt in enumerate(kts)]
                    nk = len(kts)
                    obank = {}

                    def post(h, m):
                        ob = obank[(h, m)]
                        osl = nxt("oT", 2)
                        cp(V(oT_t, osl * 512, [(1, nq)], 0, 65), V(ob, 0, [(1, nq)], 0, 65), eng="dve")
                        return post2(h, m, osl)

                    def post2(h, m, osl):
                        def s_tr():
                            for j in range(nj):
                                if isA:
                                    cbk = ps[6 + j // 2]
                                    col = (m * 2 + j % 2) * 65
                                else:
                                    cbk = ps[6 + h % 2]
                                    col = j * 65
                                S.op("pe", lambda e, cbk=cbk, col=col, osl=osl, j=j: e.transpose(
                                    V(cbk, col, [(1, 65)]), V(oT_t, osl * 512 + j * 128, [(1, 128)], 0, 65),
                                    V(cf_t, 517, [(1, 65)], 0, 65)),
                                    reads=[V(oT_t, osl * 512 + j * 128, [(1, 128)], 0, 65)], writes=[V(cbk, col, [(1, 65)])])
                        stages = [(0, s_tr)]
                        if isA and m == 0:
                            return stages
                        t0_ = grp[0]
                        if not isA:
                            def s_b():
                                cbk = ps[6 + h % 2]
                                recip(stat(64, nj), V(cbk, 64, [(65, nj)]))
                                tt(V(cmb_t, 0, [(64, nj), (1, 64)]), V(cbk, 0, [(65, nj), (1, 64)]),
                                   V(st_t, 64, [(1, nj), (0, 64)]), ALU.mult)
                                tt(V(y_t, t0_ * 1024 + ycol + h * 64, [(1024, nj), (1, 64)]), V(cmb_t, 0, [(64, nj), (1, 64)]),
                                   V(gate_t, gs * 1024 + h * 64, [(256, nj), (1, 64)]), ALU.mult)
                            stages.append((2, s_b))
                            return stages
                        nbk = (nj + 1) // 2

                        def s_a1():
                            for bk in range(nbk):
                                cbk = ps[6 + bk]
                                so = 64 + bk * 8
                                oo = bk * 128
                                recip(stat(so, 4), V(cbk, 64, [(65, 4)]))
                                tt(stat(so + 4, 2), stat(so + 2, 2), V(st_t, 3, [(0, 2)]), ALU.mult)
                                tt(V(cmb_t, 256 + oo, [(64, 2), (1, 64)]), V(cbk, 130, [(65, 2), (1, 64)]),
                                   V(st_t, so + 4, [(1, 2), (0, 64)]), ALU.mult)
                                tt(V(cmb_t, oo, [(64, 2), (1, 64)]), V(cbk, 0, [(65, 2), (1, 64)]),
                                   V(st_t, so, [(1, 2), (0, 64)]), ALU.mult)
                                tt(V(cmb_t, oo, [(1, 128)]), V(cmb_t, oo, [(1, 128)]), V(cmb_t, 256 + oo, [(1, 128)]), ALU.add)
                                tt(V(cmb_t, 256 + oo, [(1, 128)]), V(cmb_t, oo, [(1, 128)]), V(cmb_t, oo, [(1, 128)]), ALU.mult)
                                red(stat(so + 6, 2), V(cmb_t, 256 + oo, [(64, 2), (1, 64)]))

                        def s_a2():
                            for bk in range(nbk):
                                so = 64 + bk * 8
                                act(stat(so + 6, 2), stat(so + 6, 2), AF.Ln, scale=float(1.0 / 64), bias=float(EPS))
                                act(stat(so + 6, 2), stat(so + 6, 2), AF.Exp, scale=-0.5)

                        def s_a3():
                            for bk in range(nbk):
                                so = 64 + bk * 8
                                oo = bk * 128
                                tt(V(cmb_t, oo, [(64, 2), (1, 64)]), V(cmb_t, oo, [(64, 2), (1, 64)]),
                                   V(st_t, so + 6, [(1, 2), (0, 64)]), ALU.mult)
                                tt(V(cmb_t, oo, [(64, 2), (1, 64)]), V(cmb_t, oo, [(64, 2), (1, 64)]),
                                   V(sgeff_t, 0, [(0, 2), (1, 64)]), ALU.mult)
                                tj = grp[2 * bk]
                                tt(V(y_t, tj * 1024 + ycol + h * 64, [(1024, 2), (1, 64)]), V(cmb_t, oo, [(64, 2), (1, 64)]),
                                   V(gate_t, gs * 1024 + (2 * bk) * 256 + h * 64, [(256, 2), (1, 64)]), ALU.mult)
                        stages += [(2, s_a1), (6, s_a2), (8, s_a3)]
                        return stages

                    LOOK = 2
                    pending = []
                    nsteps = len(items)
                    sched_at = {}
                    if nxt_calls:
                        spacing = max(1, nsteps // len(nxt_calls))
                        for qi, call in enumerate(nxt_calls):
                            base = 1 + qi * spacing
                            for si, f in enumerate(qproj_stages(*call)):
                                off = (0, 1, 3, 5, 7, 9, 12, 15)[si] if spacing >= 17 else 0
                                sched_at.setdefault(base + off, []).append(f)
                    delayed = []
                    for idx in range(len(items) + LOOK):
                        for f in sched_at.pop(idx, []):
                            f()
                        while delayed and delayed[0][0] <= idx:
                            delayed.pop(0)[1]()
                        if idx < len(items):
                            h, m, ki, kt = items[idx]
                            sbk = psS()
                            if isA:
                                hm = h * 2 + m
                                mm(V(sbk, 0, [(1, nq)]),
                                   V(kT_t, (hm // 4) * 2304 + kt * 128, [(1, 128)]),
                                   V(fpb_t, gs * 4096 + hm * 512, [(1, nq)]),
                                   start=True, stop=True)
                            else:
                                mm(V(sbk, 0, [(1, nq)]),
                                   V(kT_t, kt * 128, [(1, 128)]),
                                   V(fpb_t, gs * 4096 + h * 512, [(1, nq)]),
                                   start=True, stop=True)
                            pi = nxt("pT", 4)
                            act(V(pT_t, pi * 512, [(1, nq)]), V(sbk, 0, [(1, nq)]), AF.Exp, scale=scale)
                            pending.append((h, m, ki, kt, pi))
                        if idx >= LOOK and pending:
                            h, m, ki, kt, pi = pending.pop(0)
                            if ki == 0:
                                obank[(h, m)] = ps[4 + nxt("OT", 2)]
                            hv = h if isA else h // 2
                            mm(V(obank[(h, m)], 0, [(1, nq)]),
                               V(V_t, (kt * nvh + hv) * 65, [(1, 128)]),
                               V(pT_t, pi * 512, [(1, nq)]),
                               start=(ki == 0), stop=(ki == nk - 1))
                            if ki == nk - 1:
                                for dl, fn in post(h, m):
                                    delayed.append((idx + 3 + (dl if nk >= 8 else 0), fn))
                                delayed.sort(key=lambda x: x[0])
                    for _, fn in delayed:
                        fn()
                    for k_ in sorted(sched_at):
                        for f in sched_at[k_]:
                            f()

            wsA1 = next_wslot()
            wsA2 = next_wslot()
            wsB1 = next_wslot()
            chains_ = [stage1_chain(0, tiles[0::2]), stagger(stage1_chain(1, tiles[1::2]), 3),
                       attn_branch("A", "p1e", wsA1), stagger(attn_branch("A", "p1o", wsA1), 2),
                       stagger(attn_branch("B", "p1e", wsB1), 4), stagger(attn_branch("B", "p1o", wsB1), 8)]
            if b == 0 and l + 1 < depth:
                chains_.append(mods_gen(l + 1, y32_t, lambda: ps[4], "pool"))
            run_chains(chains_)
            wrelease()
            chk("A1")
            attn_branch("A", "p2", wsA2)
            wrelease()
            chk("A")
            wrelease()
            chk("B1")
            wsB2 = next_wslot()
            attn_branch("B", "p2", wsB2)
            wrelease()
            chk("B")

            pbanks[:] = [0, 2, 3]
            ws1 = next_wslot()
            ws2 = next_wslot()

            def sgu_chain(par, tl):
                pt, pz, po = (ps[0], ps[2], ps[4]) if par == 0 else (ps[3], ps[5], ps[6])
                sb_ = 32 + 48 * par
                vnb = V(vnb_t) if par == 0 else V(fpb_t, 2 * 4608, [(1, 256)])
                uu = (lambda off, n: V(uu_t, off, [(1, n)])) if par == 0 else (lambda off, n: V(fp_t, 4096 + off, [(1, n)]))
                for t in tl:
                    for kc in range(8):
                        mm(V(pt, 0, [(1, 512)]), V(hT_t, kc * 2304 + t * 128, [(1, 128)]), ring_ap(ws1, kc, 0, 512),
                           start=(kc == 0), stop=(kc == 7))
                    yield
                    for kc in range(8):
                        mm(V(pz, 0, [(1, 256)]), V(hT_t, kc * 2304 + t * 128, [(1, 128)]), ring_ap(ws2, kc, 0, 256),
                           start=(kc == 0), stop=(kc == 7))
                    yield
                    S.op("dve", lambda e, pt=pt, sb_=sb_: e.bn_stats(out=V(st_t, sb_, [(1, 6)]), in_=V(pt, 256, [(1, 256)])),
                         reads=[V(pt, 256, [(1, 256)])], writes=[V(st_t, sb_, [(1, 6)])])
                    cp(uu(0, 256), V(pt, 0, [(1, 256)]), eng="act")
                    yield
                    S.op("dve", lambda e, sb_=sb_: e.bn_aggr(out=V(st_t, sb_ + 8, [(1, 2)]), in_=V(st_t, sb_, [(1, 6)])),
                         reads=[V(st_t, sb_, [(1, 6)])], writes=[V(st_t, sb_ + 8, [(1, 2)])])
                    act(sAp(par, 512, 256), V(pz, 0, [(1, 256)]), AF.Exp, scale=-1.0)
                    yield
                    act(stat(sb_ + 10), stat(sb_ + 9), AF.Ln, scale=1.0, bias=float(EPS))
                    act(sAp(par, 512, 256), sAp(par, 512, 256), AF.Ln, bias=1.0)
                    yield
                    act(stat(sb_ + 10), stat(sb_ + 10), AF.Exp, scale=-0.5)
                    act(sAp(par, 512, 256), sAp(par, 512, 256), AF.Exp, scale=-1.0)
                    yield
                    stt(stat(sb_ + 11), stat(sb_ + 8), -1.0, stat(sb_ + 10), ALU.mult, ALU.mult)
                    yield
                    ts(sAp(par, 0, 256), V(pt, 256, [(1, 256)]), stat(sb_ + 10), stat(sb_ + 11), ALU.mult, ALU.add)
                    yield
                    tt(sAp(par, 0, 256), sAp(par, 0, 256), pb("lng", 256), ALU.mult)
                    yield
                    tt(vnb, sAp(par, 0, 256), pb("lnb", 256), ALU.add)
                    yield
                    for g in range(4):
                        mm(V(po, g * 64, [(1, 64)]), V(wsT_t, g * 128, [(1, 128)]),
                           bass.AP(vnb.tensor, vnb.offset + g * 64, [list(vnb.ap[0]), [1, 64]]), start=True, stop=True)
                    tt(sAp(par, 256, 256), sAp(par, 512, 256), V(pz, 0, [(1, 256)]), ALU.mult)
                    yield
                    for g in range(4):
                        stt(uu(256 + g * 64, 64), V(po, g * 64, [(1, 64)]), V(bsT_t, g, [(1, 1)]),
                            uu(g * 64, 64), ALU.add, ALU.mult)
                    yield
                    tt(V(y_t, t * 1024 + 512, [(1, 256)]), uu(256, 256), sAp(par, 256, 256), ALU.mult)
                    yield

            run_chains([sgu_chain(0, out_tiles[0::2]), stagger(sgu_chain(1, out_tiles[1::2]), 7)])
            wrelease(2)
            chk("C")
            pbanks[:] = [0]
            wd1 = next_wslot()
            wd2 = next_wslot()

            class _B:
                pass

            bF = _B()
            bF.P, bF.S, bF.O = ps[0], ps[2], ps[6]
            bF.rTb, bF.qeb, bF.keb, bF.vb, bF.qeT = V(rTb_t, 0, [(1, 128)], 0, 32), V(qeb_t), V(keb_t), V(vb_t), V(qeT_t)
            bF.kTm = lambda off, dims: V(kTm_t, off, dims)
            bF.am = lambda off, dims: V(am_t, off, dims)
            bF.Sst, bF.Sb, bF.Wt = V(Sst_t), (lambda off, n: V(Sb_t, off, [(1, n)])), V(Wt_t)
            bF.gl = lambda off, n=128: V(gl_t, off, [(1, n)])
            bF.E = stat(48)
            bB = _B()
            bB.P, bB.S, bB.O = ps[5], ps[3], ps[7]
            FB = 7168 * 2
            bB.rTb, bB.qeb, bB.keb = V(fpb_t, FB, [(1, 128)], 0, 32), V(fpb_t, FB + 128, [(1, 128)]), V(fpb_t, FB + 256, [(1, 128)])
            bB.vb, bB.qeT = V(fpb_t, FB + 384, [(1, 256)]), V(fpb_t, FB + 640, [(1, 128)])
            bB.kTm = lambda off, dims: V(fpb_t, FB + 768 + off, dims)
            bB.am = lambda off, dims: V(fpb_t, FB + 1280 + off, dims)
            bB.Sb = lambda off, n: V(fpb_t, FB + 1792 + off, [(1, n)])
            bB.Sst, bB.Wt = V(fp_t, 6144, [(1, 256)]), V(fp_t, 6400, [(1, 256)])
            bB.gl = lambda off, n=128: V(fp_t, 6656 + off, [(1, n)])
            bB.E = stat(49)

            bF.rTb128 = V(rTb_t)
            bB.rTb128 = V(fpb_t, FB, [(1, 128)])
            bF.qeT2 = [V(qeT_t), V(pT_t, 0, [(1, 128)])]
            bF.vb2 = [V(vb_t), V(pT_t, 256, [(1, 256)])]
            bF.am2 = [lambda off, dims: V(am_t, off, dims), lambda off, dims: V(pT_t, 512 + off, dims)]
            bF.E2 = [stat(48), stat(50)]
            bF.Wt2 = [V(Wt_t), V(oT_t, 0, [(1, 256)])]
            bB.qeT2 = [bB.qeT, V(pT_t, 1024, [(1, 128)])]
            bB.vb2 = [bB.vb, V(pT_t, 1024 + 256, [(1, 256)])]
            bB.am2 = [bB.am, lambda off, dims: V(pT_t, 1024 + 512 + off, dims)]
            bB.E2 = [stat(49), stat(51)]
            bB.Wt2 = [bB.Wt, V(oT_t, 256, [(1, 256)])]
            gdone = {}

            def qkv(t, off, n):
                if t < 9:
                    return V(kTA_t, t * 512 + off, [(1, n)])
                return V(VA_t, (t - 9) * 512 + off, [(1, n)])

            def gla_prep(sweep, order, B, C_, kp, ks, par):
                Ltri = LtriF if sweep == 0 else LtriB
                for i, t in enumerate(order):
                    if i % 2 != par:
                        continue
                    while gdone.get(ks, 0) < i - 1 or t not in qdone:
                        yield
                    s_ = i % 2
                    qeT, vb, am, E, Wt = B.qeT2[s_], B.vb2[s_], B.am2[s_], B.E2[s_], B.Wt2[s_]
                    need_out = (t in out_tiles)
                    p1 = C_.bank
                    vb = qkv(t, 256, 256)
                    mm(V(C_.bank, 0, [(1, 128)]), V(kTB_t, t * 128, [(1, 128)], 0, 32), V(w2bd_t, sweep * 128, [(1, 128)], 0, 32),
                       start=True, stop=True)
                    yield
                    tt(C_.gl(384), V(C_.bank, 0, [(1, 128)]), pb("gb", 128, sweep * 128), ALU.add)
                    yield
                    act(C_.gl(384), C_.gl(384), AF.Exp, scale=-1.0)
                    yield
                    act(C_.gl(0), C_.gl(384), AF.Ln, bias=1.0)
                    yield
                    mm(V(C_.bank, 0, [(1, 128)]), Ltri, C_.gl(0), start=True, stop=True)
                    mm(V(C_.bank, 128, [(1, 1)]), C_.gl(0), ones_c, start=True, stop=True)
                    yield
                    act(E, V(C_.bank, 128, [(1, 1)]), AF.Exp, scale=float(-1.0 / 16))
                    act(C_.gl(128), V(C_.bank, 0, [(1, 128)]), AF.Exp, scale=float(-1.0 / 16))
                    act(C_.gl(256), V(C_.bank, 0, [(1, 128)]), AF.Exp, scale=float(1.0 / 16))
                    yield
                    tt(C_.keb, qkv(t, 128, 128), C_.gl(256), ALU.mult)
                    if need_out:
                        stt(C_.qeb, qkv(t, 0, 128), float(32 ** -0.5), C_.gl(128), ALU.mult, ALU.mult)
                    yield
                    mm(V(p1, 0, [(1, 256)]), C_.keb, vb, start=True, stop=True)
                    yield
                    stt(Wt, V(p1, 0, [(1, 256)]), E, blockmask, ALU.mult, ALU.mult)
                    yield
                    if need_out:
                        ptq = psT()
                        tr(V(ptq, 0, [(1, 128)]), C_.qeb)
                        tr(V(ptq, 128, [(1, 128)]), C_.keb)
                        cp(qeT, V(ptq, 0, [(1, 128)]), eng="dve")
                        tt(C_.kTm(0, [(128, 4), (1, 128)]), V(ptq, 0, [(0, 4), (1, 128)]),
                           V(cf_t, 512, [(1, 4), (0, 128)]), ALU.mult)
                        cp(C_.keT, V(ptq, 128, [(1, 128)]), eng="act")
                        yield
                        mm(V(C_.bank, 0, [(1, 512)]), C_.keT, C_.kTm(0, [(1, 512)]), start=True, stop=True)
                        yield
                        tt(am(0, [(128, 4), (1, 128)]), V(C_.bank, 0, [(128, 4), (1, 128)]),
                           V(cf_t, 0 if sweep == 0 else 128, [(0, 4), (1, 128)]), ALU.mult)
                        yield
                    pdone[kp].add(i)

            def gla_seq(sweep, order, B, kp, ks):
                memset(B.Sst, 0.0)
                memset(B.Sb(0, 256), 0.0)
                yield
                for i, t in enumerate(order):
                    while i not in pdone[kp]:
                        yield
                    s_ = i % 2
                    qeT, am, E, Wt = B.qeT2[s_], B.am2[s_], B.E2[s_], B.Wt2[s_]
                    vb = qkv(t, 256, 256)
                    if t in out_tiles:
                        py = ps[4]
                        mm(V(py, 0, [(1, 256)]), qeT, B.Sb(0, 256), start=True, stop=False)
                        for h in range(4):
                            mm(V(py, h * 64, [(1, 64)]), am(h * 128, [(1, 128)]),
                               bass.AP(vb.tensor, vb.offset + h * 64, [list(vb.ap[0]), [1, 64]]), start=False, stop=True)
                        if sweep == 0:
                            cp(V(y_t, t * 1024 + 768, [(1, 256)]), V(py, 0, [(1, 256)]), eng="act")
                        else:
                            cp(V(fpb_t, t * 256, [(1, 256)]), V(py, 0, [(1, 256)]), eng="act")
                        yield
                    stt(B.Sst, B.Sst, E, Wt, ALU.mult, ALU.add)
                    yield
                    cp(B.Sb(0, 256), B.Sst, eng="dve")
                    yield
                    gdone[ks] = i + 1

            for cg in range(5):
                ncol = 512 if cg < 4 else 256
                pr_ = ps[(2, 3)[cg % 2]]
                for kc in range(8):
                    mm(V(pr_, 0, [(1, ncol)], 0, 32), ring_ap(wd2, kc, 256, 32), V(hT_t, kc * 2304 + cg * 512, [(1, ncol)]),
                       start=(kc == 0), stop=(kc == 7))
                cp(V(kTB_t, cg * 512, [(1, ncol)], 0, 32), V(pr_, 0, [(1, ncol)], 0, 32), eng=("act" if cg % 2 else "dve"))
            qdone = set()
            qorder = [0, 1]
            lo_, hi_ = 2, NT - 1
            while lo_ <= hi_:
                qorder.append(hi_)
                if lo_ != hi_:
                    qorder.append(lo_)
                lo_, hi_ = lo_ + 1, hi_ - 1

            def qkv_chain():
                for n_, t in enumerate(qorder):
                    pq_ = ps[6 + n_ % 2]
                    for kc in range(8):
                        mm(V(pq_, 0, [(1, 512)]), V(hT_t, kc * 2304 + t * 128, [(1, 128)]), ring_ap(wd1, kc, 0, 512),
                           start=(kc == 0), stop=(kc == 7))
                    cp(qkv(t, 0, 512), V(pq_, 0, [(1, 512)]), eng=("act" if n_ % 2 else "dve"))
                    qdone.add(t)
                    yield
            ordB = [1, 0] + list(range(NT - 1, 1, -1))
            pdone = {"pF": set(), "pB": set()}
            cF0, cF1, cB0, cB1 = _B(), _B(), _B(), _B()
            cF0.bank, cF1.bank, cB0.bank, cB1.bank = ps[0], ps[2], ps[3], ps[5]
            cF0.gl, cF0.qeb, cF0.keb, cF0.kTm, cF0.keT = bF.gl, bF.qeb, bF.keb, bF.kTm, bF.rTb128
            cB0.gl, cB0.qeb, cB0.keb, cB0.kTm, cB0.keT = bB.gl, bB.qeb, bB.keb, bB.kTm, bB.rTb128
            for c_, fo, ho in ((cF1, 4096, 0), (cB1, 4608, 1024)):
                c_.gl = lambda off, n=128, fo=fo: V(fp_t, fo + off, [(1, n)])
                c_.qeb = V(hbf_t, ho, [(1, 128)])
                c_.keb = V(hbf_t, ho + 128, [(1, 128)])
                c_.keT = V(hbf_t, ho + 256, [(1, 128)])
                c_.kTm = lambda off, dims, ho=ho: V(hbf_t, ho + 384 + off, dims)
            run_chains([qkv_chain(),
                        gla_prep(0, tiles, bF, cF0, "pF", "sF", 0), stagger(gla_prep(1, ordB, bB, cB0, "pB", "sB", 0), 3),
                        stagger(gla_prep(0, tiles, bF, cF1, "pF", "sF", 1), 6),
                        stagger(gla_prep(1, ordB, bB, cB1, "pB", "sB", 1), 9),
                        gla_seq(0, tiles, bF, "pF", "sF"), gla_seq(1, ordB, bB, "pB", "sB")])
            def dfin_chain(fs, tl):
                pz = ps[0] if fs == 0 else ps[2]
                o_ = lambda off, dims: V(fp_t, 6144 + fs * 1024 + off, dims)
                for t in tl:
                    yD = V(y_t, t * 1024 + 768, [(1, 256)])
                    for kc in range(8):
                        mm(V(pz, 0, [(1, 256)]), V(hT_t, kc * 2304 + t * 128, [(1, 128)]), ring_ap(wd2, kc, 0, 256),
                           start=(kc == 0), stop=(kc == 7))
                    tt(o_(0, [(1, 256)]), V(fpb_t, t * 256, [(1, 256)]), yD, ALU.add)
                    yield
                    act(o_(256, [(1, 256)]), o_(0, [(1, 256)]), AF.Square)
                    act(o_(512, [(1, 256)]), V(pz, 0, [(1, 256)]), AF.Exp, scale=-1.0)
                    yield
                    red(stat(52 + 4 * fs, 4), o_(256, [(64, 4), (1, 64)]))
                    act(o_(512, [(1, 256)]), o_(512, [(1, 256)]), AF.Ln, bias=1.0)
                    yield
                    act(stat(52 + 4 * fs, 4), stat(52 + 4 * fs, 4), AF.Ln, scale=float(1.0 / 64), bias=float(EPS))
                    act(o_(512, [(1, 256)]), o_(512, [(1, 256)]), AF.Exp, scale=-1.0)
                    yield
                    act(stat(52 + 4 * fs, 4), stat(52 + 4 * fs, 4), AF.Exp, scale=-0.5)
                    tt(o_(256, [(1, 256)]), o_(512, [(1, 256)]), V(pz, 0, [(1, 256)]), ALU.mult)
                    yield
                    tt(o_(0, [(64, 4), (1, 64)]), o_(0, [(64, 4), (1, 64)]), V(st_t, 52 + 4 * fs, [(1, 4), (0, 64)]), ALU.mult)
                    yield
                    tt(o_(0, [(64, 4), (1, 64)]), o_(0, [(64, 4), (1, 64)]), V(pbc_t, PO["gng"], [(0, 4), (1, 64)]), ALU.mult)
                    yield
                    tt(yD, o_(0, [(1, 256)]), o_(256, [(1, 256)]), ALU.mult)
                    yield

            run_chains([dfin_chain(0, out_tiles[0::2]), stagger(dfin_chain(1, out_tiles[1::2]), 4)])

            memset(V(VA_t, 64, [(65, NT * 4)]), 1.0)
            wrelease(2)
            chk("D")
            if debug_y and b == 0 and l == 0:
                for t in tiles:
                    cp(sA(), V(y_t, t * 1024, [(1, 1024)]), eng="dve")
                    dma("sp", dbg_d[t * 128:(t + 1) * 128, :], sA(), reads=[sA()], writes=[("dbg", t, t + 1, 0, 1)])

            for slot, row in ((0, 2), (1, b)):
                src = bass.AP(mods_d.tensor, l * 9 * D + row * 3 * D + 2 * D, [[0, 128], [1, D]])
                dma("sp", mods(slot), src, reads=[("modsd", l, l + 1, 0, 1)], writes=[mods(slot)])
            if last:
                dma("sp", mods(2), bass.AP(fng_d.tensor, 0, [[0, 128], [1, D]]), writes=[mods(2)])
            pbanks[:] = [0, 2, 3]
            wo = [next_wslot(), next_wslot()]

            def stageW_chain(par, tl):
                banks = (ps[0], ps[2]) if par == 0 else (ps[3], ps[4])
                for t in tl:
                    ys = par
                    pt = psT()
                    for kc in range(8):
                        tr(V(pt, kc * 128, [(1, 128)]), V(y_t, t * 1024 + kc * 128, [(1, 128)]), inc=(kc == 7))
                    cp(V(ytT_t, ys * 1024, [(1, 1024)]), V(pt, 0, [(1, 1024)]), eng=("dve" if par else "act"))
                    sl = par
                    if l == 0:
                        src = ctx_d[b, t * 128:(t + 1) * 128, :] if t < 2 else x_d[b, (t - 2) * 128:(t - 1) * 128, :]
                        dma("sp", xt(sl), src, writes=[xt(sl)])
                    else:
                        dma("sp", xt(sl), xres_d[b, t * 128:(t + 1) * 128, :], reads=[("xres", b * NT + t, b * NT + t + 1, 0, 1)],
                            writes=[xt(sl)])
                    yield
                    gsl = 0 if t < 2 else 1
                    for cb in range(2):
                        po = banks[cb]
                        for kc in range(8):
                            mm(V(po, 0, [(1, 512)]), V(ytT_t, ys * 1024 + kc * 128, [(1, 128)]), ring_ap(wo[cb], kc, 0, 512),
                               start=(kc == 0), stop=(kc == 7))
                        yield
                    for cb in range(2):
                        po = banks[cb]
                        tt(sAp(par, cb * 512, 512), V(po, 0, [(1, 512)]), mods(gsl, cb * 512, 512), ALU.mult)
                        yield
                        tt(V(fp_t, 4096 + sl * 1024 + cb * 512, [(1, 512)]), V(fp_t, 4096 + sl * 1024 + cb * 512, [(1, 512)]),
                           sAp(par, cb * 512, 512), ALU.add)
                        yield
                    if not last:
                        dma("sp", xres_d[b, t * 128:(t + 1) * 128, :], xt(sl), reads=[xt(sl)],
                            writes=[("xres", b * NT + t, b * NT + t + 1, 0, 1)])
                    else:
                        act(sAp(par), xt(sl), AF.Square, accum=stat(8 + 2 * par))
                        yield
                        act(stat(9 + 2 * par), stat(8 + 2 * par), AF.Ln, scale=float(1.0 / D), bias=float(EPS))
                        yield
                        act(stat(9 + 2 * par), stat(9 + 2 * par), AF.Exp, scale=-0.5)
                        yield
                        stt(xt(sl), xt(sl), stat(9 + 2 * par), mods(2), ALU.mult, ALU.mult)
                        tok = S.op("sp", lambda e, t=t, sl=sl, b=b: e.dma_start(out=out_d[b, (t - 2) * 128:(t - 1) * 128, :], in_=xt(sl)),
                                   reads=[xt(sl)], writes=[], dma=True)
                        out_toks.append(tok)
                    yield

            run_chains([stageW_chain(0, out_tiles[0::2]), stagger(stageW_chain(1, out_tiles[1::2]), 4)])
            wrelease(2)
            chk("W")

    try:
        main_loops()
    except _Stop:
        if debug_y and stop_at == "stage1":
            for kc in range(8):
                stg = V(fp_t, (kc % 2) * 2304, [(1, 2304)])
                cp(stg, V(hT_t, kc * 2304, [(1, 2304)]), eng="dve")
                dma("sp", bass.AP(dbg_d.tensor, kc * 128 * 2304, [[2304, 128], [1, 2304]]), stg, reads=[stg],
                    writes=[("dbg", kc, kc + 1, 0, 1)])
        elif debug_y and stop_at == "A1":
            for c in range(2):
                stg = V(fp_t, c * 2304, [(1, 2304)])
                cp(stg, V(kTA_t, c * 2304, [(1, 2304)]), eng="dve")
                dma("sp", bass.AP(dbg_d.tensor, c * 128 * 2304, [[2304, 128], [1, 2304]]), stg, reads=[stg],
                    writes=[("dbg", c, c + 1, 0, 1)])
        elif debug_y:
            for t in range(NT):
                cp(sA(), V(y_t, t * 1024, [(1, 1024)]), eng="dve")
                dma("sp", dbg_d[t * 128:(t + 1) * 128, :], sA(), reads=[sA()], writes=[("dbg", t, t + 1, 0, 1)])

    final = {}
    for tok in out_toks:
        final[tok[0]] = max(final.get(tok[0], 0), tok[1])
    for q in ("sp", "pool"):
        for i, lt_ in enumerate(S.dq[q]["last"]):
            if lt_ is not None:
                final[lt_[0]] = max(final.get(lt_[0], 0), lt_[1])
    S.ops["sp"].append((list(final.items()), None, None))

    engmap = {"pe": "tensor", "act": "scalar", "dve": "vector", "pool": "gpsimd", "sp": "sync"}

    def replay(eng, e):
        for waits, fn, tok in S.ops[eng]:
            for k, v in waits:
                e.wait_ge(sems[k], v)
            if fn is None:
                continue
            ins = fn(e)
            if tok is not None:
                ins.then_inc(sems[tok[0]], 16 if tok[0][0] == "d" else 1)

    with nc.Block() as block:
        @block.tensor
        def _(e):
            replay("pe", e)

        @block.scalar
        def _(e):
            replay("act", e)

        @block.vector
        def _(e):
            replay("dve", e)

        @block.gpsimd
        def _(e):
            replay("pool", e)

        @block.sync
        def _(e):
            replay("sp", e)

    st.close()
    return nc, S


def _const_tables():
    bf = ml_dtypes.bfloat16
    cf = np.zeros((128, 647), np.float32)
    j = np.arange(128)[:, None]
    i = np.arange(128)[None, :]
    cf[:, 0:128] = (j <= i)
    cf[:, 128:256] = (j >= i)
    hc = np.arange(128) // 32
    hv = np.arange(256) // 64
    cf[:, 256:512] = (hc[:, None] == hv[None, :])
    cf[:, 512:516] = (hc[:, None] == np.arange(4)[None, :])
    cf[:, 516] = 1.0
    cf[:, 517:645] = np.eye(128)
    cf[:, 645:647] = ((np.arange(128) // 64)[:, None] == np.arange(2)[None, :])
    cb = np.zeros((128, 3200), np.float32)
    cb[:, 0:128] = np.eye(128)
    tt_ = np.arange(TL)
    rows = (tt_ // 64).astype(np.float32)
    cols = (tt_ % 64).astype(np.float32)

    def tables(half):
        inv = (np.float32(10000.0) ** (-np.arange(0, half, 2, dtype=np.float32) / np.float32(half))).astype(np.float32)
        ar = rows[:, None] * inv[None, :]
        ac = cols[:, None] * inv[None, :]
        F = half // 2
        C = np.zeros((TL, 2, 2, F), np.float32)
        Sg = np.zeros((TL, 2, 2, F), np.float32)
        for rc, a in enumerate((ar, ac)):
            C[:, rc, 0] = np.cos(a)
            C[:, rc, 1] = np.cos(a)
            Sg[:, rc, 0] = -np.sin(a)
            Sg[:, rc, 1] = np.sin(a)
        W = 4 * F
        C = C.reshape(16, 128, W).transpose(1, 0, 2).reshape(128, 16 * W)
        Sg = Sg.reshape(16, 128, W).transpose(1, 0, 2).reshape(128, 16 * W)
        return C, Sg

    CA, SA = tables(16)
    CB, SB = tables(32)
    cb[:, 128:640] = CA
    cb[:, 640:1152] = SA
    cb[:, 1152:2176] = CB
    cb[:, 2176:3200] = SB
    return cf, cb.astype(bf)


_CACHE = {}


def _prep(x, c, ctx, c_ctx, ada_w, ada_b, norm_g, w_in, da_lq1, da_lk1, da_lq2, da_lk2,
          da_subln_g, gq_qnorm_g, gq_knorm_g, sg_ln_g, sg_ln_b, sg_w, sg_b,
          gla_w2_f, gla_b_f, gla_w2_b, gla_b_b, gla_norm_g, w_out, final_norm_g, build=True):
    f = lambda a: np.ascontiguousarray(np.asarray(a, dtype=np.float32))
    x, c, ctx, c_ctx = f(x), f(c), f(ctx), f(c_ctx)
    ncores = 8
    nc = None
    if build:
        if "nc" not in _CACHE:
            _CACHE["nc"] = build_program()[0]
        nc = _CACHE["nc"]
    cf, cb = _const_tables()
    L = DEPTH
    pvec = np.concatenate([f(sg_ln_g), f(sg_ln_b), f(gla_b_f), f(gla_b_b), f(da_subln_g), f(gq_qnorm_g),
                           f(gq_knorm_g), f(gla_norm_g), f(da_lq1), f(da_lq2), f(da_lk1), f(da_lk2)], axis=1)
    assert pvec.shape == (L, NPV)
    sgbT = np.ascontiguousarray(f(sg_b).transpose(0, 2, 1))
    sgwT = np.ascontiguousarray(f(sg_w).transpose(0, 3, 1, 2))
    w2bd = np.zeros((L, 32, 256), np.float32)
    w2bd[:, 0:16, 0:128] = f(gla_w2_f)
    w2bd[:, 16:32, 128:256] = f(gla_w2_b)
    shared = dict(ada_w=f(ada_w), ada_b=f(ada_b), norm_g=f(norm_g), w_in=f(w_in), w_out=f(w_out), pvec=pvec,
                  sgbT=sgbT, sgwT=sgwT, w2bd=w2bd, fng=f(final_norm_g), cf32=cf, cbf=cb)
    in_maps = []
    for i in range(ncores):
        rows = np.stack([c[2 * i], c[2 * i + 1], c_ctx], axis=0)
        cT = np.ascontiguousarray(rows.reshape(3, 8, 128).transpose(2, 1, 0))
        m = dict(shared)
        m["x"] = x[2 * i:2 * i + 2]
        m["ctx"] = ctx[2 * i:2 * i + 2]
        m["cT"] = cT
        in_maps.append(m)
    return nc, in_maps


def kernel(**inputs):
    nc, in_maps = _prep(**inputs)
    res = run_bass_kernel_spmd(nc, in_maps, core_ids=list(range(len(in_maps))))
    out = np.concatenate([r["out"] for r in res.results], axis=0)
    return out.astype(np.float32)
```
